# Optimizing a Trainium2 kernel written in Bass

```python
import jax
import jax.numpy as jnp
from jax import lax
import numpy as np

D_MODEL = 1024
BATCH = 32
SEQ = 2048
DEPTH = 4

CTX_LEN = 256
GRID_W = 64
RMS_EPS = 1e-6

MLA_HEADS = 8
MLA_Q_RANK = 256
MLA_KV_RANK = 128
MLA_NOPE = 64
MLA_ROPE = 32
MLA_V = 64
MLA_WIDTH = MLA_HEADS * MLA_V
MLA_SCALE = (MLA_NOPE + MLA_ROPE) ** -0.5
ROPE_PAIRS = MLA_ROPE // 4
ROPE_BASE = 10000.0
Q_BLOCK = 128

HG_HEADS = 8
HG_DK = 64
HG_DV = 64
HG_KWIDTH = HG_HEADS * HG_DK
HG_WIDTH = HG_HEADS * HG_DV
HG_CHUNK = 16

CV_WIDTH = 512
CV_K = 3

N_BRANCH = 3
BR_WIDTH = 512

IN_SIZES = (MLA_Q_RANK, MLA_KV_RANK, MLA_ROPE, MLA_WIDTH,
            HG_KWIDTH, HG_WIDTH, HG_KWIDTH, HG_KWIDTH, HG_WIDTH,
            CV_WIDTH, CV_WIDTH, CV_WIDTH, CV_WIDTH,
            N_BRANCH * D_MODEL)
N_IN = sum(IN_SIZES)

kernel_name = "hybrid_mla_hgrn2_shortconv_dit"


def rms_norm(x, g):
    xf = x.astype(jnp.float32)
    y = xf * lax.rsqrt(jnp.mean(xf * xf, axis=-1, keepdims=True) + RMS_EPS)
    return (y * g.astype(jnp.float32)).astype(x.dtype)


def split_columns(z):
    cuts = np.cumsum(np.array(IN_SIZES))[:-1].tolist()
    return jnp.split(z, cuts, axis=-1)


def axial_rope_tables(n_lat, dtype):
    rows = n_lat // GRID_W
    row_id = jnp.broadcast_to(jnp.arange(rows, dtype=jnp.float32)[:, None], (rows, GRID_W)).reshape(-1)
    col_id = jnp.broadcast_to(jnp.arange(GRID_W, dtype=jnp.float32)[None, :], (rows, GRID_W)).reshape(-1)
    inv_freq = jnp.power(ROPE_BASE, -jnp.arange(ROPE_PAIRS, dtype=jnp.float32) / ROPE_PAIRS)
    ang = jnp.stack([row_id[:, None] * inv_freq, col_id[:, None] * inv_freq], axis=1)
    return jnp.cos(ang)[:, None].astype(dtype), jnp.sin(ang)[:, None].astype(dtype)


def apply_axial_rope(x, cos, sin):
    xs = x.reshape(x.shape[:-1] + (2, 2, ROPE_PAIRS))
    x1, x2 = xs[..., 0, :], xs[..., 1, :]
    rot = jnp.stack([x1 * cos - x2 * sin, x1 * sin + x2 * cos], axis=-2)
    return rot.reshape(x.shape)


def mla_attend(q_nope, q_rope, k_nope, k_rope, v):
    s = jnp.einsum('bqhd,bkhd->bhqk', q_nope, k_nope) + jnp.einsum('bqhr,bkr->bhqk', q_rope, k_rope)
    p = jax.nn.softmax(s.astype(jnp.float32) * MLA_SCALE, axis=-1).astype(v.dtype)
    return jnp.einsum('bhqk,bkhd->bqhd', p, v)


def mla_branch(cq, ckv, kr, q_norm_g, kv_norm_g, w_uq, w_ukv, n_ctx, with_ctx):
    bsz, n_tok, _ = cq.shape
    n_lat = n_tok - n_ctx
    q = (rms_norm(cq, q_norm_g) @ w_uq).reshape(bsz, n_tok, MLA_HEADS, MLA_NOPE + MLA_ROPE)
    kv = (rms_norm(ckv, kv_norm_g) @ w_ukv).reshape(bsz, n_tok, MLA_HEADS, MLA_NOPE + MLA_V)
    q_nope, q_rope = q[..., :MLA_NOPE], q[..., MLA_NOPE:]
    k_nope, v = kv[..., :MLA_NOPE], kv[..., MLA_NOPE:]
    cos, sin = axial_rope_tables(n_lat, q.dtype)
    q_rope_lat = apply_axial_rope(q_rope[:, n_ctx:], cos, sin)
    k_rope = jnp.concatenate([kr[:, :n_ctx], apply_axial_rope(kr[:, n_ctx:, None], cos, sin)[:, :, 0]], axis=1)
    n_blk = n_lat // Q_BLOCK
    qn_blk = q_nope[:, n_ctx:].reshape(bsz, n_blk, Q_BLOCK, MLA_HEADS, MLA_NOPE).swapaxes(0, 1)
    qr_blk = q_rope_lat.reshape(bsz, n_blk, Q_BLOCK, MLA_HEADS, MLA_ROPE).swapaxes(0, 1)
    o_lat = lax.map(lambda qs: mla_attend(qs[0], qs[1], k_nope, k_rope, v), (qn_blk, qr_blk))
    o_lat = o_lat.swapaxes(0, 1).reshape(bsz, n_lat, MLA_WIDTH)
    if not with_ctx:
        return o_lat
    o_ctx = mla_attend(q_nope[:, :n_ctx], q_rope[:, :n_ctx], k_nope[:, :n_ctx], k_rope[:, :n_ctx], v[:, :n_ctx])
    return jnp.concatenate([o_ctx.reshape(bsz, n_ctx, MLA_WIDTH), o_lat], axis=1)


def chunked_gated_scan(q, k, v, log_f, s0):
    bsz, n_tok, heads, _ = q.shape
    n_chunk = n_tok // HG_CHUNK
    to_chunks = lambda a: a.reshape(bsz, n_chunk, HG_CHUNK, heads, a.shape[-1]).transpose(1, 0, 3, 2, 4)
    mask = jnp.tril(jnp.ones((HG_CHUNK, HG_CHUNK), dtype=bool))

    def step(s, inp):
        qc, kc, vc, gc = inp
        b = jnp.cumsum(gc, axis=-2)
        b_last = b[..., -1:, :]
        q_dec = qc * jnp.exp(b)
        k_dec = kc * jnp.exp(-b)
        a = jnp.where(mask, jnp.einsum('bhtd,bhsd->bhts', q_dec, k_dec), 0.0)
        o = jnp.einsum('bhts,bhsv->bhtv', a, vc) + jnp.einsum('bhtd,bhdv->bhtv', q_dec, s)
        s_new = jnp.exp(b_last[..., 0, :])[..., None] * s + jnp.einsum('bhsd,bhsv->bhdv', kc * jnp.exp(b_last - b), vc)
        return s_new, o

    s_fin, o = lax.scan(step, s0, (to_chunks(q), to_chunks(k), to_chunks(v), to_chunks(log_f)))
    return o.transpose(1, 0, 3, 2, 4).reshape(bsz, n_tok, heads, v.shape[-1]), s_fin


def hgrn2_branch(q, i, f_fwd, f_bwd, lb, norm_g, n_ctx):
    bsz, n_tok, _ = q.shape
    heads = lambda a, d: a.astype(jnp.float32).reshape(bsz, n_tok, HG_HEADS, d)
    qh, vh = heads(q, HG_DK), heads(i, HG_DV)
    s0 = jnp.zeros((bsz, HG_HEADS, HG_DK, HG_DV), jnp.float32)

    def gates(f_logit, lb_d):
        z = heads(f_logit, HG_DK)
        lb_h = lb_d.reshape(HG_HEADS, HG_DK)
        log_f = jnp.logaddexp(jnp.log(lb_h), jnp.log1p(-lb_h) + jax.nn.log_sigmoid(z))
        k = (1.0 - lb_h) * jax.nn.sigmoid(-z)
        return k, log_f

    k_f, g_f = gates(f_fwd, lb[0])
    o_fwd, _ = chunked_gated_scan(qh, k_f, vh, g_f, s0)
    flip = lambda a: jnp.concatenate([jnp.flip(a[:, :n_ctx], axis=1), jnp.flip(a[:, n_ctx:], axis=1)], axis=1)
    k_b, g_b = gates(f_bwd, lb[1])
    o_bwd, _ = chunked_gated_scan(flip(qh), flip(k_b), flip(vh), flip(g_b), s0)
    o = o_fwd + flip(o_bwd)
    o = rms_norm(o, norm_g.reshape(HG_HEADS, HG_DV))
    return o.reshape(bsz, n_tok, HG_WIDTH).astype(q.dtype)


def short_conv(u, w, b):
    up = jnp.pad(u, ((0, 0), (1, 1), (0, 0)))
    return up[:, :-2] * w[0] + up[:, 1:-1] * w[1] + up[:, 2:] * w[2] + b


def conv_branch(xin, bg, cg, w, b, n_ctx, with_ctx):
    u = cg * xin
    y_lat = bg[:, n_ctx:] * short_conv(u[:, n_ctx:], w, b)
    if not with_ctx:
        return y_lat
    y_ctx = bg[:, :n_ctx] * short_conv(u[:, :n_ctx], w, b)
    return jnp.concatenate([y_ctx, y_lat], axis=1)


def setup_inputs(seed: int = 0) -> dict:
    key = jax.random.key(seed)
    ks = jax.random.split(key, 19)
    nrm = lambda k, shape, s: jax.random.normal(k, shape, jnp.float32) * s
    return {
        "x": nrm(ks[0], (BATCH, SEQ, D_MODEL), 1.0),
        "c": nrm(ks[1], (BATCH, D_MODEL), 1.0),
        "ctx": nrm(ks[2], (BATCH, CTX_LEN, D_MODEL), 1.0),
        "c_ctx": nrm(ks[3], (D_MODEL,), 1.0),
        "ada_w": nrm(ks[4], (DEPTH, D_MODEL, 3 * D_MODEL), 0.5 * D_MODEL ** -0.5),
        "ada_b": nrm(ks[5], (DEPTH, 3 * D_MODEL), 0.02),
        "norm_g": 1.0 + nrm(ks[6], (DEPTH, D_MODEL), 0.02),
        "w_in": nrm(ks[7], (DEPTH, D_MODEL, N_IN), D_MODEL ** -0.5),
        "mla_q_norm_g": 1.0 + nrm(ks[8], (DEPTH, MLA_Q_RANK), 0.02),
        "mla_kv_norm_g": 1.0 + nrm(ks[9], (DEPTH, MLA_KV_RANK), 0.02),
        "mla_w_uq": nrm(ks[10], (DEPTH, MLA_Q_RANK, MLA_HEADS * (MLA_NOPE + MLA_ROPE)), MLA_Q_RANK ** -0.5),
        "mla_w_ukv": nrm(ks[11], (DEPTH, MLA_KV_RANK, MLA_HEADS * (MLA_NOPE + MLA_V)), MLA_KV_RANK ** -0.5),
        "hg_lb_logits": nrm(ks[12], (DEPTH, 2, HG_KWIDTH), 0.5),
        "hg_norm_g": 1.0 + nrm(ks[13], (DEPTH, HG_WIDTH), 0.02),
        "conv_w": nrm(ks[14], (DEPTH, CV_K, CV_WIDTH), CV_K ** -0.5),
        "conv_b": nrm(ks[15], (DEPTH, CV_WIDTH), 0.02),
        "w_branch": nrm(ks[16], (DEPTH, N_BRANCH, BR_WIDTH, D_MODEL), BR_WIDTH ** -0.5),
        "w_out": nrm(ks[17], (DEPTH, D_MODEL, D_MODEL), D_MODEL ** -0.5),
        "final_norm_g": 1.0 + nrm(ks[18], (D_MODEL,), 0.02),
    }


def reference(x, c, ctx, c_ctx, ada_w, ada_b, norm_g, w_in, mla_q_norm_g, mla_kv_norm_g, mla_w_uq, mla_w_ukv,
              hg_lb_logits, hg_norm_g, conv_w, conv_b, w_branch, w_out, final_norm_g):
    n_ctx = ctx.shape[1]
    n_lat = x.shape[1]
    lb_all = jnp.cumsum(jax.nn.softmax(hg_lb_logits.astype(jnp.float32), axis=0), axis=0)
    lb_all = lb_all - lb_all[0:1]
    silu_c = jax.nn.silu(c)
    silu_cc = jax.nn.silu(c_ctx)
    h_lat, h_ctx = x, ctx
    for l in range(DEPTH):
        last = l == DEPTH - 1
        lo = n_ctx if last else 0
        mod = silu_c @ ada_w[l] + ada_b[l]
        mod_c = silu_cc @ ada_w[l] + ada_b[l]
        shift, scale, gate = jnp.split(mod[:, None, :], 3, axis=-1)
        shift_c, scale_c, gate_c = jnp.split(mod_c, 3, axis=-1)
        u = jnp.concatenate([rms_norm(h_ctx, norm_g[l]) * (1.0 + scale_c) + shift_c,
                             rms_norm(h_lat, norm_g[l]) * (1.0 + scale) + shift], axis=1)
        z = u @ w_in[l]
        (cq, ckv, kr, g_mla, hq, hi, hf_fwd, hf_bwd, g_hg, cx, cb, cc, g_cv, br_gate) = split_columns(z)
        y_mla = mla_branch(cq, ckv, kr, mla_q_norm_g[l], mla_kv_norm_g[l], mla_w_uq[l], mla_w_ukv[l],
                           n_ctx, not last) * jax.nn.silu(g_mla[:, lo:])
        y_hg = hgrn2_branch(hq, hi, hf_fwd, hf_bwd, lb_all[l], hg_norm_g[l], n_ctx)[:, lo:] * jax.nn.silu(g_hg[:, lo:])
        y_cv = conv_branch(cx, cb, cc, conv_w[l], conv_b[l], n_ctx, not last) * jax.nn.silu(g_cv[:, lo:])
        br_gate = jax.nn.sigmoid(br_gate[:, lo:])
        merged = (br_gate[..., :D_MODEL] * (y_mla @ w_branch[l, 0])
                  + br_gate[..., D_MODEL:2 * D_MODEL] * (y_hg @ w_branch[l, 1])
                  + br_gate[..., 2 * D_MODEL:] * (y_cv @ w_branch[l, 2]))
        out = merged @ w_out[l]
        h_lat = h_lat + gate * out[:, n_ctx - lo:]
        if not last:
            h_ctx = h_ctx + gate_c * out[:, :n_ctx]
    return rms_norm(h_lat, final_norm_g)
```

```python
import math
import numpy as np
import concourse.bass as bass
import concourse.mybir as mybir
from concourse.bass_utils import run_bass_kernel_spmd
from contextlib import ExitStack

F32 = mybir.dt.float32
BF16 = mybir.dt.bfloat16
U8 = mybir.dt.uint8
I32 = mybir.dt.int32
AF = mybir.ActivationFunctionType
ALU = mybir.AluOpType
AX = mybir.AxisListType

L = 4
D = 1024
NTOK = 2304
NCTX = 256
NLAT = 2048
NT = 18
EPS = 1e-6
MLA_SCALE = 96.0 ** -0.5
TG = [(0, 512), (512, 1024), (1024, 1536), (1536, 2048), (2048, 2304)]
QG = [(0, 256), (256, 768), (768, 1280), (1280, 1792), (1792, 2304)]

CH_CONV = 0
CH_HG = 16
CH_CQ = 32
CH_CKV = 34
CH_KRA = 35
CH_KRB = 36
CH_GMLA = 37
CH_GATE = 41
CH_HI = 65
NCH = 69

SM_NG = 0
SM_QG = SM_NG + L * 8
SM_KVG = SM_QG + L * 2
SM_HGG = SM_KVG + L
SM_CW = SM_HGG + L * 4
SM_CB = SM_CW + L * 12
SM_LB = SM_CB + L * 4
SM_ROPE = SM_LB + L * 8
NSM = SM_ROPE + 3


class Buf:
    __slots__ = ("name", "w", "r", "chan")

    def __init__(self, name):
        self.name = name
        self.w = {}
        self.r = {}
        self.chan = None


class Eng:
    def __init__(self, name, h):
        self.name = name
        self.h = h
        self.sem = None
        self.count = 0
        self.seen = {}
        self.pend_r = []
        self.pend_w = []


class Sync:
    def __init__(self, nc, es):
        self.nc = nc
        self.es = es
        self.nsem = 0
        self.E = {
            "pe": Eng("pe", nc.tensor),
            "act": Eng("act", nc.scalar),
            "dve": Eng("dve", nc.vector),
            "pool": Eng("pool", nc.gpsimd),
            "sp": Eng("sp", nc.sync),
        }
        for e in self.E.values():
            self._new_sem(e)
        self.chans = []
        self.halted = False

    def sem(self, name):
        self.nsem += 1
        return self.es.enter_context(self.nc.semaphore(f"{name}_{self.nsem}"))

    def _new_sem(self, e):
        e.sem = self.sem("e" + e.name)
        e.count = 0

    def _merge(self, d, src, skip=None):
        for k, (s, v) in src.items():
            if skip is not None and k == skip:
                continue
            if k not in d or d[k][1] < v:
                d[k] = (s, v)

    def _waits(self, e, reads, writes):
        d = {}
        own = e.sem.num if e.name in ("pe", "sp") else None
        for b in reads:
            self._merge(d, b.w)
        for b in writes:
            self._merge(d, b.w, skip=own)
            self._merge(d, b.r, skip=own)
        for k, (s, v) in d.items():
            if e.seen.get(k, 0) < v:
                e.h.wait_ge(s, v)
                e.seen[k] = v

    def op(self, eng, fn, r=(), w=(), mark=True):
        if self.halted:
            return None
        e = self.E[eng]
        self._waits(e, r, w)
        inst = fn(e.h)
        if mark:
            if e.count >= 60000:
                if not e.pend_r and not e.pend_w:
                    self._new_sem(e)
            inst.then_inc(e.sem, 1)
            e.count += 1
            tok = (e.sem, e.count)
            k = e.sem.num
            for b in list(r) + e.pend_r:
                b.r[k] = tok
            for b in list(w) + e.pend_w:
                b.w[k] = tok
            e.pend_r = []
            e.pend_w = []
        else:
            e.pend_r.extend(r)
            e.pend_w.extend(w)
        return inst

    def dma(self, q, out, in_, r=(), w=(), chan=None, **kw):
        if self.halted:
            return None
        e = self.E[q]
        self._waits(e, r, w)
        if chan.chan is None:
            chan.chan = [self.sem("d"), 0]
            self.chans.append(chan)
        inst = e.h.dma_start(out=out, in_=in_, **kw)
        chan.chan[1] += 16
        inst.then_inc(chan.chan[0], 16)
        tok = (chan.chan[0], chan.chan[1])
        k = chan.chan[0].num
        for b in r:
            b.r[k] = tok
        for b in w:
            b.w[k] = tok
        return inst

    def self_wait(self, eng):
        if self.halted:
            return
        e = self.E[eng]
        assert not e.pend_r and not e.pend_w
        e.h.wait_ge(e.sem, e.count)

    def drain(self, eng, bufs):
        if self.halted:
            return
        e = self.E[eng]
        self._waits(e, bufs, bufs)


class _Stop(Exception):
    pass


def build(NB=4, NL=4, dbg=False, stop_after=None):
    nc = bass.Bass("TRN2", target_bir_lowering=False)

    def din(name, shape):
        return nc.dram_tensor(name, shape, F32, kind="ExternalInput").ap()

    xin = din("xin", [NB, NTOK, D])
    cT_d = din("cT", [128, 8, 5])
    ada_d = din("ada", [L, 128, 8, 3072])
    adab_d = din("adab", [L, 1, 3072])
    w_in_d = din("w_in", [L, NCH, 128, 8, 128])
    w_uq_d = din("w_uq", [L, 16, 128, 2, 96])
    w_kn_d = din("w_kn", [L, 128, 8, 64])
    w_v_d = din("w_v", [L, 128, 512])
    w_br_d = din("w_br", [L, 24, 128, 4, 128])
    w_out_d = din("w_out", [L, 128, 8, 1024])
    smallp_d = din("smallp", [128, NSM])
    fng_d = din("fng", [1, D])
    out_d = nc.dram_tensor("out", [NB, NLAT, D], F32, kind="ExternalOutput").ap()
    hbuf_d = nc.dram_tensor("hbuf", [NTOK, D], F32, kind="Internal").ap()
    modrows_d = nc.dram_tensor("modrows", [L, 5, 3072], F32, kind="Internal").ap()
    ropetab_d = nc.dram_tensor("ropetab", [2, 32, NLAT], F32, kind="Internal").ap()
    if dbg:
        dbg_uT = nc.dram_tensor("dbg_uT", [128, 8, NTOK], BF16, kind="ExternalOutput").ap()
        dbg_yT = nc.dram_tensor("dbg_yT", [128, 12, NTOK], BF16, kind="ExternalOutput").ap()
        dbg_h = nc.dram_tensor("dbg_h", [NTOK, D], F32, kind="ExternalOutput").ap()
        dbg_of = nc.dram_tensor("dbg_of", [2, 128, NT * 128], F32, kind="ExternalOutput").ap()
        dbg_S = nc.dram_tensor("dbg_S", [2, 128, 36 * 64], BF16, kind="ExternalOutput").ap()
        dbg_q = nc.dram_tensor("dbg_q", [2, 4, 128, NTOK], BF16, kind="ExternalOutput").ap()
        dbg_st = nc.dram_tensor("dbg_st", [2, 3, 128, 36], F32, kind="ExternalOutput").ap()

    with ExitStack() as es:
        S = Sync(nc, es)

        def ck(n):
            if stop_after == n:
                S.halted = True

        def sb(name, shape, dt):
            return es.enter_context(nc.sbuf_tensor("s_" + name, shape, dt))

        def ps(name, shape, dt):
            return es.enter_context(nc.psum_tensor("p_" + name, shape, dt))

        ident_f = sb("ident_f", [128, 128], F32)
        ident_b = sb("ident_b", [128, 128], BF16)
        ones_f = sb("ones_f", [128, 128], F32)
        maskf32 = sb("maskf32", [128, 2, 128], F32)
        masks = sb("masks", [128, 2, 128], U8)
        smallp = sb("smallp", [128, NSM], F32)
        lbt = sb("lbt", [128, L, 8], F32)
        omlt = sb("omlt", [128, L, 8], F32)
        fng_bc = sb("fng_bc", [128, D], F32)
        resetm = sb("resetm", [128, 512], F32)
        uT = sb("uT", [128, 8, NTOK], BF16)
        yT = sb("yT", [128, 12, NTOK], BF16)
        wslot = [sb(f"wslot{i}", [128, 1024], BF16) for i in range(6)]
        modT = sb("modT", [128, 2, 16], F32)
        cscale = sb("cscale", [128, 2, 8], F32)
        gate_bc = sb("gate_bc", [128, 2, D], F32)
        stat = sb("stat", [128, 64], F32)
        AR_WORDS = 22000
        arena = sb("arena", [128, AR_WORDS], F32)

        B_const = Buf("const")
        B_small = Buf("small")
        B_uT = Buf("uT")
        B_yT = [Buf(f"yT{i}") for i in range(3)]
        B_w = [Buf(f"w{i}") for i in range(6)]
        B_mod = Buf("mod")
        B_gate = Buf("gatebc")
        B_stat = Buf("stat")
        B_hb = [Buf(f"hb{t}") for t in range(NT)]
        B_modrows = Buf("modrows")
        B_rope = Buf("ropetab")

        gen = [ps(f"gen{i}", [128, 512], F32) for i in range(3)]
        aux = [ps(f"scr{i}", [128, 512], F32) for i in range(2)]
        accp = [ps(f"accp{i}", [128, 512], F32) for i in range(2)]
        pbf = ps("pbf", [128, 1024], BF16)
        B_gen = [Buf(f"gen{i}") for i in range(3)]
        B_aux = [Buf(f"aux{i}") for i in range(2)]
        B_acc = [Buf("acc0"), Buf("acc1")]
        B_pbf = [Buf("pbf0"), Buf("pbf1")]
        ctr = {"gen": 0, "aux": 0, "pbf": 0, "acc": 0, "w": 0}

        def next_gen():
            i = ctr["gen"] % 3
            ctr["gen"] += 1
            return gen[i], B_gen[i]

        def next_aux():
            i = ctr["aux"] % 2
            ctr["aux"] += 1
            return aux[i], B_aux[i]

        def next_pbf():
            return pbf[:, 0:512], B_pbf[0]

        def next_acc():
            i = ctr["acc"] % 2
            ctr["acc"] += 1
            return accp[i], B_acc[i]

        class Arena:
            def __init__(self):
                self.off = 0
                self.bufs = {}

            def buf(self, name):
                if name not in self.bufs:
                    self.bufs[name] = Buf(name)
                return self.bufs[name]

            def reset(self):
                self.off = 0

            def f32(self, words, name):
                a = arena[:, self.off:self.off + words]
                self.off += words
                assert self.off <= AR_WORDS, (name, self.off)
                return a, self.buf(name)

            def bf(self, elems, name):
                words = (elems + 1) // 2
                a = arena[:, self.off:self.off + words].bitcast(BF16)
                self.off += words
                assert self.off <= AR_WORDS, (name, self.off)
                return a, self.buf(name)

        AR = Arena()
        B_arena_all = Buf("arena_all")

        def load_w(src_ap, kc, ncol):
            i = ctr["w"] % 6
            ctr["w"] += 1
            view = wslot[i][:, 0:kc * ncol].rearrange("p (k c) -> p k c", c=ncol)
            S.dma("pool", view, src_ap, r=(), w=(B_w[i],), chan=B_w[i])
            return view, B_w[i]

        S.dma("sp", smallp[:], smallp_d[:, :], w=(B_small,), chan=B_small)
        S.dma("sp", fng_bc[:], fng_d[0:1, :].partition_broadcast(128), w=(B_const,), chan=B_const)
        S.op("pool", lambda h: h.memset(ident_f[:], 1.0), w=(B_const,))
        S.op("pool", lambda h: h.affine_select(out=ident_f[:], in_=ident_f[:], pattern=[[-1, 128]],
                                                compare_op=ALU.is_equal, fill=0.0, base=0, channel_multiplier=1),
             r=(B_const,), w=(B_const,))
        S.op("pool", lambda h: h.tensor_copy(out=ident_b[:], in_=ident_f[:]), r=(B_const,), w=(B_const,))
        S.op("pool", lambda h: h.memset(ones_f[:], 1.0), w=(B_const,))
        S.op("pool", lambda h: h.memset(maskf32[:], 1.0), w=(B_const,))
        S.op("pool", lambda h: h.affine_select(out=maskf32[:, 0, :], in_=maskf32[:, 0, :], pattern=[[1, 128]],
                                                compare_op=ALU.is_ge, fill=0.0, base=0, channel_multiplier=-1),
             r=(B_const,), w=(B_const,))
        S.op("pool", lambda h: h.affine_select(out=maskf32[:, 1, :], in_=maskf32[:, 1, :], pattern=[[-1, 128]],
                                                compare_op=ALU.is_ge, fill=0.0, base=0, channel_multiplier=1),
             r=(B_const,), w=(B_const,))
        S.op("pool", lambda h: h.memset(maskf32[0:64, 0, 64:128], 0.0), r=(B_const,), w=(B_const,))
        S.op("pool", lambda h: h.memset(maskf32[64:128, 1, 0:64], 0.0), r=(B_const,), w=(B_const,))
        S.op("pool", lambda h: h.tensor_copy(out=masks[:], in_=maskf32[:]), r=(B_const,), w=(B_const,))
        S.op("pool", lambda h: h.memset(resetm[:], 1.0), w=(B_const,))
        S.op("pool", lambda h: h.memset(resetm[:].rearrange("p (c t) -> p c t", t=64)[:, :, 0:1], 0.0),
             r=(B_const,), w=(B_const,))

        def fence_arena(bufs):
            for en in ("pe", "act", "dve", "pool", "sp"):
                S.drain(en, bufs)

        ck(0)
        AR.reset()
        e_lb, B_elb = AR.f32(L * 8, "e_lb")
        s_lb, B_slb = AR.f32(8, "s_lb")
        e3 = e_lb.rearrange("p (l k) -> p l k", k=8)
        S.op("act", lambda h: h.activation(out=e_lb, in_=smallp[:, SM_LB:SM_LB + L * 8], func=AF.Exp),
             r=(B_small,), w=(B_elb,))
        S.op("dve", lambda h: h.tensor_tensor(out=s_lb, in0=e3[:, 0, :], in1=e3[:, 1, :], op=ALU.add),
             r=(B_elb,), w=(B_slb,))
        S.op("dve", lambda h: h.tensor_tensor(out=s_lb, in0=s_lb, in1=e3[:, 2, :], op=ALU.add),
             r=(B_elb, B_slb), w=(B_slb,))
        S.op("dve", lambda h: h.tensor_tensor(out=s_lb, in0=s_lb, in1=e3[:, 3, :], op=ALU.add),
             r=(B_elb, B_slb), w=(B_slb,))
        S.op("dve", lambda h: h.reciprocal(out=s_lb, in_=s_lb), r=(B_slb,), w=(B_slb,))
        S.op("dve", lambda h: h.memset(lbt[:, 0, :], 0.0), w=(B_const,))
        for l in range(1, L):
            S.op("dve", lambda h, l=l: h.tensor_tensor(out=e3[:, l, :], in0=e3[:, l, :], in1=s_lb, op=ALU.mult),
                 r=(B_elb, B_slb), w=(B_elb,))
            S.op("dve", lambda h, l=l: h.tensor_tensor(out=lbt[:, l, :], in0=lbt[:, l - 1, :], in1=e3[:, l, :],
                                                        op=ALU.add),
                 r=(B_elb, B_const), w=(B_const,))
        S.op("dve", lambda h: h.tensor_scalar(out=omlt[:], in0=lbt[:], scalar1=-1.0, scalar2=1.0,
                                               op0=ALU.mult, op1=ALU.add), r=(B_const,), w=(B_const,))

        ck(1)
        fence_arena([B_elb, B_slb])
        AR.reset()
        cTs, B_cT = AR.f32(40, "cT")
        adat = [AR.f32(4096, f"adat{i}") for i in range(2)]
        biasr, B_biasr = AR.f32(3072, "biasr")
        mrow = [AR.f32(512, f"mrow{i}") for i in range(2)]
        cT3 = cTs.rearrange("p (k r) -> p k r", r=5)
        S.dma("sp", cT3, cT_d[:, :, :], w=(B_cT,), chan=B_cT)
        S.op("act", lambda h: h.activation(out=cTs, in_=cTs, func=AF.Silu), r=(B_cT,), w=(B_cT,))
        ci = 0
        for l in range(L):
            S.dma("sp", biasr[0:5, :], adab_d[l, 0:1, :].partition_broadcast(5), w=(B_biasr,), chan=B_biasr)
            for cb in range(6):
                at, B_at = adat[ci % 2]
                mr, B_mr = mrow[ci % 2]
                ci += 1
                at3 = at.rearrange("p (k c) -> p k c", c=512)
                S.dma("sp", at3, ada_d[l, :, :, cb * 512:(cb + 1) * 512], w=(B_at,), chan=B_at)
                pg, B_pg = next_gen()
                for kc in range(8):
                    S.op("pe", lambda h, kc=kc: h.matmul(pg[0:5, :], lhsT=cT3[:, kc, :], rhs=at3[:, kc, :],
                                                           start=(kc == 0), stop=(kc == 7)),
                         r=(B_cT, B_at), w=(B_pg,), mark=(kc == 7))
                S.op("dve", lambda h: h.tensor_tensor(out=mr[0:5, :], in0=pg[0:5, :],
                                                       in1=biasr[0:5, cb * 512:(cb + 1) * 512], op=ALU.add),
                     r=(B_pg, B_biasr), w=(B_mr,))
                S.dma("sp", modrows_d[l, :, cb * 512:(cb + 1) * 512], mr[0:5, :], r=(B_mr,), w=(B_modrows,),
                      chan=B_mr)

        ck(2)
        fence_arena([B_cT, B_biasr] + [b for _, b in adat] + [b for _, b in mrow])
        AR.reset()
        rowf, B_r0 = AR.f32(2048, "rowf")
        colf, B_r1 = AR.f32(2048, "colf")
        t_a, B_r2 = AR.f32(2048, "t_a")
        t_b, B_r3 = AR.f32(2048, "t_b")
        t_c, B_r4 = AR.f32(2048, "t_c")
        t_i = arena[:, AR.off:AR.off + 2048].bitcast(I32)
        AR.off += 2048
        B_r5 = AR.buf("t_i")
        RF = smallp[:, SM_ROPE:SM_ROPE + 1]
        RA = smallp[:, SM_ROPE + 1:SM_ROPE + 2]
        RS = smallp[:, SM_ROPE + 2:SM_ROPE + 3]
        S.op("pool", lambda h: h.iota(rowf.rearrange("p (a b) -> p a b", b=64), pattern=[[1, 32], [0, 64]], base=0,
                                       channel_multiplier=0, allow_small_or_imprecise_dtypes=True), w=(B_r0,))
        S.op("pool", lambda h: h.iota(colf.rearrange("p (a b) -> p a b", b=64), pattern=[[0, 32], [1, 64]], base=0,
                                       channel_multiplier=0, allow_small_or_imprecise_dtypes=True), w=(B_r1,))
        S.op("dve", lambda h: h.tensor_tensor(out=colf, in0=colf, in1=rowf, op=ALU.subtract),
             r=(B_r0, B_r1), w=(B_r1,))
        S.op("dve", lambda h: h.scalar_tensor_tensor(out=t_a, in0=colf, scalar=RA, in1=rowf, op0=ALU.mult,
                                                      op1=ALU.add), r=(B_r0, B_r1, B_small), w=(B_r2,))
        S.op("dve", lambda h: h.tensor_scalar(out=t_a, in0=t_a, scalar1=RF, scalar2=None, op0=ALU.mult),
             r=(B_r2, B_small), w=(B_r2,))
        S.op("dve", lambda h: h.tensor_scalar(out=t_b, in0=t_a, scalar1=1.0 / (2 * math.pi), scalar2=None,
                                               op0=ALU.mult), r=(B_r2,), w=(B_r3,))
        S.op("dve", lambda h: h.tensor_copy(out=t_i, in_=t_b), r=(B_r3,), w=(B_r5,))
        S.op("dve", lambda h: h.tensor_copy(out=t_b, in_=t_i), r=(B_r5,), w=(B_r3,))
        S.op("dve", lambda h: h.scalar_tensor_tensor(out=t_a, in0=t_b, scalar=-2 * math.pi, in1=t_a, op0=ALU.mult,
                                                      op1=ALU.add), r=(B_r2, B_r3), w=(B_r2,))
        S.op("act", lambda h: h.activation(out=t_b, in_=t_a, func=AF.Sin, scale=0.25), r=(B_r2,), w=(B_r3,))
        S.op("dve", lambda h: h.tensor_scalar(out=t_c, in0=t_a, scalar1=0.25, scalar2=math.pi / 2, op0=ALU.mult,
                                               op1=ALU.add), r=(B_r2,), w=(B_r4,))
        S.op("act", lambda h: h.activation(out=t_c, in_=t_c, func=AF.Sin), r=(B_r4,), w=(B_r4,))
        S.op("dve", lambda h: h.scalar_tensor_tensor(out=t_a, in0=t_b, scalar=2.0, in1=t_c, op0=ALU.mult,
                                                      op1=ALU.mult), r=(B_r3, B_r4), w=(B_r2,))
        S.op("dve", lambda h: h.tensor_tensor(out=t_b, in0=t_b, in1=t_b, op=ALU.mult), r=(B_r3,), w=(B_r3,))
        S.op("dve", lambda h: h.tensor_scalar(out=t_b, in0=t_b, scalar1=-2.0, scalar2=1.0, op0=ALU.mult,
                                               op1=ALU.add), r=(B_r3,), w=(B_r3,))
        S.op("dve", lambda h: h.scalar_tensor_tensor(out=t_c, in0=t_a, scalar=2.0, in1=t_b, op0=ALU.mult,
                                                      op1=ALU.mult), r=(B_r2, B_r3), w=(B_r4,))
        S.op("dve", lambda h: h.tensor_tensor(out=rowf, in0=t_a, in1=t_a, op=ALU.mult), r=(B_r2,), w=(B_r0,))
        S.op("dve", lambda h: h.tensor_scalar(out=rowf, in0=rowf, scalar1=-2.0, scalar2=1.0, op0=ALU.mult,
                                               op1=ALU.add), r=(B_r0,), w=(B_r0,))
        S.op("dve", lambda h: h.tensor_scalar(out=t_c, in0=t_c, scalar1=RS, scalar2=None, op0=ALU.mult),
             r=(B_r4, B_small), w=(B_r4,))
        S.dma("sp", ropetab_d[0, :, :], rowf[64:96, :], r=(B_r0,), w=(B_rope,), chan=B_r0)
        S.dma("sp", ropetab_d[1, :, :], t_c[64:96, :], r=(B_r4,), w=(B_rope,), chan=B_r4)
        pro_bufs = [B_elb, B_slb, B_cT, B_biasr, B_r0, B_r1, B_r2, B_r3, B_r4, B_r5] + \
                   [b for _, b in adat] + [b for _, b in mrow]

        fence_arena(pro_bufs)
        ck(3)

        def bc3(ap2, n):
            g = ap2.shape[1]
            return ap2.rearrange("p (g o) -> p g o", o=1).broadcast_to([ap2.shape[0], g, n])

        def proj_fm(wv, Bw, kcn, rhs_fn, rbufs, ntok_groups, consumer, m=128):
            for (a0, a1) in ntok_groups:
                pg, B_pg = next_gen()
                n = a1 - a0
                for kc in range(kcn):
                    S.op("pe", lambda h, kc=kc: h.matmul(pg[0:m, 0:n], lhsT=wv[:, kc, 0:m], rhs=rhs_fn(kc, a0, a1),
                                                           start=(kc == 0), stop=(kc == kcn - 1)),
                         r=(Bw,) + tuple(rbufs), w=(B_pg,), mark=(kc == kcn - 1))
                consumer(a0, a1, pg, B_pg)

        def uT_rhs(kc, a0, a1):
            return uT[:, kc, a0:a1]

        def layer(b, l, last):
            r_b = b
            for ri, row in enumerate((r_b, 4)):
                S.dma("sp", modT[:, ri, :], modrows_d[l, row, 0:2048].rearrange("(c p) -> p c", p=128),
                      r=(B_modrows,), w=(B_mod,), chan=B_mod, allow_slow_non_contiguous=True)
                S.drain("sp", (B_mod,))
            for ri, row in enumerate((r_b, 4)):
                S.dma("sp", gate_bc[:, ri, :], modrows_d[l, row:row + 1, 2048:3072].partition_broadcast(128),
                      r=(B_modrows,), w=(B_gate,), chan=B_gate)
                S.drain("sp", (B_gate,))
            S.op("dve", lambda h: h.tensor_scalar(out=cscale[:], in0=modT[:, :, 8:16], scalar1=1.0, scalar2=None,
                                                   op0=ALU.add), r=(B_mod,), w=(B_mod,))
            ng = smallp[:, SM_NG + l * 8:SM_NG + (l + 1) * 8]
            for ri in range(2):
                S.op("dve", lambda h, ri=ri: h.tensor_tensor(out=cscale[:, ri, :], in0=cscale[:, ri, :], in1=ng,
                                                              op=ALU.mult), r=(B_mod, B_small), w=(B_mod,))

            ck(4)
            AR.reset()
            xs = [AR.f32(1024, f"xs{i}") for i in range(8)]
            junk, B_junk = AR.bf(1024, "junk")
            pa_bufs = [bb for _, bb in xs] + [B_junk]
            groups = [(0, 2, 1), (2, 6, 0), (6, 10, 0), (10, 14, 0), (14, 18, 0)]
            si = 0
            for (t0, t1, ri) in groups:
                tiles = []
                for t in range(t0, t1):
                    xt, B_xt = xs[si % 8]
                    si += 1
                    src = xin[b, t * 128:(t + 1) * 128, :] if l == 0 else hbuf_d[t * 128:(t + 1) * 128, :]
                    S.dma("sp", xt, src, r=(() if l == 0 else (B_hb[t],)), w=(B_xt,), chan=B_xt)
                    S.op("act", lambda h: h.activation(out=junk, in_=xt, func=AF.Square,
                                                        accum_out=stat[:, t:t + 1]),
                         r=(B_xt,), w=(B_junk, B_stat))
                    S.op("act", lambda h: h.activation(out=stat[:, t:t + 1], in_=stat[:, t:t + 1], func=AF.Sqrt,
                                                        scale=1.0 / D, bias=EPS), r=(B_stat,), w=(B_stat,))
                    S.op("dve", lambda h: h.reciprocal(out=stat[:, t:t + 1], in_=stat[:, t:t + 1]),
                         r=(B_stat,), w=(B_stat,))
                    S.op("dve", lambda h: h.tensor_scalar(out=xt, in0=xt, scalar1=stat[:, t:t + 1], scalar2=None,
                                                           op0=ALU.mult), r=(B_xt, B_stat), w=(B_xt,))
                    tiles.append((xt, B_xt))
                nt_ = t1 - t0
                for kc in range(8):
                    pg, B_pg = next_gen()
                    for ti, (xt, B_xt) in enumerate(tiles):
                        S.op("pe", lambda h, ti=ti, xt=xt: h.transpose(pg[:, ti * 128:(ti + 1) * 128],
                                                                         xt[:, kc * 128:(kc + 1) * 128], ident_f[:]),
                             r=(B_xt, B_const), w=(B_pg,), mark=(ti == nt_ - 1))
                    S.op("act", lambda h: h.activation(out=uT[:, kc, t0 * 128:t1 * 128], in_=pg[:, 0:nt_ * 128],
                                                        func=AF.Identity, scale=cscale[:, ri, kc:kc + 1],
                                                        bias=modT[:, ri, kc:kc + 1]),
                         r=(B_pg, B_mod), w=(B_uT,))
            fence_arena(pa_bufs + [B_stat])
            if dbg and b == 0 and l == 0:
                S.dma("sp", dbg_uT[:, :, :], uT[:], r=(B_uT,), w=(), chan=B_uT)
                S.drain("sp", (B_uT,))

            ck(5)
            AR.reset()
            cxs, B_cxs = AR.f32(NTOK, "cxs")
            uu, B_uu = AR.f32(NTOK, "uu")
            cacc, B_cacc = AR.f32(NTOK, "cacc")
            cbs, B_cbs = AR.f32(NTOK, "cbs")
            sgc, B_sgc = AR.f32(NTOK, "sgc")
            pb_bufs = [B_cxs, B_uu, B_cacc, B_cbs, B_sgc]
            for j in range(4):
                wv = [load_w(w_in_d[l, CH_CONV + 4 * j + q], 8, 128) for q in range(4)]

                def c_cx(a0, a1, pg, B_pg):
                    S.op("act", lambda h: h.copy(out=cxs[:, a0:a1], in_=pg[:, 0:a1 - a0]), r=(B_pg,), w=(B_cxs,))

                def c_cc(a0, a1, pg, B_pg):
                    S.op("dve", lambda h: h.tensor_tensor(out=uu[:, a0:a1], in0=pg[:, 0:a1 - a0], in1=cxs[:, a0:a1],
                                                           op=ALU.mult), r=(B_pg, B_cxs), w=(B_uu,))

                def c_cb(a0, a1, pg, B_pg):
                    S.op("act", lambda h: h.copy(out=cbs[:, a0:a1], in_=pg[:, 0:a1 - a0]), r=(B_pg,), w=(B_cbs,))

                def c_g(a0, a1, pg, B_pg):
                    S.op("act", lambda h: h.activation(out=sgc[:, a0:a1], in_=pg[:, 0:a1 - a0], func=AF.Silu),
                         r=(B_pg,), w=(B_sgc,))

                for q, cons in enumerate((c_cx, c_cc, c_cb, c_g)):
                    proj_fm(wv[q][0], wv[q][1], 8, uT_rhs, (B_uT,), TG, cons)
                cw = lambda k: smallp[:, SM_CW + l * 12 + k * 4 + j:SM_CW + l * 12 + k * 4 + j + 1]
                cbias = smallp[:, SM_CB + l * 4 + j:SM_CB + l * 4 + j + 1]
                S.op("dve", lambda h: h.tensor_scalar(out=cacc, in0=uu, scalar1=cw(1), scalar2=cbias, op0=ALU.mult,
                                                       op1=ALU.add), r=(B_uu, B_small), w=(B_cacc,))
                for (s0, s1) in ((0, NCTX), (NCTX, NTOK)):
                    S.op("dve", lambda h: h.scalar_tensor_tensor(out=cacc[:, s0 + 1:s1], in0=uu[:, s0:s1 - 1],
                                                                  scalar=cw(0), in1=cacc[:, s0 + 1:s1],
                                                                  op0=ALU.mult, op1=ALU.add),
                         r=(B_uu, B_small, B_cacc), w=(B_cacc,))
                    S.op("dve", lambda h: h.scalar_tensor_tensor(out=cacc[:, s0:s1 - 1], in0=uu[:, s0 + 1:s1],
                                                                  scalar=cw(2), in1=cacc[:, s0:s1 - 1],
                                                                  op0=ALU.mult, op1=ALU.add),
                         r=(B_uu, B_small, B_cacc), w=(B_cacc,))
                S.op("pool", lambda h: h.tensor_tensor(out=cacc, in0=cacc, in1=cbs, op=ALU.mult),
                     r=(B_cacc, B_cbs), w=(B_cacc,))
                S.op("pool", lambda h: h.tensor_tensor(out=yT[:, 8 + j, :], in0=cacc, in1=sgc, op=ALU.mult),
                     r=(B_cacc, B_sgc), w=(B_yT[2],))
            fence_arena(pb_bufs)

            ck(6)
            hgrn2(b, l)
            ck(7)
            mla(b, l)
            ck(8)
            if dbg and b == 0 and l == 0:
                S.dma("sp", dbg_yT[:, :, :], yT[:], r=tuple(B_yT), w=(), chan=B_yT[0])
                S.drain("sp", (B_yT[0],))
            merge_out(b, l, last)

        def hgrn2(b, l):
            AR.reset()
            qd, B_qd = AR.bf(NTOK, "qd")
            qz = [[AR.bf(NTOK, f"qz{i}{hh}") for hh in range(2)] for i in range(2)]
            kdz = [AR.bf(NTOK, f"kdz{hh}") for hh in range(2)]
            kz = [AR.bf(NT * 128, f"kz{i}") for i in range(2)]
            itz = [AR.bf(NT * 128, f"itz{i}") for i in range(2)]
            sgh, B_sgh = AR.bf(NTOK, "sgh")
            Sall, B_Sall = AR.bf(36 * 64, "Sall")
            oacc, B_oacc = AR.f32(NT * 128, "oacc")
            onb, B_onb = qd, B_qd
            Asb = [[AR.bf(128, f"A{d}{i}") for i in range(2)] for d in range(2)]
            Ub = [AR.f32(64, f"U{i}") for i in range(2)]
            hqs, B_hqs = AR.f32(512, "hqs")
            T1, B_T1 = AR.f32(512, "T1")
            T2, B_T2 = AR.f32(512, "T2")
            T3, B_T3 = AR.f32(512, "T3")
            T4, B_T4 = AR.f32(512, "T4")
            T5, B_T5 = AR.f32(512, "T5")
            refs, B_refs = AR.f32(36, "refs")
            lasts, B_lasts = AR.f32(36, "lasts")
            alph, B_alph = AR.f32(36, "alph")
            ssq, B_ssq = AR.f32(36, "ssq")
            allb = [B_qd, B_sgh, B_Sall, B_oacc, B_hqs, B_T1, B_T2, B_T3, B_T4, B_T5, B_refs,
                    B_lasts, B_alph, B_ssq] + [x[1] for x in kz] + [x[1] for x in kdz] + [x[1] for x in itz] + \
                   [x[1] for q_ in qz for x in q_] + [x[1] for d in Asb for x in d] + [x[1] for x in Ub]
            kz3 = [k[0].rearrange("p (t c) -> p t c", c=128) for k in kz]
            itz3 = [k[0].rearrange("p (t c) -> p t c", c=128) for k in itz]
            Sall3 = Sall.rearrange("p (c v) -> p c v", v=64)
            oacc3 = oacc.rearrange("p (t c) -> p t c", c=128)
            onb3 = onb.rearrange("p (t c) -> p t c", c=128)
            for i in range(2):
                for hh in range(2):
                    S.op("pool", lambda h, i=i, hh=hh: h.memset(qz[i][hh][0], 0.0), w=(qz[i][hh][1],))
                S.op("pool", lambda h, i=i: h.memset(kdz[i][0], 0.0), w=(kdz[i][1],))
                S.op("pool", lambda h, i=i: h.memset(itz[i][0], 0.0), w=(itz[i][1],))
                for d in range(2):
                    S.op("pool", lambda h, i=i, d=d: h.memset(Asb[d][i][0], 0.0), w=(Asb[d][i][1],))
            ck(10)
            for hp in range(4):
                w_q = load_w(w_in_d[l, CH_HG + 4 * hp + 0], 8, 128)
                w_f = [load_w(w_in_d[l, CH_HG + 4 * hp + 1 + d], 8, 128) for d in range(2)]
                w_g = load_w(w_in_d[l, CH_HG + 4 * hp + 3], 8, 128)
                w_i = load_w(w_in_d[l, CH_HI + hp], 8, 128)
                proj_fm(w_g[0], w_g[1], 8, uT_rhs, (B_uT,), TG,
                        lambda a0, a1, pg, B_pg: S.op("act", lambda h: h.activation(out=sgh[:, a0:a1],
                                                                                    in_=pg[:, 0:a1 - a0],
                                                                                    func=AF.Silu),
                                                     r=(B_pg,), w=(B_sgh,)))
                for t in range(NT):
                    pg, B_pg = next_gen()
                    for kc in range(8):
                        S.op("pe", lambda h, kc=kc: h.matmul(pg[:, 0:128], lhsT=uT[:, kc, t * 128:(t + 1) * 128],
                                                               rhs=w_i[0][:, kc, :], start=(kc == 0), stop=(kc == 7)),
                             r=(B_uT, w_i[1]), w=(B_pg,), mark=(kc == 7))
                    S.op("act", lambda h: h.copy(out=itz3[0][0:64, t, :], in_=pg[0:64, 0:128]), r=(B_pg,),
                         w=(itz[0][1],))
                    S.op("dve", lambda h: h.tensor_copy(out=itz3[1][64:128, t, :], in_=pg[64:128, 0:128]),
                         r=(B_pg,), w=(itz[1][1],))
                if hp == 0:
                    ck(11)
                lbv = lbt[:, l, :]
                for d in range(2):
                    lbc = lbt[:, l, d * 4 + hp:d * 4 + hp + 1]
                    omc = omlt[:, l, d * 4 + hp:d * 4 + hp + 1]
                    for (a0, a1) in TG:
                        n = a1 - a0
                        nch = n // 64
                        c0 = a0 // 64
                        pq, B_pq = next_gen()
                        for kc in range(8):
                            S.op("pe", lambda h, kc=kc: h.matmul(pq[:, 0:n], lhsT=w_q[0][:, kc, :],
                                                                   rhs=uT[:, kc, a0:a1], start=(kc == 0),
                                                                   stop=(kc == 7)),
                                 r=(w_q[1], B_uT), w=(B_pq,), mark=(kc == 7))
                        pf, B_pf = next_gen()
                        for kc in range(8):
                            S.op("pe", lambda h, kc=kc: h.matmul(pf[:, 0:n], lhsT=w_f[d][0][:, kc, :],
                                                                   rhs=uT[:, kc, a0:a1], start=(kc == 0),
                                                                   stop=(kc == 7)),
                                 r=(w_f[d][1], B_uT), w=(B_pf,), mark=(kc == 7))
                        S.op("act", lambda h: h.copy(out=hqs[:, 0:n], in_=pq[:, 0:n]), r=(B_pq,), w=(B_hqs,))
                        S.op("act", lambda h: h.activation(out=T1[:, 0:n], in_=pf[:, 0:n], func=AF.Sigmoid),
                             r=(B_pf,), w=(B_T1,))
                        S.op("dve", lambda h: h.tensor_scalar(out=T1[:, 0:n], in0=T1[:, 0:n], scalar1=omc,
                                                               scalar2=lbc, op0=ALU.mult, op1=ALU.add),
                             r=(B_T1, B_const), w=(B_T1,))
                        S.op("act", lambda h: h.activation(out=T2[:, 0:n], in_=T1[:, 0:n], func=AF.Ln),
                             r=(B_T1,), w=(B_T2,))
                        S.op("pool", lambda h: h.tensor_scalar(out=T1[:, 0:n], in0=T1[:, 0:n], scalar1=-1.0,
                                                                scalar2=1.0, op0=ALU.mult, op1=ALU.add),
                             r=(B_T1, B_T2), w=(B_T1,))
                        S.op("dve", lambda h: h.tensor_tensor_scan(out=T3[:, 0:n], data0=resetm[:, 0:n],
                                                                    data1=T2[:, 0:n], initial=0.0, op0=ALU.mult,
                                                                    op1=ALU.add), r=(B_T2, B_const), w=(B_T3,))
                        T3v = T3[:, 0:n].rearrange("p (c t) -> p c t", t=64)
                        T2v = T2[:, 0:n].rearrange("p (c t) -> p c t", t=64)
                        if d == 0:
                            cum, B_cum, cumv = T3, B_T3, T3v
                            iref, ilast = 31, 63
                        else:
                            S.op("dve", lambda h: h.tensor_tensor(out=T2[:, 0:n], in0=T2[:, 0:n], in1=T3[:, 0:n],
                                                                   op=ALU.subtract), r=(B_T2, B_T3), w=(B_T2,))
                            S.op("dve", lambda h: h.tensor_tensor(out=T2v, in0=T2v,
                                                                   in1=T3v[:, :, 63:64].broadcast_to([128, nch, 64]),
                                                                   op=ALU.add), r=(B_T2, B_T3), w=(B_T2,))
                            cum, B_cum, cumv = T2, B_T2, T2v
                            iref, ilast = 32, 0
                        S.op("dve", lambda h: h.tensor_copy(out=refs[:, c0:c0 + nch], in_=cumv[:, :, iref]),
                             r=(B_cum,), w=(B_refs,))
                        S.op("dve", lambda h: h.tensor_tensor(out=cumv, in0=cumv,
                                                               in1=bc3(refs[:, c0:c0 + nch], 64), op=ALU.subtract),
                             r=(B_cum, B_refs), w=(B_cum,))
                        S.op("dve", lambda h: h.tensor_copy(out=lasts[:, c0:c0 + nch], in_=cumv[:, :, ilast]),
                             r=(B_cum,), w=(B_lasts,))
                        S.op("act", lambda h: h.activation(out=T4[:, 0:n], in_=cum[:, 0:n], func=AF.Exp),
                             r=(B_cum,), w=(B_T4,))
                        S.op("act", lambda h: h.activation(out=T5[:, 0:n], in_=cum[:, 0:n], func=AF.Exp,
                                                            scale=-1.0), r=(B_cum,), w=(B_T5,))
                        S.op("dve", lambda h: h.tensor_tensor(out=qd[:, a0:a1], in0=hqs[:, 0:n], in1=T4[:, 0:n],
                                                               op=ALU.mult), r=(B_hqs, B_T4), w=(B_qd,))
                        hv = hqs[:, 0:n].rearrange("p (c two t) -> p c two t", two=2, t=64)
                        ev = T4[:, 0:n].rearrange("p (c two t) -> p c two t", two=2, t=64)
                        for i in range(2):
                            for hh in range(2):
                                rw = slice(hh * 64, (hh + 1) * 64)
                                qv = qz[i][hh][0][:, a0:a1].rearrange("p (c two t) -> p c two t", two=2, t=64)
                                S.op("pool", lambda h, i=i, qv=qv, rw=rw: h.tensor_tensor(
                                    out=qv[rw, :, i, :], in0=hv[rw, :, i, :], in1=ev[rw, :, i, :], op=ALU.mult),
                                     r=(B_hqs, B_T4), w=(qz[i][hh][1],))
                        for hh in range(2):
                            rw = slice(hh * 64, (hh + 1) * 64)
                            S.op("dve", lambda h, rw=rw, hh=hh: h.tensor_tensor(out=kdz[hh][0][rw, a0:a1],
                                                                                 in0=T1[rw, 0:n], in1=T5[rw, 0:n],
                                                                                 op=ALU.mult),
                                 r=(B_T1, B_T5), w=(kdz[hh][1],))
                    if hp == 0 and d == 0:
                        ck(12)
                    if d == 0:
                        S.op("dve", lambda h: h.tensor_tensor(out=alph[:, 0:35], in0=lasts[:, 0:35],
                                                               in1=refs[:, 1:36], op=ALU.add),
                             r=(B_lasts, B_refs), w=(B_alph,))
                        order = list(range(36))
                        prev_of = {c: c - 1 for c in range(1, 36)}
                    else:
                        S.op("dve", lambda h: h.tensor_tensor(out=alph[:, 1:36], in0=lasts[:, 1:36],
                                                               in1=refs[:, 0:35], op=ALU.add),
                             r=(B_lasts, B_refs), w=(B_alph,))
                        S.op("dve", lambda h: h.tensor_tensor(out=alph[:, 0:1], in0=lasts[:, 0:1],
                                                               in1=refs[:, 35:36], op=ALU.add),
                             r=(B_lasts, B_refs), w=(B_alph,))
                        order = [3, 2, 1, 0] + list(range(35, 3, -1))
                        prev_of = {order[i]: order[i - 1] for i in range(1, 36)}
                    na_ = 35 if d == 0 else 36
                    S.op("act", lambda h: h.activation(out=alph[:, 0:na_], in_=alph[:, 0:na_], func=AF.Exp),
                         r=(B_alph,), w=(B_alph,))
                    if hp == 0 and d == 0:
                        ck(13)
                    for t in range(NT):
                        for hh in range(2):
                            pt, B_pt = next_pbf()
                            S.op("pe", lambda h: h.transpose(pt[:, 0:128], kdz[hh][0][:, t * 128:(t + 1) * 128],
                                                              ident_b[:]), r=(kdz[hh][1], B_const), w=(B_pt,))
                            if hh == 0:
                                S.op("act", lambda h: h.copy(out=kz3[0][:, t, :], in_=pt[:, 0:128]), r=(B_pt,),
                                     w=(kz[0][1],))
                            else:
                                S.op("dve", lambda h: h.tensor_copy(out=kz3[1][:, t, :], in_=pt[:, 0:128]),
                                     r=(B_pt,), w=(kz[1][1],))
                    if hp == 0 and d == 0:
                        ck(14)
                    first = order[0]
                    import os as _os
                    if not _os.environ.get("DBG_NO_MEMSET"):
                        S.op("dve", lambda h: h.memset(Sall3[:, first, :], 0.0), w=(B_Sall,))
                    Pregs = {}
                    for t in (range(NT) if d == 0 else [1, 0] + list(range(NT - 1, 1, -1))):
                        pa, B_pa = next_gen() if _os.environ.get("DBG_P_GEN") else (next_acc() if _os.environ.get("DBG_P_ACC") else next_aux())
                        for cc in range(2):
                            c = 2 * t + cc
                            Pr = pa[:, cc * 64:(cc + 1) * 64]
                            if _os.environ.get("DBG_NO_P"):
                                continue
                            if _os.environ.get("DBG_P_CC0") and cc == 1:
                                continue
                            if _os.environ.get("DBG_P_CC1") and cc == 0:
                                continue
                            S.op("pe", lambda h: h.matmul(Pr, lhsT=kz3[0][:, t, :], rhs=itz3[cc][:, t, 0:64],
                                                           start=True, stop=False),
                                 r=(kz[0][1], itz[cc][1]), w=(B_pa,), mark=False)
                            S.op("pe", lambda h: h.matmul(Pr, lhsT=kz3[1][:, t, :], rhs=itz3[cc][:, t, 64:128],
                                                           start=False, stop=True),
                                 r=(kz[1][1], itz[cc][1]), w=(B_pa,), mark=(cc == 1))
                            Pregs[c] = (Pr, B_pa)
                        import os as _os
                        for c in ((2 * t, 2 * t + 1) if d == 0 else (2 * t + 1, 2 * t)):
                            if _os.environ.get("DBG_SKIP_CHAIN"):
                                continue
                            Pr, B_pr = Pregs[c]
                            if c == first:
                                U, B_U = Ub[0]
                                S.op("dve", lambda h: h.tensor_copy(out=U, in_=Pr), r=(B_pr,), w=(B_U,))
                                ui = 0
                            else:
                                p = prev_of[c]
                                U, B_U = Ub[ui]
                                Un, B_Un = Ub[1 - ui]
                                S.op("dve", lambda h: h.tensor_scalar(out=Sall3[:, c, :], in0=U,
                                                                       scalar1=alph[:, p:p + 1], scalar2=None,
                                                                       op0=ALU.mult),
                                     r=(B_U, B_alph), w=(B_Sall,))
                                S.op("dve", lambda h: h.scalar_tensor_tensor(out=Un, in0=U, scalar=alph[:, p:p + 1],
                                                                              in1=Pr, op0=ALU.mult, op1=ALU.add),
                                     r=(B_U, B_alph, B_pr), w=(B_Un,))
                                ui = 1 - ui
                        if hp == 0 and d == 0 and t < 3:
                            ck(150 + t)
                    if hp == 0 and d == 0:
                        ck(15)
                    for t in range(NT):
                        po, B_po = next_acc()
                        for hh in range(2):
                            rows = slice(hh * 64, (hh + 1) * 64)
                            pa, B_pa = next_aux()
                            A, B_A = Asb[d][(2 * t + hh) % 2]
                            tl = slice(t * 128, (t + 1) * 128)
                            S.op("pe", lambda h: h.matmul(pa[:, 0:128], lhsT=kdz[hh][0][:, tl], rhs=qd[:, tl],
                                                           start=True, stop=True), r=(kdz[hh][1], B_qd), w=(B_pa,))
                            S.op("dve", lambda h: h.copy_predicated(out=A, mask=masks[:, d, :], data=pa[:, 0:128]),
                                 r=(B_pa, B_const), w=(B_A,))
                            oreg = po[:, hh * 64:(hh + 1) * 64]
                            hc = slice(hh * 64, (hh + 1) * 64)
                            S.op("pe", lambda h: h.matmul(oreg, lhsT=A, rhs=itz3[0][:, t, hc], start=True,
                                                           stop=False), r=(B_A, itz[0][1]), w=(B_po,), mark=False)
                            S.op("pe", lambda h: h.matmul(oreg, lhsT=A, rhs=itz3[1][:, t, hc], start=False,
                                                           stop=False), r=(B_A, itz[1][1]), w=(B_po,), mark=False)
                            S.op("pe", lambda h: h.matmul(oreg, lhsT=qz[0][hh][0][:, tl], rhs=Sall3[:, 2 * t, :],
                                                           start=False, stop=False),
                                 r=(qz[0][hh][1], B_Sall), w=(B_po,), mark=False)
                            S.op("pe", lambda h: h.matmul(oreg, lhsT=qz[1][hh][0][:, tl],
                                                           rhs=Sall3[:, 2 * t + 1, :], start=False, stop=True),
                                 r=(qz[1][hh][1], B_Sall), w=(B_po,), mark=(hh == 1))
                        if d == 0:
                            S.op("act", lambda h: h.copy(out=oacc3[:, t, :], in_=po[:, 0:128]), r=(B_po,),
                                 w=(B_oacc,))
                        else:
                            S.op("dve", lambda h: h.tensor_tensor(out=oacc3[:, t, :], in0=po[:, 0:128],
                                                                   in1=oacc3[:, t, :], op=ALU.add),
                                 r=(B_po, B_oacc), w=(B_oacc,))
                    if dbg and hp == 1 and b == 0 and l == 0:
                        S.dma("sp", dbg_of[d], oacc, r=(B_oacc,), w=(), chan=B_oacc)
                        S.drain("sp", (B_oacc,))
                        S.dma("sp", dbg_S[d], Sall, r=(B_Sall,), w=(), chan=B_Sall)
                        S.drain("sp", (B_Sall,))
                        for qi_, (qa, qb) in enumerate(((refs, B_refs), (lasts, B_lasts), (alph, B_alph))):
                            S.dma("sp", dbg_st[d, qi_], qa, r=(qb,), w=(), chan=qb)
                            S.drain("sp", (qb,))
                        for qi_, (qa, qb) in enumerate(((qd, B_qd),)):
                            S.dma("sp", dbg_q[d, qi_], qa, r=(qb,), w=(), chan=qb)
                            S.drain("sp", (qb,))
                if hp == 0:
                    ck(16)
                S.op("pool", lambda h: h.tensor_tensor(out=onb, in0=oacc, in1=oacc, op=ALU.mult),
                     r=(B_oacc,), w=(B_onb,))
                S.op("dve", lambda h: h.tensor_reduce(out=ssq, in_=onb.rearrange("p (g v) -> p g v", v=64),
                                                       axis=AX.X, op=ALU.add), r=(B_onb,), w=(B_ssq,))
                S.op("act", lambda h: h.activation(out=ssq, in_=ssq, func=AF.Sqrt, scale=1.0 / 64, bias=EPS),
                     r=(B_ssq,), w=(B_ssq,))
                S.op("dve", lambda h: h.reciprocal(out=ssq, in_=ssq), r=(B_ssq,), w=(B_ssq,))
                S.op("dve", lambda h: h.tensor_tensor(out=onb.rearrange("p (g v) -> p g v", v=64),
                                                       in0=oacc.rearrange("p (g v) -> p g v", v=64),
                                                       in1=bc3(ssq, 64),
                                                       op=ALU.mult), r=(B_oacc, B_ssq, B_onb), w=(B_onb,))
                gcol = smallp[:, SM_HGG + l * 4 + hp:SM_HGG + l * 4 + hp + 1]
                for t4 in range(0, NT, 4):
                    nt_ = min(4, NT - t4)
                    pt, B_pt = next_pbf()
                    for ti in range(nt_):
                        S.op("pe", lambda h, ti=ti: h.transpose(pt[:, ti * 128:(ti + 1) * 128], onb3[:, t4 + ti, :],
                                                                  ident_b[:]), r=(B_onb, B_const), w=(B_pt,),
                             mark=(ti == nt_ - 1))
                    S.op("dve", lambda h: h.scalar_tensor_tensor(out=yT[:, 4 + hp, t4 * 128:(t4 + nt_) * 128],
                                                                  in0=pt[:, 0:nt_ * 128], scalar=gcol,
                                                                  in1=sgh[:, t4 * 128:(t4 + nt_) * 128],
                                                                  op0=ALU.mult, op1=ALU.mult),
                         r=(B_pt, B_small, B_sgh), w=(B_yT[1],))
            fence_arena(allb)

        def mla(b, l):
            AR.reset()
            kT = [AR.bf(NTOK, f"kT{h_}") for h_ in range(4)]
            krope, B_krope = AR.bf(NTOK, "krope")
            Vaug, B_V = AR.bf(NT * 4 * 65, "Vaug")
            cqn, B_cqn = AR.bf(2 * NTOK, "cqn")
            ckvn, B_ckvn = AR.bf(NTOK, "ckvn")
            qT = [AR.bf(512, f"qT{h_}") for h_ in range(4)]
            wuq, B_wuq = AR.bf(16 * 2 * 96, "wuq")
            wkn, B_wkn = AR.bf(8 * 64, "wkn")
            wvv, B_wvv = AR.bf(512, "wvv")
            sgm, B_sgm = AR.bf(2 * 512, "sgm")
            osb, B_osb = AR.bf(4 * 256, "osb")
            pT = [AR.bf(512, f"pT{i}") for i in range(3)]
            rope, B_ropeS = AR.f32(2 * 512, "rope")
            cqs, B_cqs = AR.f32(2 * 512, "cqs")
            sqs, B_sqs = AR.f32(2 * 512, "sqs")
            rst, B_rst = AR.f32(512, "rst")
            tr1, B_tr1 = AR.f32(512, "tr1")
            tr2, B_tr2 = AR.f32(512, "tr2")
            rec, B_rec = AR.f32(4, "rec")
            allb = [x[1] for x in kT] + [x[1] for x in qT] + [x[1] for x in pT] + \
                   [B_krope, B_V, B_cqn, B_ckvn, B_wuq, B_wkn, B_wvv, B_sgm, B_osb, B_ropeS, B_cqs, B_sqs, B_rst,
                    B_tr1, B_tr2, B_rec]
            V4 = Vaug.rearrange("p (t h c) -> p t h c", h=4, c=65)
            V3 = Vaug.rearrange("p (g c) -> p g c", c=65)
            cqn3 = cqn.rearrange("p (k n) -> p k n", n=NTOK)
            wuq4 = wuq.rearrange("p (g k c) -> p g k c", k=2, c=96)
            wkn3 = wkn.rearrange("p (h c) -> p h c", c=64)
            sgm3 = sgm.rearrange("p (j n) -> p j n", n=512)
            osb3 = osb.rearrange("p (q c) -> p q c", c=256)
            rope3 = rope.rearrange("p (a n) -> p a n", n=512)
            cqs3 = cqs.rearrange("p (k n) -> p k n", n=512)
            sqs3 = sqs.rearrange("p (k n) -> p k n", n=512)
            S.dma("pool", wuq4, w_uq_d[l].rearrange("g p k c -> p g k c"), w=(B_wuq,), chan=B_wuq)
            S.dma("pool", wkn3, w_kn_d[l], w=(B_wkn,), chan=B_wkn)
            S.dma("pool", wvv, w_v_d[l], w=(B_wvv,), chan=B_wvv)
            S.op("pool", lambda h: h.memset(V3[:, :, 64:65], 1.0), w=(B_V,))
            qg_ = smallp[:, SM_QG + l * 2:SM_QG + l * 2 + 2]
            kvg_ = smallp[:, SM_KVG + l:SM_KVG + l + 1]

            def load_rope(n0, n1, off):
                for a in range(2):
                    S.dma("sp", rope3[64:96, a, off:off + (n1 - n0)], ropetab_d[a, :, n0:n1], r=(B_rope,),
                          w=(B_ropeS,), chan=B_ropeS)
                    S.drain("sp", (B_ropeS,))

            def rope_rows(dst, B_dst, pA, B_pA, pB, B_pB, c0, c1, off):
                n = c1 - c0
                S.op("dve", lambda h: h.tensor_tensor(out=tr1[64:96, 0:n], in0=pA[64:96, c0:c1],
                                                       in1=rope3[64:96, 0, off:off + n], op=ALU.mult),
                     r=(B_pA, B_ropeS), w=(B_tr1,))
                S.op("dve", lambda h: h.tensor_tensor(out=tr2[64:96, 0:n], in0=pB[64:96, c0:c1],
                                                       in1=rope3[64:96, 1, off:off + n], op=ALU.mult),
                     r=(B_pB, B_ropeS), w=(B_tr2,))
                S.op("pool", lambda h: h.tensor_tensor(out=dst, in0=tr1[64:96, 0:n], in1=tr2[64:96, 0:n],
                                                        op=ALU.add), r=(B_tr1, B_tr2), w=(B_dst,))

            w_cq = [load_w(w_in_d[l, CH_CQ + j], 8, 128) for j in range(2)]
            w_ckv = load_w(w_in_d[l, CH_CKV], 8, 128)
            w_kra = load_w(w_in_d[l, CH_KRA], 8, 128)
            w_krb = load_w(w_in_d[l, CH_KRB], 8, 128)

            def norm_fm(nk, src3, dst_fn, gcols, a0, a1):
                n = a1 - a0
                for j in range(nk):
                    S.op("pool", lambda h, j=j: h.tensor_tensor(out=sqs3[:, j, 0:n], in0=src3[:, j, 0:n],
                                                                 in1=src3[:, j, 0:n], op=ALU.mult),
                         r=(B_cqs,), w=(B_sqs,))
                pss, B_pss = next_aux()
                for j in range(nk):
                    S.op("pe", lambda h, j=j: h.matmul(pss[:, 0:n], lhsT=ones_f[:], rhs=sqs3[:, j, 0:n],
                                                         start=(j == 0), stop=(j == nk - 1)),
                         r=(B_const, B_sqs), w=(B_pss,), mark=(j == nk - 1))
                S.op("act", lambda h: h.activation(out=rst[:, 0:n], in_=pss[:, 0:n], func=AF.Sqrt,
                                                    scale=1.0 / (128 * nk), bias=EPS), r=(B_pss,), w=(B_rst,))
                S.op("dve", lambda h: h.reciprocal(out=rst[:, 0:n], in_=rst[:, 0:n]), r=(B_rst,), w=(B_rst,))
                for j in range(nk):
                    dst, B_dst = dst_fn(j)
                    S.op("dve", lambda h, j=j, dst=dst: h.scalar_tensor_tensor(out=dst, in0=src3[:, j, 0:n],
                                                                                 scalar=gcols[:, j:j + 1],
                                                                                 in1=rst[:, 0:n], op0=ALU.mult,
                                                                                 op1=ALU.mult),
                         r=(B_cqs, B_rst, B_small), w=(B_dst,))

            for (a0, a1) in TG:
                n = a1 - a0
                for j in range(2):
                    proj_fm(w_cq[j][0], w_cq[j][1], 8, uT_rhs, (B_uT,), [(a0, a1)],
                            lambda x0, x1, pg, B_pg, j=j: S.op("act", lambda h: h.copy(out=cqs3[:, j, 0:n],
                                                                                        in_=pg[:, 0:n]),
                                                               r=(B_pg,), w=(B_cqs,)))
                norm_fm(2, cqs3, lambda j: (cqn3[:, j, a0:a1], B_cqn), qg_, a0, a1)
                proj_fm(w_ckv[0], w_ckv[1], 8, uT_rhs, (B_uT,), [(a0, a1)],
                        lambda x0, x1, pg, B_pg: S.op("act", lambda h: h.copy(out=cqs3[:, 0, 0:n], in_=pg[:, 0:n]),
                                                     r=(B_pg,), w=(B_cqs,)))
                norm_fm(1, cqs3, lambda j: (ckvn[:, a0:a1], B_ckvn), kvg_, a0, a1)
                pA, B_pA = next_gen()
                for kc in range(8):
                    S.op("pe", lambda h, kc=kc: h.matmul(pA[:, 0:n], lhsT=w_kra[0][:, kc, :], rhs=uT[:, kc, a0:a1],
                                                           start=(kc == 0), stop=(kc == 7)),
                         r=(w_kra[1], B_uT), w=(B_pA,), mark=(kc == 7))
                lat0 = max(a0, NCTX)
                if a0 < NCTX:
                    S.op("act", lambda h: h.copy(out=krope[64:96, a0:NCTX], in_=pA[64:96, 0:NCTX - a0]),
                         r=(B_pA,), w=(B_krope,))
                pB, B_pB = next_gen()
                for kc in range(8):
                    S.op("pe", lambda h, kc=kc: h.matmul(pB[:, 0:n], lhsT=w_krb[0][:, kc, :], rhs=uT[:, kc, a0:a1],
                                                           start=(kc == 0), stop=(kc == 7)),
                         r=(w_krb[1], B_uT), w=(B_pB,), mark=(kc == 7))
                load_rope(lat0 - NCTX, a1 - NCTX, 0)
                rope_rows(krope[64:96, lat0:a1], B_krope, pA, B_pA, pB, B_pB, lat0 - a0, a1 - a0, 0)

            for hg_ in range(2):
                for hl in range(4):
                    h_ = hg_ * 4 + hl
                    S.op("pool", lambda h: h.tensor_copy(out=kT[hl][0][64:96, :], in_=krope[64:96, :]),
                         r=(B_krope,), w=(kT[hl][1],))
                    for gi, (a0, a1) in enumerate(TG):
                        n = a1 - a0
                        pg, B_pg = next_gen()
                        S.op("pe", lambda h: h.matmul(pg[0:64, 0:n], lhsT=wkn3[:, h_, :], rhs=ckvn[:, a0:a1],
                                                       start=True, stop=True), r=(B_wkn, B_ckvn), w=(B_pg,))
                        if (hl + gi) % 2 == 0:
                            S.op("act", lambda h: h.copy(out=kT[hl][0][0:64, a0:a1], in_=pg[0:64, 0:n]), r=(B_pg,),
                                 w=(kT[hl][1],))
                        else:
                            S.op("dve", lambda h: h.tensor_copy(out=kT[hl][0][0:64, a0:a1], in_=pg[0:64, 0:n]),
                                 r=(B_pg,), w=(kT[hl][1],))
                for t in range(NT):
                    pg, B_pg = next_gen()
                    S.op("pe", lambda h: h.matmul(pg[:, 0:256], lhsT=ckvn[:, t * 128:(t + 1) * 128],
                                                   rhs=wvv[:, hg_ * 256:(hg_ + 1) * 256], start=True, stop=True),
                         r=(B_ckvn, B_wvv), w=(B_pg,))
                    pv = pg[:, 0:256].rearrange("p (h c) -> p h c", c=64)
                    if t % 2 == 0:
                        S.op("act", lambda h: h.copy(out=V4[:, t, :, 0:64], in_=pv), r=(B_pg,), w=(B_V,))
                    else:
                        S.op("dve", lambda h: h.tensor_copy(out=V4[:, t, :, 0:64], in_=pv), r=(B_pg,), w=(B_V,))
                for qi_, (q0, q1) in enumerate(QG):
                    nq = q1 - q0
                    nqt = nq // 128
                    isctx = (qi_ == 0)
                    kl = list(range(2)) if isctx else list(range(NT))
                    if not isctx:
                        load_rope(q0 - NCTX, q1 - NCTX, 0)
                    for jl in range(2):
                        w_g = load_w(w_in_d[l, CH_GMLA + hg_ * 2 + jl], 8, 128)
                        pg, B_pg = next_gen()
                        for kc in range(8):
                            S.op("pe", lambda h, kc=kc: h.matmul(pg[:, 0:nq], lhsT=w_g[0][:, kc, :],
                                                                   rhs=uT[:, kc, q0:q1], start=(kc == 0),
                                                                   stop=(kc == 7)),
                                 r=(w_g[1], B_uT), w=(B_pg,), mark=(kc == 7))
                        S.op("act", lambda h: h.activation(out=sgm3[:, jl, 0:nq], in_=pg[:, 0:nq], func=AF.Silu),
                             r=(B_pg,), w=(B_sgm,))
                    for hl in range(4):
                        h_ = hg_ * 4 + hl
                        pq, B_pq = next_gen()
                        for kc in range(2):
                            S.op("pe", lambda h, kc=kc: h.matmul(pq[0:96, 0:nq], lhsT=wuq4[:, h_, kc, :],
                                                                   rhs=cqn3[:, kc, q0:q1], start=(kc == 0),
                                                                   stop=(kc == 1)),
                                 r=(B_wuq, B_cqn), w=(B_pq,), mark=(kc == 1))
                        if isctx:
                            S.op("act", lambda h: h.copy(out=qT[hl][0][0:96, 0:nq], in_=pq[0:96, 0:nq]), r=(B_pq,),
                                 w=(qT[hl][1],))
                        else:
                            pq2, B_pq2 = next_gen()
                            for kc in range(2):
                                S.op("pe", lambda h, kc=kc: h.matmul(pq2[0:96, 0:nq], lhsT=wuq4[:, 8 + h_, kc, :],
                                                                       rhs=cqn3[:, kc, q0:q1], start=(kc == 0),
                                                                       stop=(kc == 1)),
                                     r=(B_wuq, B_cqn), w=(B_pq2,), mark=(kc == 1))
                            S.op("act", lambda h: h.copy(out=qT[hl][0][0:64, 0:nq], in_=pq[0:64, 0:nq]), r=(B_pq,),
                                 w=(qT[hl][1],))
                            rope_rows(qT[hl][0][64:96, 0:nq], qT[hl][1], pq, B_pq, pq2, B_pq2, 0, nq, 0)
                    pi_ = 0
                    for hl in range(4):
                        po, B_po = next_acc()
                        po3 = po[:, 0:nqt * 65].rearrange("p (q c) -> p q c", c=65)
                        for ki, kt in enumerate(kl):
                            pa, B_pa = next_aux()
                            S.op("pe", lambda h: h.matmul(pa[:, 0:nq], lhsT=kT[hl][0][0:96, kt * 128:(kt + 1) * 128],
                                                           rhs=qT[hl][0][0:96, 0:nq], start=True, stop=True),
                                 r=(kT[hl][1], qT[hl][1]), w=(B_pa,))
                            pt_, B_pt = pT[pi_ % 3]
                            pi_ += 1
                            S.op("act", lambda h: h.activation(out=pt_[:, 0:nq], in_=pa[:, 0:nq], func=AF.Exp,
                                                                scale=MLA_SCALE), r=(B_pa,), w=(B_pt,))
                            for qq in range(nqt):
                                S.op("pe", lambda h, qq=qq: h.matmul(po3[:, qq, :],
                                                                       lhsT=pt_[:, qq * 128:(qq + 1) * 128],
                                                                       rhs=V4[:, kt, hl, :],
                                                                       start=(ki == 0 and qq == 0),
                                                                       stop=(ki == len(kl) - 1),
                                                                       skip_group_check=True),
                                     r=(B_pt, B_V), w=(B_po,), mark=(qq == nqt - 1))
                        S.op("dve", lambda h: h.reciprocal(out=rec[:, 0:nqt], in_=po3[:, :, 64]), r=(B_po,),
                             w=(B_rec,))
                        S.op("dve", lambda h: h.tensor_tensor(out=osb3[:, 0:nqt, hl * 64:(hl + 1) * 64],
                                                               in0=po3[:, :, 0:64], in1=bc3(rec[:, 0:nqt], 64),
                                                               op=ALU.mult), r=(B_po, B_rec), w=(B_osb,))
                    for jl in range(2):
                        pt, B_pt = next_pbf()
                        for qq in range(nqt):
                            S.op("pe", lambda h, qq=qq: h.transpose(pt[:, qq * 128:(qq + 1) * 128],
                                                                      osb3[:, qq, jl * 128:(jl + 1) * 128],
                                                                      ident_b[:]),
                                 r=(B_osb, B_const), w=(B_pt,), mark=(qq == nqt - 1))
                        S.op("dve", lambda h: h.tensor_tensor(out=yT[:, hg_ * 2 + jl, q0:q1], in0=pt[:, 0:nq],
                                                               in1=sgm3[:, jl, 0:nq], op=ALU.mult),
                             r=(B_pt, B_sgm), w=(B_yT[0],))
            fence_arena(allb)

        def merge_out(b, l, last):
            AR.reset()
            mT, B_mT = AR.bf(8 * NTOK, "mT")
            wo, B_wo = AR.bf(8 * 1024, "wo")
            sgt = [AR.f32(512, f"sg{i}") for i in range(3)]
            macc = [AR.f32(512, f"macc{i}") for i in range(2)]
            tmpm = [AR.f32(512, f"tmpm{i}") for i in range(2)]
            hts = [AR.f32(1024, f"ht{i}") for i in range(3)]
            tmo = [AR.f32(512, f"tmo{i}") for i in range(2)]
            junk, B_junk = AR.bf(1024, "junkm")
            allb = [B_mT, B_wo, B_junk] + [x[1] for x in sgt + macc + tmpm + hts + tmo]
            mT3 = mT.rearrange("p (k n) -> p k n", n=NTOK)
            wo3 = wo.rearrange("p (k c) -> p k c", c=1024)
            S.dma("pool", wo3, w_out_d[l], w=(B_wo,), chan=B_wo)
            si = 0
            for fc in range(8):
                wg = [load_w(w_in_d[l, CH_GATE + 3 * fc + br], 8, 128) for br in range(3)]
                wb = [load_w(w_br_d[l, 3 * fc + br], 4, 128) for br in range(3)]
                for gi, (a0, a1) in enumerate(TG):
                    n = a1 - a0
                    ma, B_ma = macc[gi % 2]
                    for br in range(3):
                        pg, B_pg = next_gen()
                        for kc in range(8):
                            S.op("pe", lambda h, kc=kc: h.matmul(pg[:, 0:n], lhsT=wg[br][0][:, kc, :],
                                                                   rhs=uT[:, kc, a0:a1], start=(kc == 0),
                                                                   stop=(kc == 7)),
                                 r=(wg[br][1], B_uT), w=(B_pg,), mark=(kc == 7))
                        sg, B_sg = sgt[si % 3]
                        si += 1
                        S.op("act", lambda h: h.activation(out=sg[:, 0:n], in_=pg[:, 0:n], func=AF.Sigmoid),
                             r=(B_pg,), w=(B_sg,))
                        pb_, B_pb = next_gen()
                        for kc in range(4):
                            S.op("pe", lambda h, kc=kc: h.matmul(pb_[:, 0:n], lhsT=wb[br][0][:, kc, :],
                                                                   rhs=yT[:, br * 4 + kc, a0:a1], start=(kc == 0),
                                                                   stop=(kc == 3)),
                                 r=(wb[br][1], B_yT[br]), w=(B_pb,), mark=(kc == 3))
                        if br == 0:
                            S.op("dve", lambda h: h.tensor_tensor(out=ma[:, 0:n], in0=pb_[:, 0:n], in1=sg[:, 0:n],
                                                                   op=ALU.mult), r=(B_pb, B_sg), w=(B_ma,))
                        else:
                            tm, B_tm = tmpm[br - 1]
                            S.op("dve", lambda h: h.tensor_tensor(out=tm[:, 0:n], in0=pb_[:, 0:n], in1=sg[:, 0:n],
                                                                   op=ALU.mult), r=(B_pb, B_sg), w=(B_tm,))
                            if br == 1:
                                S.op("pool", lambda h: h.tensor_tensor(out=ma[:, 0:n], in0=ma[:, 0:n],
                                                                        in1=tm[:, 0:n], op=ALU.add),
                                     r=(B_ma, B_tm), w=(B_ma,))
                            else:
                                S.op("pool", lambda h: h.tensor_tensor(out=mT3[:, fc, a0:a1], in0=ma[:, 0:n],
                                                                        in1=tm[:, 0:n], op=ALU.add),
                                     r=(B_ma, B_tm), w=(B_mT,))
            for t in range(NT):
                ri = 1 if t < 2 else 0
                if last and t < 2:
                    continue
                ht, B_ht = hts[t % 3]
                src = xin[b, t * 128:(t + 1) * 128, :] if l == 0 else hbuf_d[t * 128:(t + 1) * 128, :]
                S.dma("sp", ht, src, r=(() if l == 0 else (B_hb[t],)), w=(B_ht,), chan=B_ht)
                for half in range(2):
                    pg, B_pg = next_gen()
                    for kc in range(8):
                        S.op("pe", lambda h, kc=kc: h.matmul(pg[:, :], lhsT=mT3[:, kc, t * 128:(t + 1) * 128],
                                                               rhs=wo3[:, kc, half * 512:(half + 1) * 512],
                                                               start=(kc == 0), stop=(kc == 7)),
                             r=(B_mT, B_wo), w=(B_pg,), mark=(kc == 7))
                    to, B_to = tmo[half]
                    S.op("dve", lambda h: h.tensor_tensor(out=to, in0=pg[:, :],
                                                           in1=gate_bc[:, ri, half * 512:(half + 1) * 512],
                                                           op=ALU.mult), r=(B_pg, B_gate), w=(B_to,))
                    S.op("pool", lambda h: h.tensor_tensor(out=ht[:, half * 512:(half + 1) * 512],
                                                            in0=ht[:, half * 512:(half + 1) * 512], in1=to,
                                                            op=ALU.add), r=(B_ht, B_to), w=(B_ht,))
                if not last:
                    S.dma("sp", hbuf_d[t * 128:(t + 1) * 128, :], ht, r=(B_ht,), w=(B_hb[t],), chan=B_ht)
                    if dbg and b == 0 and l == 0:
                        S.drain("sp", (B_ht,))
                        S.dma("sp", dbg_h[t * 128:(t + 1) * 128, :], ht, r=(B_ht,), w=(), chan=B_ht)
                else:
                    S.op("act", lambda h: h.activation(out=junk, in_=ht, func=AF.Square,
                                                        accum_out=stat[:, 32 + t:33 + t]),
                         r=(B_ht,), w=(B_junk, B_stat))
                    S.op("act", lambda h: h.activation(out=stat[:, 32 + t:33 + t], in_=stat[:, 32 + t:33 + t],
                                                        func=AF.Sqrt, scale=1.0 / D, bias=EPS), r=(B_stat,),
                         w=(B_stat,))
                    S.op("dve", lambda h: h.reciprocal(out=stat[:, 32 + t:33 + t], in_=stat[:, 32 + t:33 + t]),
                         r=(B_stat,), w=(B_stat,))
                    S.op("dve", lambda h: h.tensor_scalar(out=ht, in0=ht, scalar1=stat[:, 32 + t:33 + t],
                                                           scalar2=None, op0=ALU.mult), r=(B_ht, B_stat),
                         w=(B_ht,))
                    S.op("pool", lambda h: h.tensor_tensor(out=ht, in0=ht, in1=fng_bc[:], op=ALU.mult),
                         r=(B_ht, B_const), w=(B_ht,))
                    S.dma("sp", out_d[b, (t - 2) * 128:(t - 1) * 128, :], ht, r=(B_ht,), w=(), chan=B_ht)
            fence_arena(allb + [B_stat])

        for b in range(NB):
            for l in range(NL):
                layer(b, l, l == NL - 1)
        for cb in S.chans:
            S.E["sp"].h.wait_ge(cb.chan[0], cb.chan[1])
    return nc


IN_OFF = {}
_o = 0
for _n, _s in (("cq", 256), ("ckv", 128), ("kr", 32), ("gmla", 512), ("hq", 512), ("hi", 512), ("hff", 512),
               ("hfb", 512), ("ghg", 512), ("cx", 512), ("cb", 512), ("cc", 512), ("gcv", 512), ("gate", 3072)):
    IN_OFF[_n] = _o
    _o += _s


def _stat(wcols):
    K, ncol = wcols.shape
    return np.ascontiguousarray(wcols.reshape(K // 128, 128, ncol).transpose(1, 0, 2))


def _swap_idx():
    d = np.arange(32)
    a, hf, p = d // 16, (d // 8) % 2, d % 8
    return a * 16 + (1 - hf) * 8 + p


def prep_weights(w_in, mla_w_uq, mla_w_ukv, w_branch, w_out, ada_w, ada_b):
    sw = _swap_idx()
    W_in = np.zeros((L, NCH, 128, 8, 128), np.float32)
    W_uq = np.zeros((L, 16, 128, 2, 96), np.float32)
    W_kn = np.zeros((L, 128, 8, 64), np.float32)
    W_v = np.zeros((L, 128, 512), np.float32)
    W_br = np.zeros((L, 24, 128, 4, 128), np.float32)
    W_out = np.zeros((L, 128, 8, 1024), np.float32)
    ADA = np.zeros((L, 128, 8, 3072), np.float32)
    for l in range(L):
        w = w_in[l]

        def cols(name, j, width=128):
            o = IN_OFF[name] + j * width
            return w[:, o:o + width]

        for j in range(4):
            for q, nm in enumerate(("cx", "cc", "cb", "gcv")):
                W_in[l, CH_CONV + 4 * j + q] = _stat(cols(nm, j))
        for hp in range(4):
            for q, nm in enumerate(("hq", "hff", "hfb", "ghg")):
                W_in[l, CH_HG + 4 * hp + q] = _stat(cols(nm, hp))
            W_in[l, CH_HI + hp] = _stat(cols("hi", hp))
        for j in range(2):
            W_in[l, CH_CQ + j] = _stat(cols("cq", j))
        W_in[l, CH_CKV] = _stat(cols("ckv", 0))
        kr = w[:, IN_OFF["kr"]:IN_OFF["kr"] + 32]
        ka = np.zeros((1024, 128), np.float32)
        kb = np.zeros((1024, 128), np.float32)
        ka[:, 64:96] = kr
        kb[:, 64:96] = kr[:, sw]
        W_in[l, CH_KRA] = _stat(ka)
        W_in[l, CH_KRB] = _stat(kb)
        for j in range(4):
            W_in[l, CH_GMLA + j] = _stat(cols("gmla", j))
        for fc in range(8):
            for br in range(3):
                o = IN_OFF["gate"] + br * 1024 + fc * 128
                W_in[l, CH_GATE + 3 * fc + br] = _stat(w[:, o:o + 128])
        uq = mla_w_uq[l]
        for h in range(8):
            blk = uq[:, h * 96:(h + 1) * 96]
            W_uq[l, h] = _stat(blk)
            blk2 = blk.copy()
            blk2[:, 64:96] = blk[:, 64 + sw]
            W_uq[l, 8 + h] = _stat(blk2)
        ukv = mla_w_ukv[l]
        for h in range(8):
            W_kn[l, :, h, :] = ukv[:, h * 128:h * 128 + 64]
            W_v[l, :, h * 64:(h + 1) * 64] = ukv[:, h * 128 + 64:h * 128 + 128]
        for fc in range(8):
            for br in range(3):
                W_br[l, 3 * fc + br] = _stat(w_branch[l, br][:, fc * 128:(fc + 1) * 128])
        W_out[l] = _stat(w_out[l])
        ADA[l] = _stat(ada_w[l])
    return dict(w_in=W_in, w_uq=W_uq, w_kn=W_kn, w_v=W_v, w_br=W_br, w_out=W_out, ada=ADA,
                adab=np.ascontiguousarray(ada_b.reshape(L, 1, 3072)))


def prep_small(norm_g, mla_q_norm_g, mla_kv_norm_g, hg_norm_g, conv_w, conv_b, hg_lb_logits):
    sm = np.zeros((128, NSM), np.float32)
    fm = lambda v: np.ascontiguousarray(v.reshape(-1, 128).T)
    for l in range(L):
        sm[:, SM_NG + l * 8:SM_NG + (l + 1) * 8] = fm(norm_g[l])
        sm[:, SM_QG + l * 2:SM_QG + (l + 1) * 2] = fm(mla_q_norm_g[l])
        sm[:, SM_KVG + l:SM_KVG + l + 1] = fm(mla_kv_norm_g[l])
        sm[:, SM_HGG + l * 4:SM_HGG + (l + 1) * 4] = fm(hg_norm_g[l])
        for k in range(3):
            sm[:, SM_CW + l * 12 + k * 4:SM_CW + l * 12 + (k + 1) * 4] = fm(conv_w[l, k])
        sm[:, SM_CB + l * 4:SM_CB + (l + 1) * 4] = fm(conv_b[l])
        for d in range(2):
            sm[:, SM_LB + l * 8 + d * 4:SM_LB + l * 8 + (d + 1) * 4] = fm(hg_lb_logits[l, d])
    d = np.arange(32)
    a, hf, p = d // 16, (d // 8) % 2, d % 8
    sm[64:96, SM_ROPE] = (10000.0 ** (-(p.astype(np.float64)) / 8.0)).astype(np.float32)
    sm[64:96, SM_ROPE + 1] = a.astype(np.float32)
    sm[64:96, SM_ROPE + 2] = np.where(hf == 0, -1.0, 1.0).astype(np.float32)
    return sm


_CACHE = {}


def kernel(x, c, ctx, c_ctx, ada_w, ada_b, norm_g, w_in, mla_q_norm_g, mla_kv_norm_g, mla_w_uq, mla_w_ukv,
           hg_lb_logits, hg_norm_g, conv_w, conv_b, w_branch, w_out, final_norm_g):
    f = lambda a: np.asarray(a, dtype=np.float32)
    x, c, ctx, c_ctx = f(x), f(c), f(ctx), f(c_ctx)
    W = prep_weights(f(w_in), f(mla_w_uq), f(mla_w_ukv), f(w_branch), f(w_out), f(ada_w), f(ada_b))
    sm = prep_small(f(norm_g), f(mla_q_norm_g), f(mla_kv_norm_g), f(hg_norm_g), f(conv_w), f(conv_b),
                    f(hg_lb_logits))
    fng = np.ascontiguousarray(f(final_norm_g).reshape(1, D))
    n_cores = 8
    NB = x.shape[0] // n_cores
    if "nc" not in _CACHE:
        _CACHE["nc"] = build(NB=NB, NL=L)
    nc = _CACHE["nc"]
    in_maps = []
    for ci in range(n_cores):
        bs = slice(ci * NB, (ci + 1) * NB)
        xin = np.ascontiguousarray(np.concatenate([ctx[bs], x[bs]], axis=1))
        rows = np.concatenate([c[bs], c_ctx[None, :]], axis=0)
        cT = np.ascontiguousarray(rows.T.reshape(8, 128, 5).transpose(1, 0, 2))
        m = dict(xin=xin, cT=cT, smallp=sm, fng=fng)
        m.update(W)
        in_maps.append(m)
    res = run_bass_kernel_spmd(nc, in_maps, core_ids=list(range(n_cores)))
    return np.concatenate([r["out"] for r in res.results], axis=0).astype(np.float32)
```

```python
import math
import numpy as np
import concourse.bass as bass
import concourse.mybir as mybir
from concourse.bass_utils import run_bass_kernel_spmd
from contextlib import ExitStack

F32 = mybir.dt.float32
BF16 = mybir.dt.bfloat16
U8 = mybir.dt.uint8
I32 = mybir.dt.int32
AF = mybir.ActivationFunctionType
ALU = mybir.AluOpType
AX = mybir.AxisListType

L = 4
D = 1024
NTOK = 2304
NCTX = 256
NLAT = 2048
NT = 18
EPS = 1e-6
MLA_SCALE = 96.0 ** -0.5
TG = [(0, 512), (512, 1024), (1024, 1536), (1536, 2048), (2048, 2304)]
QG = [(0, 256), (256, 768), (768, 1280), (1280, 1792), (1792, 2304)]

CH_CONV = 0
CH_HG = 16
CH_CQ = 32
CH_CKV = 34
CH_KRA = 35
CH_KRB = 36
CH_GMLA = 37
CH_GATE = 41
CH_HI = 65
NCH = 69

SM_NG = 0
SM_QG = SM_NG + L * 8
SM_KVG = SM_QG + L * 2
SM_HGG = SM_KVG + L
SM_CW = SM_HGG + L * 4
SM_CB = SM_CW + L * 12
SM_LB = SM_CB + L * 4
SM_ROPE = SM_LB + L * 8
NSM = SM_ROPE + 3


class Buf:
    __slots__ = ("name", "w", "r", "chan")

    def __init__(self, name):
        self.name = name
        self.w = {}
        self.r = {}
        self.chan = None


class Eng:
    def __init__(self, name, h):
        self.name = name
        self.h = h
        self.sem = None
        self.count = 0
        self.seen = {}
        self.pend_r = []
        self.pend_w = []


class Sync:
    def __init__(self, nc, es):
        self.nc = nc
        self.es = es
        self.nsem = 0
        self.E = {
            "pe": Eng("pe", nc.tensor),
            "act": Eng("act", nc.scalar),
            "dve": Eng("dve", nc.vector),
            "pool": Eng("pool", nc.gpsimd),
            "sp": Eng("sp", nc.sync),
        }
        for e in self.E.values():
            self._new_sem(e)
        self.chans = []
        self.halted = False

    def sem(self, name):
        self.nsem += 1
        return self.es.enter_context(self.nc.semaphore(f"{name}_{self.nsem}"))

    def _new_sem(self, e):
        e.sem = self.sem("e" + e.name)
        e.count = 0

    def _merge(self, d, src, skip=None):
        for k, (s, v) in src.items():
            if skip is not None and k == skip:
                continue
            if k not in d or d[k][1] < v:
                d[k] = (s, v)

    def _waits(self, e, reads, writes):
        d = {}
        own = e.sem.num if e.name in ("pe", "sp") else None
        for b in reads:
            self._merge(d, b.w)
        for b in writes:
            self._merge(d, b.w, skip=own)
            self._merge(d, b.r, skip=own)
        for k, (s, v) in d.items():
            if e.seen.get(k, 0) < v:
                e.h.wait_ge(s, v)
                e.seen[k] = v

    def op(self, eng, fn, r=(), w=(), mark=True):
        if self.halted:
            return None
        e = self.E[eng]
        self._waits(e, r, w)
        inst = fn(e.h)
        if mark:
            if e.count >= 60000:
                if not e.pend_r and not e.pend_w:
                    self._new_sem(e)
            inst.then_inc(e.sem, 1)
            e.count += 1
            tok = (e.sem, e.count)
            k = e.sem.num
            for b in list(r) + e.pend_r:
                b.r[k] = tok
            for b in list(w) + e.pend_w:
                b.w[k] = tok
            e.pend_r = []
            e.pend_w = []
        else:
            e.pend_r.extend(r)
            e.pend_w.extend(w)
        return inst

    def dma(self, q, out, in_, r=(), w=(), chan=None, **kw):
        if self.halted:
            return None
        e = self.E[q]
        self._waits(e, r, w)
        if chan.chan is None:
            chan.chan = [self.sem("d"), 0]
            self.chans.append(chan)
        inst = e.h.dma_start(out=out, in_=in_, **kw)
        chan.chan[1] += 16
        inst.then_inc(chan.chan[0], 16)
        tok = (chan.chan[0], chan.chan[1])
        k = chan.chan[0].num
        for b in r:
            b.r[k] = tok
        for b in w:
            b.w[k] = tok
        return inst

    def self_wait(self, eng):
        if self.halted:
            return
        e = self.E[eng]
        assert not e.pend_r and not e.pend_w
        e.h.wait_ge(e.sem, e.count)

    def drain(self, eng, bufs):
        if self.halted:
            return
        e = self.E[eng]
        self._waits(e, bufs, bufs)


class _Stop(Exception):
    pass


def build(NB=4, NL=4, dbg=False, stop_after=None):
    nc = bass.Bass("TRN2", target_bir_lowering=False)

    def din(name, shape):
        return nc.dram_tensor(name, shape, F32, kind="ExternalInput").ap()

    xin = din("xin", [NB, NTOK, D])
    cT_d = din("cT", [128, 8, 5])
    ada_d = din("ada", [L, 128, 8, 3072])
    adab_d = din("adab", [L, 1, 3072])
    w_in_d = din("w_in", [L, NCH, 128, 8, 128])
    w_uq_d = din("w_uq", [L, 16, 128, 2, 96])
    w_kn_d = din("w_kn", [L, 128, 8, 64])
    w_v_d = din("w_v", [L, 128, 512])
    w_br_d = din("w_br", [L, 24, 128, 4, 128])
    w_out_d = din("w_out", [L, 128, 8, 1024])
    smallp_d = din("smallp", [128, NSM])
    fng_d = din("fng", [1, D])
    out_d = nc.dram_tensor("out", [NB, NLAT, D], F32, kind="ExternalOutput").ap()
    hbuf_d = nc.dram_tensor("hbuf", [NTOK, D], F32, kind="Internal").ap()
    modrows_d = nc.dram_tensor("modrows", [L, 5, 3072], F32, kind="Internal").ap()
    ropetab_d = nc.dram_tensor("ropetab", [2, 32, NLAT], F32, kind="Internal").ap()
    if dbg:
        dbg_uT = nc.dram_tensor("dbg_uT", [128, 8, NTOK], BF16, kind="ExternalOutput").ap()
        dbg_yT = nc.dram_tensor("dbg_yT", [128, 12, NTOK], BF16, kind="ExternalOutput").ap()
        dbg_h = nc.dram_tensor("dbg_h", [NTOK, D], F32, kind="ExternalOutput").ap()
        dbg_of = nc.dram_tensor("dbg_of", [2, 128, NT * 128], F32, kind="ExternalOutput").ap()
        dbg_S = nc.dram_tensor("dbg_S", [2, 128, 36 * 64], BF16, kind="ExternalOutput").ap()
        dbg_q = nc.dram_tensor("dbg_q", [2, 4, 128, NTOK], BF16, kind="ExternalOutput").ap()
        dbg_st = nc.dram_tensor("dbg_st", [2, 3, 128, 36], F32, kind="ExternalOutput").ap()

    with ExitStack() as es:
        S = Sync(nc, es)

        def ck(n):
            if stop_after == n:
                S.halted = True

        def sb(name, shape, dt):
            return es.enter_context(nc.sbuf_tensor("s_" + name, shape, dt))

        def ps(name, shape, dt):
            return es.enter_context(nc.psum_tensor("p_" + name, shape, dt))

        ident_f = sb("ident_f", [128, 128], F32)
        ident_b = sb("ident_b", [128, 128], BF16)
        ones_f = sb("ones_f", [128, 128], F32)
        maskf32 = sb("maskf32", [128, 2, 128], F32)
        masks = sb("masks", [128, 2, 128], U8)
        smallp = sb("smallp", [128, NSM], F32)
        lbt = sb("lbt", [128, L, 8], F32)
        omlt = sb("omlt", [128, L, 8], F32)
        fng_bc = sb("fng_bc", [128, D], F32)
        resetm = sb("resetm", [128, 512], F32)
        uT = sb("uT", [128, 8, NTOK], BF16)
        yT = sb("yT", [128, 12, NTOK], BF16)
        wslot = [sb(f"wslot{i}", [128, 1024], BF16) for i in range(6)]
        modT = sb("modT", [128, 2, 16], F32)
        cscale = sb("cscale", [128, 2, 8], F32)
        gate_bc = sb("gate_bc", [128, 2, D], F32)
        stat = sb("stat", [128, 64], F32)
        AR_WORDS = 22000
        arena = sb("arena", [128, AR_WORDS], F32)

        B_const = Buf("const")
        B_small = Buf("small")
        B_uT = Buf("uT")
        B_yT = [Buf(f"yT{i}") for i in range(3)]
        B_w = [Buf(f"w{i}") for i in range(6)]
        B_mod = Buf("mod")
        B_gate = Buf("gatebc")
        B_stat = Buf("stat")
        B_hb = [Buf(f"hb{t}") for t in range(NT)]
        B_modrows = Buf("modrows")
        B_rope = Buf("ropetab")

        gen = [ps(f"gen{i}", [128, 512], F32) for i in range(3)]
        aux = [ps(f"scr{i}", [128, 512], F32) for i in range(2)]
        accp = [ps(f"accp{i}", [128, 512], F32) for i in range(2)]
        pbf = ps("pbf", [128, 1024], BF16)
        B_gen = [Buf(f"gen{i}") for i in range(3)]
        B_aux = [Buf(f"aux{i}") for i in range(2)]
        B_acc = [Buf("acc0"), Buf("acc1")]
        B_pbf = [Buf("pbf0"), Buf("pbf1")]
        ctr = {"gen": 0, "aux": 0, "pbf": 0, "acc": 0, "w": 0}

        def next_gen():
            i = ctr["gen"] % 3
            ctr["gen"] += 1
            return gen[i], B_gen[i]

        def next_aux():
            i = ctr["aux"] % 2
            ctr["aux"] += 1
            return aux[i], B_aux[i]

        def next_pbf():
            return pbf[:, 0:512], B_pbf[0]

        def next_acc():
            i = ctr["acc"] % 2
            ctr["acc"] += 1
            return accp[i], B_acc[i]

        class Arena:
            def __init__(self):
                self.off = 0
                self.bufs = {}

            def buf(self, name):
                if name not in self.bufs:
                    self.bufs[name] = Buf(name)
                return self.bufs[name]

            def reset(self):
                self.off = 0

            def f32(self, words, name):
                a = arena[:, self.off:self.off + words]
                self.off += words
                assert self.off <= AR_WORDS, (name, self.off)
                return a, self.buf(name)

            def bf(self, elems, name):
                words = (elems + 1) // 2
                a = arena[:, self.off:self.off + words].bitcast(BF16)
                self.off += words
                assert self.off <= AR_WORDS, (name, self.off)
                return a, self.buf(name)

        AR = Arena()
        B_arena_all = Buf("arena_all")

        def load_w(src_ap, kc, ncol):
            i = ctr["w"] % 6
            ctr["w"] += 1
            view = wslot[i][:, 0:kc * ncol].rearrange("p (k c) -> p k c", c=ncol)
            S.dma("pool", view, src_ap, r=(), w=(B_w[i],), chan=B_w[i])
            return view, B_w[i]

        S.dma("sp", smallp[:], smallp_d[:, :], w=(B_small,), chan=B_small)
        S.dma("sp", fng_bc[:], fng_d[0:1, :].partition_broadcast(128), w=(B_const,), chan=B_const)
        S.op("pool", lambda h: h.memset(ident_f[:], 1.0), w=(B_const,))
        S.op("pool", lambda h: h.affine_select(out=ident_f[:], in_=ident_f[:], pattern=[[-1, 128]],
                                                compare_op=ALU.is_equal, fill=0.0, base=0, channel_multiplier=1),
             r=(B_const,), w=(B_const,))
        S.op("pool", lambda h: h.tensor_copy(out=ident_b[:], in_=ident_f[:]), r=(B_const,), w=(B_const,))
        S.op("pool", lambda h: h.memset(ones_f[:], 1.0), w=(B_const,))
        S.op("pool", lambda h: h.memset(maskf32[:], 1.0), w=(B_const,))
        S.op("pool", lambda h: h.affine_select(out=maskf32[:, 0, :], in_=maskf32[:, 0, :], pattern=[[1, 128]],
                                                compare_op=ALU.is_ge, fill=0.0, base=0, channel_multiplier=-1),
             r=(B_const,), w=(B_const,))
        S.op("pool", lambda h: h.affine_select(out=maskf32[:, 1, :], in_=maskf32[:, 1, :], pattern=[[-1, 128]],
                                                compare_op=ALU.is_ge, fill=0.0, base=0, channel_multiplier=1),
             r=(B_const,), w=(B_const,))
        S.op("pool", lambda h: h.memset(maskf32[0:64, 0, 64:128], 0.0), r=(B_const,), w=(B_const,))
        S.op("pool", lambda h: h.memset(maskf32[64:128, 1, 0:64], 0.0), r=(B_const,), w=(B_const,))
        S.op("pool", lambda h: h.tensor_copy(out=masks[:], in_=maskf32[:]), r=(B_const,), w=(B_const,))
        S.op("pool", lambda h: h.memset(resetm[:], 1.0), w=(B_const,))
        S.op("pool", lambda h: h.memset(resetm[:].rearrange("p (c t) -> p c t", t=64)[:, :, 0:1], 0.0),
             r=(B_const,), w=(B_const,))

        def fence_arena(bufs):
            for en in ("pe", "act", "dve", "pool", "sp"):
                S.drain(en, bufs)

        ck(0)
        AR.reset()
        e_lb, B_elb = AR.f32(L * 8, "e_lb")
        s_lb, B_slb = AR.f32(8, "s_lb")
        e3 = e_lb.rearrange("p (l k) -> p l k", k=8)
        S.op("act", lambda h: h.activation(out=e_lb, in_=smallp[:, SM_LB:SM_LB + L * 8], func=AF.Exp),
             r=(B_small,), w=(B_elb,))
        S.op("dve", lambda h: h.tensor_tensor(out=s_lb, in0=e3[:, 0, :], in1=e3[:, 1, :], op=ALU.add),
             r=(B_elb,), w=(B_slb,))
        S.op("dve", lambda h: h.tensor_tensor(out=s_lb, in0=s_lb, in1=e3[:, 2, :], op=ALU.add),
             r=(B_elb, B_slb), w=(B_slb,))
        S.op("dve", lambda h: h.tensor_tensor(out=s_lb, in0=s_lb, in1=e3[:, 3, :], op=ALU.add),
             r=(B_elb, B_slb), w=(B_slb,))
        S.op("dve", lambda h: h.reciprocal(out=s_lb, in_=s_lb), r=(B_slb,), w=(B_slb,))
        S.op("dve", lambda h: h.memset(lbt[:, 0, :], 0.0), w=(B_const,))
        for l in range(1, L):
            S.op("dve", lambda h, l=l: h.tensor_tensor(out=e3[:, l, :], in0=e3[:, l, :], in1=s_lb, op=ALU.mult),
                 r=(B_elb, B_slb), w=(B_elb,))
            S.op("dve", lambda h, l=l: h.tensor_tensor(out=lbt[:, l, :], in0=lbt[:, l - 1, :], in1=e3[:, l, :],
                                                        op=ALU.add),
                 r=(B_elb, B_const), w=(B_const,))
        S.op("dve", lambda h: h.tensor_scalar(out=omlt[:], in0=lbt[:], scalar1=-1.0, scalar2=1.0,
                                               op0=ALU.mult, op1=ALU.add), r=(B_const,), w=(B_const,))

        ck(1)
        fence_arena([B_elb, B_slb])
        AR.reset()
        cTs, B_cT = AR.f32(40, "cT")
        adat = [AR.f32(4096, f"adat{i}") for i in range(2)]
        biasr, B_biasr = AR.f32(3072, "biasr")
        mrow = [AR.f32(512, f"mrow{i}") for i in range(2)]
        cT3 = cTs.rearrange("p (k r) -> p k r", r=5)
        S.dma("sp", cT3, cT_d[:, :, :], w=(B_cT,), chan=B_cT)
        S.op("act", lambda h: h.activation(out=cTs, in_=cTs, func=AF.Silu), r=(B_cT,), w=(B_cT,))
        ci = 0
        for l in range(L):
            S.dma("sp", biasr[0:5, :], adab_d[l, 0:1, :].partition_broadcast(5), w=(B_biasr,), chan=B_biasr)
            for cb in range(6):
                at, B_at = adat[ci % 2]
                mr, B_mr = mrow[ci % 2]
                ci += 1
                at3 = at.rearrange("p (k c) -> p k c", c=512)
                S.dma("sp", at3, ada_d[l, :, :, cb * 512:(cb + 1) * 512], w=(B_at,), chan=B_at)
                pg, B_pg = next_gen()
                for kc in range(8):
                    S.op("pe", lambda h, kc=kc: h.matmul(pg[0:5, :], lhsT=cT3[:, kc, :], rhs=at3[:, kc, :],
                                                           start=(kc == 0), stop=(kc == 7)),
                         r=(B_cT, B_at), w=(B_pg,), mark=(kc == 7))
                S.op("dve", lambda h: h.tensor_tensor(out=mr[0:5, :], in0=pg[0:5, :],
                                                       in1=biasr[0:5, cb * 512:(cb + 1) * 512], op=ALU.add),
                     r=(B_pg, B_biasr), w=(B_mr,))
                S.dma("sp", modrows_d[l, :, cb * 512:(cb + 1) * 512], mr[0:5, :], r=(B_mr,), w=(B_modrows,),
                      chan=B_mr)

        ck(2)
        fence_arena([B_cT, B_biasr] + [b for _, b in adat] + [b for _, b in mrow])
        AR.reset()
        rowf, B_r0 = AR.f32(2048, "rowf")
        colf, B_r1 = AR.f32(2048, "colf")
        t_a, B_r2 = AR.f32(2048, "t_a")
        t_b, B_r3 = AR.f32(2048, "t_b")
        t_c, B_r4 = AR.f32(2048, "t_c")
        t_i = arena[:, AR.off:AR.off + 2048].bitcast(I32)
        AR.off += 2048
        B_r5 = AR.buf("t_i")
        RF = smallp[:, SM_ROPE:SM_ROPE + 1]
        RA = smallp[:, SM_ROPE + 1:SM_ROPE + 2]
        RS = smallp[:, SM_ROPE + 2:SM_ROPE + 3]
        S.op("pool", lambda h: h.iota(rowf.rearrange("p (a b) -> p a b", b=64), pattern=[[1, 32], [0, 64]], base=0,
                                       channel_multiplier=0, allow_small_or_imprecise_dtypes=True), w=(B_r0,))
        S.op("pool", lambda h: h.iota(colf.rearrange("p (a b) -> p a b", b=64), pattern=[[0, 32], [1, 64]], base=0,
                                       channel_multiplier=0, allow_small_or_imprecise_dtypes=True), w=(B_r1,))
        S.op("dve", lambda h: h.tensor_tensor(out=colf, in0=colf, in1=rowf, op=ALU.subtract),
             r=(B_r0, B_r1), w=(B_r1,))
        S.op("dve", lambda h: h.scalar_tensor_tensor(out=t_a, in0=colf, scalar=RA, in1=rowf, op0=ALU.mult,
                                                      op1=ALU.add), r=(B_r0, B_r1, B_small), w=(B_r2,))
        S.op("dve", lambda h: h.tensor_scalar(out=t_a, in0=t_a, scalar1=RF, scalar2=None, op0=ALU.mult),
             r=(B_r2, B_small), w=(B_r2,))
        S.op("dve", lambda h: h.tensor_scalar(out=t_b, in0=t_a, scalar1=1.0 / (2 * math.pi), scalar2=None,
                                               op0=ALU.mult), r=(B_r2,), w=(B_r3,))
        S.op("dve", lambda h: h.tensor_copy(out=t_i, in_=t_b), r=(B_r3,), w=(B_r5,))
        S.op("dve", lambda h: h.tensor_copy(out=t_b, in_=t_i), r=(B_r5,), w=(B_r3,))
        S.op("dve", lambda h: h.scalar_tensor_tensor(out=t_a, in0=t_b, scalar=-2 * math.pi, in1=t_a, op0=ALU.mult,
                                                      op1=ALU.add), r=(B_r2, B_r3), w=(B_r2,))
        S.op("act", lambda h: h.activation(out=t_b, in_=t_a, func=AF.Sin, scale=0.25), r=(B_r2,), w=(B_r3,))
        S.op("dve", lambda h: h.tensor_scalar(out=t_c, in0=t_a, scalar1=0.25, scalar2=math.pi / 2, op0=ALU.mult,
                                               op1=ALU.add), r=(B_r2,), w=(B_r4,))
        S.op("act", lambda h: h.activation(out=t_c, in_=t_c, func=AF.Sin), r=(B_r4,), w=(B_r4,))
        S.op("dve", lambda h: h.scalar_tensor_tensor(out=t_a, in0=t_b, scalar=2.0, in1=t_c, op0=ALU.mult,
                                                      op1=ALU.mult), r=(B_r3, B_r4), w=(B_r2,))
        S.op("dve", lambda h: h.tensor_tensor(out=t_b, in0=t_b, in1=t_b, op=ALU.mult), r=(B_r3,), w=(B_r3,))
        S.op("dve", lambda h: h.tensor_scalar(out=t_b, in0=t_b, scalar1=-2.0, scalar2=1.0, op0=ALU.mult,
                                               op1=ALU.add), r=(B_r3,), w=(B_r3,))
        S.op("dve", lambda h: h.scalar_tensor_tensor(out=t_c, in0=t_a, scalar=2.0, in1=t_b, op0=ALU.mult,
                                                      op1=ALU.mult), r=(B_r2, B_r3), w=(B_r4,))
        S.op("dve", lambda h: h.tensor_tensor(out=rowf, in0=t_a, in1=t_a, op=ALU.mult), r=(B_r2,), w=(B_r0,))
        S.op("dve", lambda h: h.tensor_scalar(out=rowf, in0=rowf, scalar1=-2.0, scalar2=1.0, op0=ALU.mult,
                                               op1=ALU.add), r=(B_r0,), w=(B_r0,))
        S.op("dve", lambda h: h.tensor_scalar(out=t_c, in0=t_c, scalar1=RS, scalar2=None, op0=ALU.mult),
             r=(B_r4, B_small), w=(B_r4,))
        S.dma("sp", ropetab_d[0, :, :], rowf[64:96, :], r=(B_r0,), w=(B_rope,), chan=B_r0)
        S.dma("sp", ropetab_d[1, :, :], t_c[64:96, :], r=(B_r4,), w=(B_rope,), chan=B_r4)
        pro_bufs = [B_elb, B_slb, B_cT, B_biasr, B_r0, B_r1, B_r2, B_r3, B_r4, B_r5] + \
                   [b for _, b in adat] + [b for _, b in mrow]

        fence_arena(pro_bufs)
        ck(3)

        def bc3(ap2, n):
            g = ap2.shape[1]
            return ap2.rearrange("p (g o) -> p g o", o=1).broadcast_to([ap2.shape[0], g, n])

        def proj_fm(wv, Bw, kcn, rhs_fn, rbufs, ntok_groups, consumer, m=128):
            for (a0, a1) in ntok_groups:
                pg, B_pg = next_gen()
                n = a1 - a0
                for kc in range(kcn):
                    S.op("pe", lambda h, kc=kc: h.matmul(pg[0:m, 0:n], lhsT=wv[:, kc, 0:m], rhs=rhs_fn(kc, a0, a1),
                                                           start=(kc == 0), stop=(kc == kcn - 1)),
                         r=(Bw,) + tuple(rbufs), w=(B_pg,), mark=(kc == kcn - 1))
                consumer(a0, a1, pg, B_pg)

        def uT_rhs(kc, a0, a1):
            return uT[:, kc, a0:a1]

        def layer(b, l, last):
            r_b = b
            for ri, row in enumerate((r_b, 4)):
                S.dma("sp", modT[:, ri, :], modrows_d[l, row, 0:2048].rearrange("(c p) -> p c", p=128),
                      r=(B_modrows,), w=(B_mod,), chan=B_mod, allow_slow_non_contiguous=True)
                S.drain("sp", (B_mod,))
            for ri, row in enumerate((r_b, 4)):
                S.dma("sp", gate_bc[:, ri, :], modrows_d[l, row:row + 1, 2048:3072].partition_broadcast(128),
                      r=(B_modrows,), w=(B_gate,), chan=B_gate)
                S.drain("sp", (B_gate,))
            S.op("dve", lambda h: h.tensor_scalar(out=cscale[:], in0=modT[:, :, 8:16], scalar1=1.0, scalar2=None,
                                                   op0=ALU.add), r=(B_mod,), w=(B_mod,))
            ng = smallp[:, SM_NG + l * 8:SM_NG + (l + 1) * 8]
            for ri in range(2):
                S.op("dve", lambda h, ri=ri: h.tensor_tensor(out=cscale[:, ri, :], in0=cscale[:, ri, :], in1=ng,
                                                              op=ALU.mult), r=(B_mod, B_small), w=(B_mod,))

            ck(4)
            AR.reset()
            xs = [AR.f32(1024, f"xs{i}") for i in range(8)]
            junk, B_junk = AR.bf(1024, "junk")
            pa_bufs = [bb for _, bb in xs] + [B_junk]
            groups = [(0, 2, 1), (2, 6, 0), (6, 10, 0), (10, 14, 0), (14, 18, 0)]
            si = 0
            for (t0, t1, ri) in groups:
                tiles = []
                for t in range(t0, t1):
                    xt, B_xt = xs[si % 8]
                    si += 1
                    src = xin[b, t * 128:(t + 1) * 128, :] if l == 0 else hbuf_d[t * 128:(t + 1) * 128, :]
                    S.dma("sp", xt, src, r=(() if l == 0 else (B_hb[t],)), w=(B_xt,), chan=B_xt)
                    S.op("act", lambda h: h.activation(out=junk, in_=xt, func=AF.Square,
                                                        accum_out=stat[:, t:t + 1]),
                         r=(B_xt,), w=(B_junk, B_stat))
                    S.op("act", lambda h: h.activation(out=stat[:, t:t + 1], in_=stat[:, t:t + 1], func=AF.Sqrt,
                                                        scale=1.0 / D, bias=EPS), r=(B_stat,), w=(B_stat,))
                    S.op("dve", lambda h: h.reciprocal(out=stat[:, t:t + 1], in_=stat[:, t:t + 1]),
                         r=(B_stat,), w=(B_stat,))
                    S.op("dve", lambda h: h.tensor_scalar(out=xt, in0=xt, scalar1=stat[:, t:t + 1], scalar2=None,
                                                           op0=ALU.mult), r=(B_xt, B_stat), w=(B_xt,))
                    tiles.append((xt, B_xt))
                nt_ = t1 - t0
                for kc in range(8):
                    pg, B_pg = next_gen()
                    for ti, (xt, B_xt) in enumerate(tiles):
                        S.op("pe", lambda h, ti=ti, xt=xt: h.transpose(pg[:, ti * 128:(ti + 1) * 128],
                                                                         xt[:, kc * 128:(kc + 1) * 128], ident_f[:]),
                             r=(B_xt, B_const), w=(B_pg,), mark=(ti == nt_ - 1))
                    S.op("act", lambda h: h.activation(out=uT[:, kc, t0 * 128:t1 * 128], in_=pg[:, 0:nt_ * 128],
                                                        func=AF.Identity, scale=cscale[:, ri, kc:kc + 1],
                                                        bias=modT[:, ri, kc:kc + 1]),
                         r=(B_pg, B_mod), w=(B_uT,))
            fence_arena(pa_bufs + [B_stat])
            if dbg and b == 0 and l == 0:
                S.dma("sp", dbg_uT[:, :, :], uT[:], r=(B_uT,), w=(), chan=B_uT)
                S.drain("sp", (B_uT,))

            ck(5)
            AR.reset()
            cxs, B_cxs = AR.f32(NTOK, "cxs")
            uu, B_uu = AR.f32(NTOK, "uu")
            cacc, B_cacc = AR.f32(NTOK, "cacc")
            cbs, B_cbs = AR.f32(NTOK, "cbs")
            sgc, B_sgc = AR.f32(NTOK, "sgc")
            pb_bufs = [B_cxs, B_uu, B_cacc, B_cbs, B_sgc]
            for j in range(4):
                wv = [load_w(w_in_d[l, CH_CONV + 4 * j + q], 8, 128) for q in range(4)]

                def c_cx(a0, a1, pg, B_pg):
                    S.op("act", lambda h: h.copy(out=cxs[:, a0:a1], in_=pg[:, 0:a1 - a0]), r=(B_pg,), w=(B_cxs,))

                def c_cc(a0, a1, pg, B_pg):
                    S.op("dve", lambda h: h.tensor_tensor(out=uu[:, a0:a1], in0=pg[:, 0:a1 - a0], in1=cxs[:, a0:a1],
                                                           op=ALU.mult), r=(B_pg, B_cxs), w=(B_uu,))

                def c_cb(a0, a1, pg, B_pg):
                    S.op("act", lambda h: h.copy(out=cbs[:, a0:a1], in_=pg[:, 0:a1 - a0]), r=(B_pg,), w=(B_cbs,))

                def c_g(a0, a1, pg, B_pg):
                    S.op("act", lambda h: h.activation(out=sgc[:, a0:a1], in_=pg[:, 0:a1 - a0], func=AF.Silu),
                         r=(B_pg,), w=(B_sgc,))

                for q, cons in enumerate((c_cx, c_cc, c_cb, c_g)):
                    proj_fm(wv[q][0], wv[q][1], 8, uT_rhs, (B_uT,), TG, cons)
                cw = lambda k: smallp[:, SM_CW + l * 12 + k * 4 + j:SM_CW + l * 12 + k * 4 + j + 1]
                cbias = smallp[:, SM_CB + l * 4 + j:SM_CB + l * 4 + j + 1]
                S.op("dve", lambda h: h.tensor_scalar(out=cacc, in0=uu, scalar1=cw(1), scalar2=cbias, op0=ALU.mult,
                                                       op1=ALU.add), r=(B_uu, B_small), w=(B_cacc,))
                for (s0, s1) in ((0, NCTX), (NCTX, NTOK)):
                    S.op("dve", lambda h: h.scalar_tensor_tensor(out=cacc[:, s0 + 1:s1], in0=uu[:, s0:s1 - 1],
                                                                  scalar=cw(0), in1=cacc[:, s0 + 1:s1],
                                                                  op0=ALU.mult, op1=ALU.add),
                         r=(B_uu, B_small, B_cacc), w=(B_cacc,))
                    S.op("dve", lambda h: h.scalar_tensor_tensor(out=cacc[:, s0:s1 - 1], in0=uu[:, s0 + 1:s1],
                                                                  scalar=cw(2), in1=cacc[:, s0:s1 - 1],
                                                                  op0=ALU.mult, op1=ALU.add),
                         r=(B_uu, B_small, B_cacc), w=(B_cacc,))
                S.op("pool", lambda h: h.tensor_tensor(out=cacc, in0=cacc, in1=cbs, op=ALU.mult),
                     r=(B_cacc, B_cbs), w=(B_cacc,))
                S.op("pool", lambda h: h.tensor_tensor(out=yT[:, 8 + j, :], in0=cacc, in1=sgc, op=ALU.mult),
                     r=(B_cacc, B_sgc), w=(B_yT[2],))
            fence_arena(pb_bufs)

            ck(6)
            hgrn2(b, l)
            ck(7)
            mla(b, l)
            ck(8)
            if dbg and b == 0 and l == 0:
                S.dma("sp", dbg_yT[:, :, :], yT[:], r=tuple(B_yT), w=(), chan=B_yT[0])
                S.drain("sp", (B_yT[0],))
            merge_out(b, l, last)

        def hgrn2(b, l):
            AR.reset()
            qd, B_qd = AR.bf(NTOK, "qd")
            qz = [[AR.bf(NTOK, f"qz{i}{hh}") for hh in range(2)] for i in range(2)]
            kdz = [AR.bf(NTOK, f"kdz{hh}") for hh in range(2)]
            kz = [AR.bf(NT * 128, f"kz{i}") for i in range(2)]
            itz = [AR.bf(NT * 128, f"itz{i}") for i in range(2)]
            sgh, B_sgh = AR.bf(NTOK, "sgh")
            Sall, B_Sall = AR.bf(36 * 64, "Sall")
            oacc, B_oacc = AR.f32(NT * 128, "oacc")
            onb, B_onb = qd, B_qd
            Asb = [[AR.bf(128, f"A{d}{i}") for i in range(2)] for d in range(2)]
            Ub = [AR.f32(64, f"U{i}") for i in range(2)]
            hqs, B_hqs = AR.f32(512, "hqs")
            T1, B_T1 = AR.f32(512, "T1")
            T2, B_T2 = AR.f32(512, "T2")
            T3, B_T3 = AR.f32(512, "T3")
            T4, B_T4 = AR.f32(512, "T4")
            T5, B_T5 = AR.f32(512, "T5")
            refs, B_refs = AR.f32(36, "refs")
            lasts, B_lasts = AR.f32(36, "lasts")
            alph, B_alph = AR.f32(36, "alph")
            ssq, B_ssq = AR.f32(36, "ssq")
            allb = [B_qd, B_sgh, B_Sall, B_oacc, B_hqs, B_T1, B_T2, B_T3, B_T4, B_T5, B_refs,
                    B_lasts, B_alph, B_ssq] + [x[1] for x in kz] + [x[1] for x in kdz] + [x[1] for x in itz] + \
                   [x[1] for q_ in qz for x in q_] + [x[1] for d in Asb for x in d] + [x[1] for x in Ub]
            kz3 = [k[0].rearrange("p (t c) -> p t c", c=128) for k in kz]
            itz3 = [k[0].rearrange("p (t c) -> p t c", c=128) for k in itz]
            Sall3 = Sall.rearrange("p (c v) -> p c v", v=64)
            oacc3 = oacc.rearrange("p (t c) -> p t c", c=128)
            onb3 = onb.rearrange("p (t c) -> p t c", c=128)
            for i in range(2):
                for hh in range(2):
                    S.op("pool", lambda h, i=i, hh=hh: h.memset(qz[i][hh][0], 0.0), w=(qz[i][hh][1],))
                S.op("pool", lambda h, i=i: h.memset(kdz[i][0], 0.0), w=(kdz[i][1],))
                S.op("pool", lambda h, i=i: h.memset(itz[i][0], 0.0), w=(itz[i][1],))
                for d in range(2):
                    S.op("pool", lambda h, i=i, d=d: h.memset(Asb[d][i][0], 0.0), w=(Asb[d][i][1],))
            ck(10)
            for hp in range(4):
                w_q = load_w(w_in_d[l, CH_HG + 4 * hp + 0], 8, 128)
                w_f = [load_w(w_in_d[l, CH_HG + 4 * hp + 1 + d], 8, 128) for d in range(2)]
                w_g = load_w(w_in_d[l, CH_HG + 4 * hp + 3], 8, 128)
                w_i = load_w(w_in_d[l, CH_HI + hp], 8, 128)
                proj_fm(w_g[0], w_g[1], 8, uT_rhs, (B_uT,), TG,
                        lambda a0, a1, pg, B_pg: S.op("act", lambda h: h.activation(out=sgh[:, a0:a1],
                                                                                    in_=pg[:, 0:a1 - a0],
                                                                                    func=AF.Silu),
                                                     r=(B_pg,), w=(B_sgh,)))
                for t in range(NT):
                    pg, B_pg = next_gen()
                    for kc in range(8):
                        S.op("pe", lambda h, kc=kc: h.matmul(pg[:, 0:128], lhsT=uT[:, kc, t * 128:(t + 1) * 128],
                                                               rhs=w_i[0][:, kc, :], start=(kc == 0), stop=(kc == 7)),
                             r=(B_uT, w_i[1]), w=(B_pg,), mark=(kc == 7))
                    S.op("act", lambda h: h.copy(out=itz3[0][0:64, t, :], in_=pg[0:64, 0:128]), r=(B_pg,),
                         w=(itz[0][1],))
                    S.op("dve", lambda h: h.tensor_copy(out=itz3[1][64:128, t, :], in_=pg[64:128, 0:128]),
                         r=(B_pg,), w=(itz[1][1],))
                if hp == 0:
                    ck(11)
                lbv = lbt[:, l, :]
                for d in range(2):
                    lbc = lbt[:, l, d * 4 + hp:d * 4 + hp + 1]
                    omc = omlt[:, l, d * 4 + hp:d * 4 + hp + 1]
                    for (a0, a1) in TG:
                        n = a1 - a0
                        nch = n // 64
                        c0 = a0 // 64
                        pq, B_pq = next_gen()
                        for kc in range(8):
                            S.op("pe", lambda h, kc=kc: h.matmul(pq[:, 0:n], lhsT=w_q[0][:, kc, :],
                                                                   rhs=uT[:, kc, a0:a1], start=(kc == 0),
                                                                   stop=(kc == 7)),
                                 r=(w_q[1], B_uT), w=(B_pq,), mark=(kc == 7))
                        pf, B_pf = next_gen()
                        for kc in range(8):
                            S.op("pe", lambda h, kc=kc: h.matmul(pf[:, 0:n], lhsT=w_f[d][0][:, kc, :],
                                                                   rhs=uT[:, kc, a0:a1], start=(kc == 0),
                                                                   stop=(kc == 7)),
                                 r=(w_f[d][1], B_uT), w=(B_pf,), mark=(kc == 7))
                        S.op("act", lambda h: h.copy(out=hqs[:, 0:n], in_=pq[:, 0:n]), r=(B_pq,), w=(B_hqs,))
                        S.op("act", lambda h: h.activation(out=T1[:, 0:n], in_=pf[:, 0:n], func=AF.Sigmoid),
                             r=(B_pf,), w=(B_T1,))
                        S.op("dve", lambda h: h.tensor_scalar(out=T1[:, 0:n], in0=T1[:, 0:n], scalar1=omc,
                                                               scalar2=lbc, op0=ALU.mult, op1=ALU.add),
                             r=(B_T1, B_const), w=(B_T1,))
                        S.op("act", lambda h: h.activation(out=T2[:, 0:n], in_=T1[:, 0:n], func=AF.Ln),
                             r=(B_T1,), w=(B_T2,))
                        S.op("pool", lambda h: h.tensor_scalar(out=T1[:, 0:n], in0=T1[:, 0:n], scalar1=-1.0,
                                                                scalar2=1.0, op0=ALU.mult, op1=ALU.add),
                             r=(B_T1, B_T2), w=(B_T1,))
                        S.op("dve", lambda h: h.tensor_tensor_scan(out=T3[:, 0:n], data0=resetm[:, 0:n],
                                                                    data1=T2[:, 0:n], initial=0.0, op0=ALU.mult,
                                                                    op1=ALU.add), r=(B_T2, B_const), w=(B_T3,))
                        T3v = T3[:, 0:n].rearrange("p (c t) -> p c t", t=64)
                        T2v = T2[:, 0:n].rearrange("p (c t) -> p c t", t=64)
                        if d == 0:
                            cum, B_cum, cumv = T3, B_T3, T3v
                            iref, ilast = 31, 63
                        else:
                            S.op("dve", lambda h: h.tensor_tensor(out=T2[:, 0:n], in0=T2[:, 0:n], in1=T3[:, 0:n],
                                                                   op=ALU.subtract), r=(B_T2, B_T3), w=(B_T2,))
                            S.op("dve", lambda h: h.tensor_tensor(out=T2v, in0=T2v,
                                                                   in1=T3v[:, :, 63:64].broadcast_to([128, nch, 64]),
                                                                   op=ALU.add), r=(B_T2, B_T3), w=(B_T2,))
                            cum, B_cum, cumv = T2, B_T2, T2v
                            iref, ilast = 32, 0
                        S.op("dve", lambda h: h.tensor_copy(out=refs[:, c0:c0 + nch], in_=cumv[:, :, iref]),
                             r=(B_cum,), w=(B_refs,))
                        S.op("dve", lambda h: h.tensor_tensor(out=cumv, in0=cumv,
                                                               in1=bc3(refs[:, c0:c0 + nch], 64), op=ALU.subtract),
                             r=(B_cum, B_refs), w=(B_cum,))
                        S.op("dve", lambda h: h.tensor_copy(out=lasts[:, c0:c0 + nch], in_=cumv[:, :, ilast]),
                             r=(B_cum,), w=(B_lasts,))
                        S.op("act", lambda h: h.activation(out=T4[:, 0:n], in_=cum[:, 0:n], func=AF.Exp),
                             r=(B_cum,), w=(B_T4,))
                        S.op("act", lambda h: h.activation(out=T5[:, 0:n], in_=cum[:, 0:n], func=AF.Exp,
                                                            scale=-1.0), r=(B_cum,), w=(B_T5,))
                        S.op("dve", lambda h: h.tensor_tensor(out=qd[:, a0:a1], in0=hqs[:, 0:n], in1=T4[:, 0:n],
                                                               op=ALU.mult), r=(B_hqs, B_T4), w=(B_qd,))
                        hv = hqs[:, 0:n].rearrange("p (c two t) -> p c two t", two=2, t=64)
                        ev = T4[:, 0:n].rearrange("p (c two t) -> p c two t", two=2, t=64)
                        for i in range(2):
                            for hh in range(2):
                                rw = slice(hh * 64, (hh + 1) * 64)
                                qv = qz[i][hh][0][:, a0:a1].rearrange("p (c two t) -> p c two t", two=2, t=64)
                                S.op("pool", lambda h, i=i, qv=qv, rw=rw: h.tensor_tensor(
                                    out=qv[rw, :, i, :], in0=hv[rw, :, i, :], in1=ev[rw, :, i, :], op=ALU.mult),
                                     r=(B_hqs, B_T4), w=(qz[i][hh][1],))
                        for hh in range(2):
                            rw = slice(hh * 64, (hh + 1) * 64)
                            S.op("dve", lambda h, rw=rw, hh=hh: h.tensor_tensor(out=kdz[hh][0][rw, a0:a1],
                                                                                 in0=T1[rw, 0:n], in1=T5[rw, 0:n],
                                                                                 op=ALU.mult),
                                 r=(B_T1, B_T5), w=(kdz[hh][1],))
                    if hp == 0 and d == 0:
                        ck(12)
                    if d == 0:
                        S.op("dve", lambda h: h.tensor_tensor(out=alph[:, 0:35], in0=lasts[:, 0:35],
                                                               in1=refs[:, 1:36], op=ALU.add),
                             r=(B_lasts, B_refs), w=(B_alph,))
                        order = list(range(36))
                        prev_of = {c: c - 1 for c in range(1, 36)}
                    else:
                        S.op("dve", lambda h: h.tensor_tensor(out=alph[:, 1:36], in0=lasts[:, 1:36],
                                                               in1=refs[:, 0:35], op=ALU.add),
                             r=(B_lasts, B_refs), w=(B_alph,))
                        S.op("dve", lambda h: h.tensor_tensor(out=alph[:, 0:1], in0=lasts[:, 0:1],
                                                               in1=refs[:, 35:36], op=ALU.add),
                             r=(B_lasts, B_refs), w=(B_alph,))
                        order = [3, 2, 1, 0] + list(range(35, 3, -1))
                        prev_of = {order[i]: order[i - 1] for i in range(1, 36)}
                    na_ = 35 if d == 0 else 36
                    S.op("act", lambda h: h.activation(out=alph[:, 0:na_], in_=alph[:, 0:na_], func=AF.Exp),
                         r=(B_alph,), w=(B_alph,))
                    if hp == 0 and d == 0:
                        ck(13)
                    for t in range(NT):
                        for hh in range(2):
                            pt, B_pt = next_pbf()
                            S.op("pe", lambda h: h.transpose(pt[:, 0:128], kdz[hh][0][:, t * 128:(t + 1) * 128],
                                                              ident_b[:]), r=(kdz[hh][1], B_const), w=(B_pt,))
                            if hh == 0:
                                S.op("act", lambda h: h.copy(out=kz3[0][:, t, :], in_=pt[:, 0:128]), r=(B_pt,),
                                     w=(kz[0][1],))
                            else:
                                S.op("dve", lambda h: h.tensor_copy(out=kz3[1][:, t, :], in_=pt[:, 0:128]),
                                     r=(B_pt,), w=(kz[1][1],))
                    if hp == 0 and d == 0:
                        ck(14)
                    first = order[0]
                    import os as _os
                    if not _os.environ.get("DBG_NO_MEMSET"):
                        S.op("dve", lambda h: h.memset(Sall3[:, first, :], 0.0), w=(B_Sall,))
                    Pregs = {}
                    for t in (range(NT) if d == 0 else [1, 0] + list(range(NT - 1, 1, -1))):
                        pa, B_pa = next_gen() if _os.environ.get("DBG_P_GEN") else (next_acc() if _os.environ.get("DBG_P_ACC") else next_aux())
                        for cc in range(2):
                            c = 2 * t + cc
                            Pr = pa[:, cc * 64:(cc + 1) * 64]
                            if _os.environ.get("DBG_NO_P"):
                                continue
                            if _os.environ.get("DBG_P_CC0") and cc == 1:
                                continue
                            if _os.environ.get("DBG_P_CC1") and cc == 0:
                                continue
                            S.op("pe", lambda h: h.matmul(Pr, lhsT=kz3[0][:, t, :], rhs=itz3[cc][:, t, 0:64],
                                                           start=True, stop=False),
                                 r=(kz[0][1], itz[cc][1]), w=(B_pa,), mark=False)
                            S.op("pe", lambda h: h.matmul(Pr, lhsT=kz3[1][:, t, :], rhs=itz3[cc][:, t, 64:128],
                                                           start=False, stop=True),
                                 r=(kz[1][1], itz[cc][1]), w=(B_pa,), mark=(cc == 1))
                            Pregs[c] = (Pr, B_pa)
                        import os as _os
                        for c in ((2 * t, 2 * t + 1) if d == 0 else (2 * t + 1, 2 * t)):
                            if _os.environ.get("DBG_SKIP_CHAIN"):
                                continue
                            Pr, B_pr = Pregs[c]
                            if c == first:
                                U, B_U = Ub[0]
                                S.op("dve", lambda h: h.tensor_copy(out=U, in_=Pr), r=(B_pr,), w=(B_U,))
                                ui = 0
                            else:
                                p = prev_of[c]
                                U, B_U = Ub[ui]
                                Un, B_Un = Ub[1 - ui]
                                S.op("dve", lambda h: h.tensor_scalar(out=Sall3[:, c, :], in0=U,
                                                                       scalar1=alph[:, p:p + 1], scalar2=None,
                                                                       op0=ALU.mult),
                                     r=(B_U, B_alph), w=(B_Sall,))
                                S.op("dve", lambda h: h.scalar_tensor_tensor(out=Un, in0=U, scalar=alph[:, p:p + 1],
                                                                              in1=Pr, op0=ALU.mult, op1=ALU.add),
                                     r=(B_U, B_alph, B_pr), w=(B_Un,))
                                ui = 1 - ui
                        if hp == 0 and d == 0 and t < 3:
                            ck(150 + t)
                    if hp == 0 and d == 0:
                        ck(15)
                    for t in range(NT):
                        po, B_po = next_acc()
                        tl = slice(t * 128, (t + 1) * 128)
                        pas = []
                        for hh in range(2):
                            pa, B_pa = next_aux()
                            S.op("pe", lambda h: h.matmul(pa[:, 0:128], lhsT=kdz[hh][0][:, tl], rhs=qd[:, tl],
                                                           start=True, stop=True), r=(kdz[hh][1], B_qd), w=(B_pa,))
                            pas.append((pa, B_pa))
                        for hh in range(2):
                            pa, B_pa = pas[hh]
                            A, B_A = Asb[d][hh]
                            S.op("dve", lambda h: h.copy_predicated(out=A, mask=masks[:, d, :], data=pa[:, 0:128]),
                                 r=(B_pa, B_const), w=(B_A,))
                        for hh in range(2):
                            A, B_A = Asb[d][hh]
                            oreg = po[:, hh * 64:(hh + 1) * 64]
                            hc = slice(hh * 64, (hh + 1) * 64)
                            S.op("pe", lambda h: h.matmul(oreg, lhsT=A, rhs=itz3[0][:, t, hc], start=True,
                                                           stop=False), r=(B_A, itz[0][1]), w=(B_po,), mark=False)
                            S.op("pe", lambda h: h.matmul(oreg, lhsT=A, rhs=itz3[1][:, t, hc], start=False,
                                                           stop=False), r=(B_A, itz[1][1]), w=(B_po,), mark=False)
                            S.op("pe", lambda h: h.matmul(oreg, lhsT=qz[0][hh][0][:, tl], rhs=Sall3[:, 2 * t, :],
                                                           start=False, stop=False),
                                 r=(qz[0][hh][1], B_Sall), w=(B_po,), mark=False)
                            S.op("pe", lambda h: h.matmul(oreg, lhsT=qz[1][hh][0][:, tl],
                                                           rhs=Sall3[:, 2 * t + 1, :], start=False, stop=True),
                                 r=(qz[1][hh][1], B_Sall), w=(B_po,), mark=(hh == 1))
                        if d == 0:
                            S.op("act", lambda h: h.copy(out=oacc3[:, t, :], in_=po[:, 0:128]), r=(B_po,),
                                 w=(B_oacc,))
                        else:
                            S.op("dve", lambda h: h.tensor_tensor(out=oacc3[:, t, :], in0=po[:, 0:128],
                                                                   in1=oacc3[:, t, :], op=ALU.add),
                                 r=(B_po, B_oacc), w=(B_oacc,))
                    if dbg and hp == 1 and b == 0 and l == 0:
                        S.dma("sp", dbg_of[d], oacc, r=(B_oacc,), w=(), chan=B_oacc)
                        S.drain("sp", (B_oacc,))
                        S.dma("sp", dbg_S[d], Sall, r=(B_Sall,), w=(), chan=B_Sall)
                        S.drain("sp", (B_Sall,))
                        for qi_, (qa, qb) in enumerate(((refs, B_refs), (lasts, B_lasts), (alph, B_alph))):
                            S.dma("sp", dbg_st[d, qi_], qa, r=(qb,), w=(), chan=qb)
                            S.drain("sp", (qb,))
                        for qi_, (qa, qb) in enumerate(((qd, B_qd),)):
                            S.dma("sp", dbg_q[d, qi_], qa, r=(qb,), w=(), chan=qb)
                            S.drain("sp", (qb,))
                if hp == 0:
                    ck(16)
                S.op("pool", lambda h: h.tensor_tensor(out=onb, in0=oacc, in1=oacc, op=ALU.mult),
                     r=(B_oacc,), w=(B_onb,))
                S.op("dve", lambda h: h.tensor_reduce(out=ssq, in_=onb.rearrange("p (g v) -> p g v", v=64),
                                                       axis=AX.X, op=ALU.add), r=(B_onb,), w=(B_ssq,))
                S.op("act", lambda h: h.activation(out=ssq, in_=ssq, func=AF.Sqrt, scale=1.0 / 64, bias=EPS),
                     r=(B_ssq,), w=(B_ssq,))
                S.op("dve", lambda h: h.reciprocal(out=ssq, in_=ssq), r=(B_ssq,), w=(B_ssq,))
                S.op("dve", lambda h: h.tensor_tensor(out=onb.rearrange("p (g v) -> p g v", v=64),
                                                       in0=oacc.rearrange("p (g v) -> p g v", v=64),
                                                       in1=bc3(ssq, 64),
                                                       op=ALU.mult), r=(B_oacc, B_ssq, B_onb), w=(B_onb,))
                gcol = smallp[:, SM_HGG + l * 4 + hp:SM_HGG + l * 4 + hp + 1]
                for t4 in range(0, NT, 4):
                    nt_ = min(4, NT - t4)
                    pt, B_pt = next_pbf()
                    for ti in range(nt_):
                        S.op("pe", lambda h, ti=ti: h.transpose(pt[:, ti * 128:(ti + 1) * 128], onb3[:, t4 + ti, :],
                                                                  ident_b[:]), r=(B_onb, B_const), w=(B_pt,),
                             mark=(ti == nt_ - 1))
                    S.op("dve", lambda h: h.scalar_tensor_tensor(out=yT[:, 4 + hp, t4 * 128:(t4 + nt_) * 128],
                                                                  in0=pt[:, 0:nt_ * 128], scalar=gcol,
                                                                  in1=sgh[:, t4 * 128:(t4 + nt_) * 128],
                                                                  op0=ALU.mult, op1=ALU.mult),
                         r=(B_pt, B_small, B_sgh), w=(B_yT[1],))
            fence_arena(allb)

        def mla(b, l):
            AR.reset()
            kT = [AR.bf(NTOK, f"kT{h_}") for h_ in range(4)]
            krope, B_krope = AR.bf(NTOK, "krope")
            Vaug, B_V = AR.bf(NT * 4 * 65, "Vaug")
            cqn, B_cqn = AR.bf(2 * NTOK, "cqn")
            ckvn, B_ckvn = AR.bf(NTOK, "ckvn")
            qT = [AR.bf(512, f"qT{h_}") for h_ in range(4)]
            wuq, B_wuq = AR.bf(16 * 2 * 96, "wuq")
            wkn, B_wkn = AR.bf(8 * 64, "wkn")
            wvv, B_wvv = AR.bf(512, "wvv")
            sgm, B_sgm = AR.bf(2 * 512, "sgm")
            osb, B_osb = AR.bf(4 * 256, "osb")
            pT = [AR.bf(512, f"pT{i}") for i in range(3)]
            rope, B_ropeS = AR.f32(2 * 512, "rope")
            cqs, B_cqs = AR.f32(2 * 512, "cqs")
            sqs, B_sqs = AR.f32(2 * 512, "sqs")
            rst, B_rst = AR.f32(512, "rst")
            tr1, B_tr1 = AR.f32(512, "tr1")
            tr2, B_tr2 = AR.f32(512, "tr2")
            rec, B_rec = AR.f32(4, "rec")
            allb = [x[1] for x in kT] + [x[1] for x in qT] + [x[1] for x in pT] + \
                   [B_krope, B_V, B_cqn, B_ckvn, B_wuq, B_wkn, B_wvv, B_sgm, B_osb, B_ropeS, B_cqs, B_sqs, B_rst,
                    B_tr1, B_tr2, B_rec]
            V4 = Vaug.rearrange("p (t h c) -> p t h c", h=4, c=65)
            V3 = Vaug.rearrange("p (g c) -> p g c", c=65)
            cqn3 = cqn.rearrange("p (k n) -> p k n", n=NTOK)
            wuq4 = wuq.rearrange("p (g k c) -> p g k c", k=2, c=96)
            wkn3 = wkn.rearrange("p (h c) -> p h c", c=64)
            sgm3 = sgm.rearrange("p (j n) -> p j n", n=512)
            osb3 = osb.rearrange("p (q c) -> p q c", c=256)
            rope3 = rope.rearrange("p (a n) -> p a n", n=512)
            cqs3 = cqs.rearrange("p (k n) -> p k n", n=512)
            sqs3 = sqs.rearrange("p (k n) -> p k n", n=512)
            S.dma("pool", wuq4, w_uq_d[l].rearrange("g p k c -> p g k c"), w=(B_wuq,), chan=B_wuq)
            S.dma("pool", wkn3, w_kn_d[l], w=(B_wkn,), chan=B_wkn)
            S.dma("pool", wvv, w_v_d[l], w=(B_wvv,), chan=B_wvv)
            S.op("pool", lambda h: h.memset(V3[:, :, 64:65], 1.0), w=(B_V,))
            qg_ = smallp[:, SM_QG + l * 2:SM_QG + l * 2 + 2]
            kvg_ = smallp[:, SM_KVG + l:SM_KVG + l + 1]

            def load_rope(n0, n1, off):
                for a in range(2):
                    S.dma("sp", rope3[64:96, a, off:off + (n1 - n0)], ropetab_d[a, :, n0:n1], r=(B_rope,),
                          w=(B_ropeS,), chan=B_ropeS)
                    S.drain("sp", (B_ropeS,))

            def rope_rows(dst, B_dst, pA, B_pA, pB, B_pB, c0, c1, off):
                n = c1 - c0
                S.op("dve", lambda h: h.tensor_tensor(out=tr1[64:96, 0:n], in0=pA[64:96, c0:c1],
                                                       in1=rope3[64:96, 0, off:off + n], op=ALU.mult),
                     r=(B_pA, B_ropeS), w=(B_tr1,))
                S.op("dve", lambda h: h.tensor_tensor(out=tr2[64:96, 0:n], in0=pB[64:96, c0:c1],
                                                       in1=rope3[64:96, 1, off:off + n], op=ALU.mult),
                     r=(B_pB, B_ropeS), w=(B_tr2,))
                S.op("pool", lambda h: h.tensor_tensor(out=dst, in0=tr1[64:96, 0:n], in1=tr2[64:96, 0:n],
                                                        op=ALU.add), r=(B_tr1, B_tr2), w=(B_dst,))

            w_cq = [load_w(w_in_d[l, CH_CQ + j], 8, 128) for j in range(2)]
            w_ckv = load_w(w_in_d[l, CH_CKV], 8, 128)
            w_kra = load_w(w_in_d[l, CH_KRA], 8, 128)
            w_krb = load_w(w_in_d[l, CH_KRB], 8, 128)

            def norm_fm(nk, src3, dst_fn, gcols, a0, a1):
                n = a1 - a0
                for j in range(nk):
                    S.op("pool", lambda h, j=j: h.tensor_tensor(out=sqs3[:, j, 0:n], in0=src3[:, j, 0:n],
                                                                 in1=src3[:, j, 0:n], op=ALU.mult),
                         r=(B_cqs,), w=(B_sqs,))
                pss, B_pss = next_aux()
                for j in range(nk):
                    S.op("pe", lambda h, j=j: h.matmul(pss[:, 0:n], lhsT=ones_f[:], rhs=sqs3[:, j, 0:n],
                                                         start=(j == 0), stop=(j == nk - 1)),
                         r=(B_const, B_sqs), w=(B_pss,), mark=(j == nk - 1))
                S.op("act", lambda h: h.activation(out=rst[:, 0:n], in_=pss[:, 0:n], func=AF.Sqrt,
                                                    scale=1.0 / (128 * nk), bias=EPS), r=(B_pss,), w=(B_rst,))
                S.op("dve", lambda h: h.reciprocal(out=rst[:, 0:n], in_=rst[:, 0:n]), r=(B_rst,), w=(B_rst,))
                for j in range(nk):
                    dst, B_dst = dst_fn(j)
                    S.op("dve", lambda h, j=j, dst=dst: h.scalar_tensor_tensor(out=dst, in0=src3[:, j, 0:n],
                                                                                 scalar=gcols[:, j:j + 1],
                                                                                 in1=rst[:, 0:n], op0=ALU.mult,
                                                                                 op1=ALU.mult),
                         r=(B_cqs, B_rst, B_small), w=(B_dst,))

            for (a0, a1) in TG:
                n = a1 - a0
                for j in range(2):
                    proj_fm(w_cq[j][0], w_cq[j][1], 8, uT_rhs, (B_uT,), [(a0, a1)],
                            lambda x0, x1, pg, B_pg, j=j: S.op("act", lambda h: h.copy(out=cqs3[:, j, 0:n],
                                                                                        in_=pg[:, 0:n]),
                                                               r=(B_pg,), w=(B_cqs,)))
                norm_fm(2, cqs3, lambda j: (cqn3[:, j, a0:a1], B_cqn), qg_, a0, a1)
                proj_fm(w_ckv[0], w_ckv[1], 8, uT_rhs, (B_uT,), [(a0, a1)],
                        lambda x0, x1, pg, B_pg: S.op("act", lambda h: h.copy(out=cqs3[:, 0, 0:n], in_=pg[:, 0:n]),
                                                     r=(B_pg,), w=(B_cqs,)))
                norm_fm(1, cqs3, lambda j: (ckvn[:, a0:a1], B_ckvn), kvg_, a0, a1)
                pA, B_pA = next_gen()
                for kc in range(8):
                    S.op("pe", lambda h, kc=kc: h.matmul(pA[:, 0:n], lhsT=w_kra[0][:, kc, :], rhs=uT[:, kc, a0:a1],
                                                           start=(kc == 0), stop=(kc == 7)),
                         r=(w_kra[1], B_uT), w=(B_pA,), mark=(kc == 7))
                lat0 = max(a0, NCTX)
                if a0 < NCTX:
                    S.op("act", lambda h: h.copy(out=krope[64:96, a0:NCTX], in_=pA[64:96, 0:NCTX - a0]),
                         r=(B_pA,), w=(B_krope,))
                pB, B_pB = next_gen()
                for kc in range(8):
                    S.op("pe", lambda h, kc=kc: h.matmul(pB[:, 0:n], lhsT=w_krb[0][:, kc, :], rhs=uT[:, kc, a0:a1],
                                                           start=(kc == 0), stop=(kc == 7)),
                         r=(w_krb[1], B_uT), w=(B_pB,), mark=(kc == 7))
                load_rope(lat0 - NCTX, a1 - NCTX, 0)
                rope_rows(krope[64:96, lat0:a1], B_krope, pA, B_pA, pB, B_pB, lat0 - a0, a1 - a0, 0)

            for hg_ in range(2):
                for hl in range(4):
                    h_ = hg_ * 4 + hl
                    S.op("pool", lambda h: h.tensor_copy(out=kT[hl][0][64:96, :], in_=krope[64:96, :]),
                         r=(B_krope,), w=(kT[hl][1],))
                    for gi, (a0, a1) in enumerate(TG):
                        n = a1 - a0
                        pg, B_pg = next_gen()
                        S.op("pe", lambda h: h.matmul(pg[0:64, 0:n], lhsT=wkn3[:, h_, :], rhs=ckvn[:, a0:a1],
                                                       start=True, stop=True), r=(B_wkn, B_ckvn), w=(B_pg,))
                        if (hl + gi) % 2 == 0:
                            S.op("act", lambda h: h.copy(out=kT[hl][0][0:64, a0:a1], in_=pg[0:64, 0:n]), r=(B_pg,),
                                 w=(kT[hl][1],))
                        else:
                            S.op("dve", lambda h: h.tensor_copy(out=kT[hl][0][0:64, a0:a1], in_=pg[0:64, 0:n]),
                                 r=(B_pg,), w=(kT[hl][1],))
                for t in range(NT):
                    pg, B_pg = next_gen()
                    S.op("pe", lambda h: h.matmul(pg[:, 0:256], lhsT=ckvn[:, t * 128:(t + 1) * 128],
                                                   rhs=wvv[:, hg_ * 256:(hg_ + 1) * 256], start=True, stop=True),
                         r=(B_ckvn, B_wvv), w=(B_pg,))
                    pv = pg[:, 0:256].rearrange("p (h c) -> p h c", c=64)
                    if t % 2 == 0:
                        S.op("act", lambda h: h.copy(out=V4[:, t, :, 0:64], in_=pv), r=(B_pg,), w=(B_V,))
                    else:
                        S.op("dve", lambda h: h.tensor_copy(out=V4[:, t, :, 0:64], in_=pv), r=(B_pg,), w=(B_V,))
                for qi_, (q0, q1) in enumerate(QG):
                    nq = q1 - q0
                    nqt = nq // 128
                    isctx = (qi_ == 0)
                    kl = list(range(2)) if isctx else list(range(NT))
                    if not isctx:
                        load_rope(q0 - NCTX, q1 - NCTX, 0)
                    for jl in range(2):
                        w_g = load_w(w_in_d[l, CH_GMLA + hg_ * 2 + jl], 8, 128)
                        pg, B_pg = next_gen()
                        for kc in range(8):
                            S.op("pe", lambda h, kc=kc: h.matmul(pg[:, 0:nq], lhsT=w_g[0][:, kc, :],
                                                                   rhs=uT[:, kc, q0:q1], start=(kc == 0),
                                                                   stop=(kc == 7)),
                                 r=(w_g[1], B_uT), w=(B_pg,), mark=(kc == 7))
                        S.op("act", lambda h: h.activation(out=sgm3[:, jl, 0:nq], in_=pg[:, 0:nq], func=AF.Silu),
                             r=(B_pg,), w=(B_sgm,))
                    for hl in range(4):
                        h_ = hg_ * 4 + hl
                        pq, B_pq = next_gen()
                        for kc in range(2):
                            S.op("pe", lambda h, kc=kc: h.matmul(pq[0:96, 0:nq], lhsT=wuq4[:, h_, kc, :],
                                                                   rhs=cqn3[:, kc, q0:q1], start=(kc == 0),
                                                                   stop=(kc == 1)),
                                 r=(B_wuq, B_cqn), w=(B_pq,), mark=(kc == 1))
                        if isctx:
                            S.op("act", lambda h: h.copy(out=qT[hl][0][0:96, 0:nq], in_=pq[0:96, 0:nq]), r=(B_pq,),
                                 w=(qT[hl][1],))
                        else:
                            pq2, B_pq2 = next_gen()
                            for kc in range(2):
                                S.op("pe", lambda h, kc=kc: h.matmul(pq2[0:96, 0:nq], lhsT=wuq4[:, 8 + h_, kc, :],
                                                                       rhs=cqn3[:, kc, q0:q1], start=(kc == 0),
                                                                       stop=(kc == 1)),
                                     r=(B_wuq, B_cqn), w=(B_pq2,), mark=(kc == 1))
                            S.op("act", lambda h: h.copy(out=qT[hl][0][0:64, 0:nq], in_=pq[0:64, 0:nq]), r=(B_pq,),
                                 w=(qT[hl][1],))
                            rope_rows(qT[hl][0][64:96, 0:nq], qT[hl][1], pq, B_pq, pq2, B_pq2, 0, nq, 0)
                    pi_ = 0
                    for hl in range(4):
                        po, B_po = next_acc()
                        po3 = po[:, 0:nqt * 65].rearrange("p (q c) -> p q c", c=65)
                        for ki, kt in enumerate(kl):
                            pa, B_pa = next_aux()
                            S.op("pe", lambda h: h.matmul(pa[:, 0:nq], lhsT=kT[hl][0][0:96, kt * 128:(kt + 1) * 128],
                                                           rhs=qT[hl][0][0:96, 0:nq], start=True, stop=True),
                                 r=(kT[hl][1], qT[hl][1]), w=(B_pa,))
                            pt_, B_pt = pT[pi_ % 3]
                            pi_ += 1
                            S.op("act", lambda h: h.activation(out=pt_[:, 0:nq], in_=pa[:, 0:nq], func=AF.Exp,
                                                                scale=MLA_SCALE), r=(B_pa,), w=(B_pt,))
                            for qq in range(nqt):
                                S.op("pe", lambda h, qq=qq: h.matmul(po3[:, qq, :],
                                                                       lhsT=pt_[:, qq * 128:(qq + 1) * 128],
                                                                       rhs=V4[:, kt, hl, :],
                                                                       start=(ki == 0 and qq == 0),
                                                                       stop=(ki == len(kl) - 1),
                                                                       skip_group_check=True),
                                     r=(B_pt, B_V), w=(B_po,), mark=(qq == nqt - 1))
                        S.op("dve", lambda h: h.reciprocal(out=rec[:, 0:nqt], in_=po3[:, :, 64]), r=(B_po,),
                             w=(B_rec,))
                        S.op("dve", lambda h: h.tensor_tensor(out=osb3[:, 0:nqt, hl * 64:(hl + 1) * 64],
                                                               in0=po3[:, :, 0:64], in1=bc3(rec[:, 0:nqt], 64),
                                                               op=ALU.mult), r=(B_po, B_rec), w=(B_osb,))
                    for jl in range(2):
                        pt, B_pt = next_pbf()
                        for qq in range(nqt):
                            S.op("pe", lambda h, qq=qq: h.transpose(pt[:, qq * 128:(qq + 1) * 128],
                                                                      osb3[:, qq, jl * 128:(jl + 1) * 128],
                                                                      ident_b[:]),
                                 r=(B_osb, B_const), w=(B_pt,), mark=(qq == nqt - 1))
                        S.op("dve", lambda h: h.tensor_tensor(out=yT[:, hg_ * 2 + jl, q0:q1], in0=pt[:, 0:nq],
                                                               in1=sgm3[:, jl, 0:nq], op=ALU.mult),
                             r=(B_pt, B_sgm), w=(B_yT[0],))
            fence_arena(allb)

        def merge_out(b, l, last):
            AR.reset()
            mT, B_mT = AR.bf(8 * NTOK, "mT")
            wo, B_wo = AR.bf(8 * 1024, "wo")
            sgt = [AR.f32(512, f"sg{i}") for i in range(3)]
            macc = [AR.f32(512, f"macc{i}") for i in range(2)]
            tmpm = [AR.f32(512, f"tmpm{i}") for i in range(2)]
            hts = [AR.f32(1024, f"ht{i}") for i in range(3)]
            tmo = [AR.f32(512, f"tmo{i}") for i in range(2)]
            junk, B_junk = AR.bf(1024, "junkm")
            allb = [B_mT, B_wo, B_junk] + [x[1] for x in sgt + macc + tmpm + hts + tmo]
            mT3 = mT.rearrange("p (k n) -> p k n", n=NTOK)
            wo3 = wo.rearrange("p (k c) -> p k c", c=1024)
            S.dma("pool", wo3, w_out_d[l], w=(B_wo,), chan=B_wo)
            si = 0
            for fc in range(8):
                wg = [load_w(w_in_d[l, CH_GATE + 3 * fc + br], 8, 128) for br in range(3)]
                wb = [load_w(w_br_d[l, 3 * fc + br], 4, 128) for br in range(3)]
                for gi, (a0, a1) in enumerate(TG):
                    n = a1 - a0
                    ma, B_ma = macc[gi % 2]
                    for br in range(3):
                        pg, B_pg = next_gen()
                        for kc in range(8):
                            S.op("pe", lambda h, kc=kc: h.matmul(pg[:, 0:n], lhsT=wg[br][0][:, kc, :],
                                                                   rhs=uT[:, kc, a0:a1], start=(kc == 0),
                                                                   stop=(kc == 7)),
                                 r=(wg[br][1], B_uT), w=(B_pg,), mark=(kc == 7))
                        sg, B_sg = sgt[si % 3]
                        si += 1
                        S.op("act", lambda h: h.activation(out=sg[:, 0:n], in_=pg[:, 0:n], func=AF.Sigmoid),
                             r=(B_pg,), w=(B_sg,))
                        pb_, B_pb = next_gen()
                        for kc in range(4):
                            S.op("pe", lambda h, kc=kc: h.matmul(pb_[:, 0:n], lhsT=wb[br][0][:, kc, :],
                                                                   rhs=yT[:, br * 4 + kc, a0:a1], start=(kc == 0),
                                                                   stop=(kc == 3)),
                                 r=(wb[br][1], B_yT[br]), w=(B_pb,), mark=(kc == 3))
                        if br == 0:
                            S.op("dve", lambda h: h.tensor_tensor(out=ma[:, 0:n], in0=pb_[:, 0:n], in1=sg[:, 0:n],
                                                                   op=ALU.mult), r=(B_pb, B_sg), w=(B_ma,))
                        else:
                            tm, B_tm = tmpm[br - 1]
                            S.op("dve", lambda h: h.tensor_tensor(out=tm[:, 0:n], in0=pb_[:, 0:n], in1=sg[:, 0:n],
                                                                   op=ALU.mult), r=(B_pb, B_sg), w=(B_tm,))
                            if br == 1:
                                S.op("pool", lambda h: h.tensor_tensor(out=ma[:, 0:n], in0=ma[:, 0:n],
                                                                        in1=tm[:, 0:n], op=ALU.add),
                                     r=(B_ma, B_tm), w=(B_ma,))
                            else:
                                S.op("pool", lambda h: h.tensor_tensor(out=mT3[:, fc, a0:a1], in0=ma[:, 0:n],
                                                                        in1=tm[:, 0:n], op=ALU.add),
                                     r=(B_ma, B_tm), w=(B_mT,))
            for t in range(NT):
                ri = 1 if t < 2 else 0
                if last and t < 2:
                    continue
                ht, B_ht = hts[t % 3]
                src = xin[b, t * 128:(t + 1) * 128, :] if l == 0 else hbuf_d[t * 128:(t + 1) * 128, :]
                S.dma("sp", ht, src, r=(() if l == 0 else (B_hb[t],)), w=(B_ht,), chan=B_ht)
                for half in range(2):
                    pg, B_pg = next_gen()
                    for kc in range(8):
                        S.op("pe", lambda h, kc=kc: h.matmul(pg[:, :], lhsT=mT3[:, kc, t * 128:(t + 1) * 128],
                                                               rhs=wo3[:, kc, half * 512:(half + 1) * 512],
                                                               start=(kc == 0), stop=(kc == 7)),
                             r=(B_mT, B_wo), w=(B_pg,), mark=(kc == 7))
                    to, B_to = tmo[half]
                    S.op("dve", lambda h: h.tensor_tensor(out=to, in0=pg[:, :],
                                                           in1=gate_bc[:, ri, half * 512:(half + 1) * 512],
                                                           op=ALU.mult), r=(B_pg, B_gate), w=(B_to,))
                    S.op("pool", lambda h: h.tensor_tensor(out=ht[:, half * 512:(half + 1) * 512],
                                                            in0=ht[:, half * 512:(half + 1) * 512], in1=to,
                                                            op=ALU.add), r=(B_ht, B_to), w=(B_ht,))
                if not last:
                    S.dma("sp", hbuf_d[t * 128:(t + 1) * 128, :], ht, r=(B_ht,), w=(B_hb[t],), chan=B_ht)
                    if dbg and b == 0 and l == 0:
                        S.drain("sp", (B_ht,))
                        S.dma("sp", dbg_h[t * 128:(t + 1) * 128, :], ht, r=(B_ht,), w=(), chan=B_ht)
                else:
                    S.op("act", lambda h: h.activation(out=junk, in_=ht, func=AF.Square,
                                                        accum_out=stat[:, 32 + t:33 + t]),
                         r=(B_ht,), w=(B_junk, B_stat))
                    S.op("act", lambda h: h.activation(out=stat[:, 32 + t:33 + t], in_=stat[:, 32 + t:33 + t],
                                                        func=AF.Sqrt, scale=1.0 / D, bias=EPS), r=(B_stat,),
                         w=(B_stat,))
                    S.op("dve", lambda h: h.reciprocal(out=stat[:, 32 + t:33 + t], in_=stat[:, 32 + t:33 + t]),
                         r=(B_stat,), w=(B_stat,))
                    S.op("dve", lambda h: h.tensor_scalar(out=ht, in0=ht, scalar1=stat[:, 32 + t:33 + t],
                                                           scalar2=None, op0=ALU.mult), r=(B_ht, B_stat),
                         w=(B_ht,))
                    S.op("pool", lambda h: h.tensor_tensor(out=ht, in0=ht, in1=fng_bc[:], op=ALU.mult),
                         r=(B_ht, B_const), w=(B_ht,))
                    S.dma("sp", out_d[b, (t - 2) * 128:(t - 1) * 128, :], ht, r=(B_ht,), w=(), chan=B_ht)
            fence_arena(allb + [B_stat])

        for b in range(NB):
            for l in range(NL):
                layer(b, l, l == NL - 1)
        for cb in S.chans:
            S.E["sp"].h.wait_ge(cb.chan[0], cb.chan[1])
    return nc


IN_OFF = {}
_o = 0
for _n, _s in (("cq", 256), ("ckv", 128), ("kr", 32), ("gmla", 512), ("hq", 512), ("hi", 512), ("hff", 512),
               ("hfb", 512), ("ghg", 512), ("cx", 512), ("cb", 512), ("cc", 512), ("gcv", 512), ("gate", 3072)):
    IN_OFF[_n] = _o
    _o += _s


def _stat(wcols):
    K, ncol = wcols.shape
    return np.ascontiguousarray(wcols.reshape(K // 128, 128, ncol).transpose(1, 0, 2))


def _swap_idx():
    d = np.arange(32)
    a, hf, p = d // 16, (d // 8) % 2, d % 8
    return a * 16 + (1 - hf) * 8 + p


def prep_weights(w_in, mla_w_uq, mla_w_ukv, w_branch, w_out, ada_w, ada_b):
    sw = _swap_idx()
    W_in = np.zeros((L, NCH, 128, 8, 128), np.float32)
    W_uq = np.zeros((L, 16, 128, 2, 96), np.float32)
    W_kn = np.zeros((L, 128, 8, 64), np.float32)
    W_v = np.zeros((L, 128, 512), np.float32)
    W_br = np.zeros((L, 24, 128, 4, 128), np.float32)
    W_out = np.zeros((L, 128, 8, 1024), np.float32)
    ADA = np.zeros((L, 128, 8, 3072), np.float32)
    for l in range(L):
        w = w_in[l]

        def cols(name, j, width=128):
            o = IN_OFF[name] + j * width
            return w[:, o:o + width]

        for j in range(4):
            for q, nm in enumerate(("cx", "cc", "cb", "gcv")):
                W_in[l, CH_CONV + 4 * j + q] = _stat(cols(nm, j))
        for hp in range(4):
            for q, nm in enumerate(("hq", "hff", "hfb", "ghg")):
                W_in[l, CH_HG + 4 * hp + q] = _stat(cols(nm, hp))
            W_in[l, CH_HI + hp] = _stat(cols("hi", hp))
        for j in range(2):
            W_in[l, CH_CQ + j] = _stat(cols("cq", j))
        W_in[l, CH_CKV] = _stat(cols("ckv", 0))
        kr = w[:, IN_OFF["kr"]:IN_OFF["kr"] + 32]
        ka = np.zeros((1024, 128), np.float32)
        kb = np.zeros((1024, 128), np.float32)
        ka[:, 64:96] = kr
        kb[:, 64:96] = kr[:, sw]
        W_in[l, CH_KRA] = _stat(ka)
        W_in[l, CH_KRB] = _stat(kb)
        for j in range(4):
            W_in[l, CH_GMLA + j] = _stat(cols("gmla", j))
        for fc in range(8):
            for br in range(3):
                o = IN_OFF["gate"] + br * 1024 + fc * 128
                W_in[l, CH_GATE + 3 * fc + br] = _stat(w[:, o:o + 128])
        uq = mla_w_uq[l]
        for h in range(8):
            blk = uq[:, h * 96:(h + 1) * 96]
            W_uq[l, h] = _stat(blk)
            blk2 = blk.copy()
            blk2[:, 64:96] = blk[:, 64 + sw]
            W_uq[l, 8 + h] = _stat(blk2)
        ukv = mla_w_ukv[l]
        for h in range(8):
            W_kn[l, :, h, :] = ukv[:, h * 128:h * 128 + 64]
            W_v[l, :, h * 64:(h + 1) * 64] = ukv[:, h * 128 + 64:h * 128 + 128]
        for fc in range(8):
            for br in range(3):
                W_br[l, 3 * fc + br] = _stat(w_branch[l, br][:, fc * 128:(fc + 1) * 128])
        W_out[l] = _stat(w_out[l])
        ADA[l] = _stat(ada_w[l])
    return dict(w_in=W_in, w_uq=W_uq, w_kn=W_kn, w_v=W_v, w_br=W_br, w_out=W_out, ada=ADA,
                adab=np.ascontiguousarray(ada_b.reshape(L, 1, 3072)))


def prep_small(norm_g, mla_q_norm_g, mla_kv_norm_g, hg_norm_g, conv_w, conv_b, hg_lb_logits):
    sm = np.zeros((128, NSM), np.float32)
    fm = lambda v: np.ascontiguousarray(v.reshape(-1, 128).T)
    for l in range(L):
        sm[:, SM_NG + l * 8:SM_NG + (l + 1) * 8] = fm(norm_g[l])
        sm[:, SM_QG + l * 2:SM_QG + (l + 1) * 2] = fm(mla_q_norm_g[l])
        sm[:, SM_KVG + l:SM_KVG + l + 1] = fm(mla_kv_norm_g[l])
        sm[:, SM_HGG + l * 4:SM_HGG + (l + 1) * 4] = fm(hg_norm_g[l])
        for k in range(3):
            sm[:, SM_CW + l * 12 + k * 4:SM_CW + l * 12 + (k + 1) * 4] = fm(conv_w[l, k])
        sm[:, SM_CB + l * 4:SM_CB + (l + 1) * 4] = fm(conv_b[l])
        for d in range(2):
            sm[:, SM_LB + l * 8 + d * 4:SM_LB + l * 8 + (d + 1) * 4] = fm(hg_lb_logits[l, d])
    d = np.arange(32)
    a, hf, p = d // 16, (d // 8) % 2, d % 8
    sm[64:96, SM_ROPE] = (10000.0 ** (-(p.astype(np.float64)) / 8.0)).astype(np.float32)
    sm[64:96, SM_ROPE + 1] = a.astype(np.float32)
    sm[64:96, SM_ROPE + 2] = np.where(hf == 0, -1.0, 1.0).astype(np.float32)
    return sm


_CACHE = {}


def kernel(x, c, ctx, c_ctx, ada_w, ada_b, norm_g, w_in, mla_q_norm_g, mla_kv_norm_g, mla_w_uq, mla_w_ukv,
           hg_lb_logits, hg_norm_g, conv_w, conv_b, w_branch, w_out, final_norm_g):
    f = lambda a: np.asarray(a, dtype=np.float32)
    x, c, ctx, c_ctx = f(x), f(c), f(ctx), f(c_ctx)
    W = prep_weights(f(w_in), f(mla_w_uq), f(mla_w_ukv), f(w_branch), f(w_out), f(ada_w), f(ada_b))
    sm = prep_small(f(norm_g), f(mla_q_norm_g), f(mla_kv_norm_g), f(hg_norm_g), f(conv_w), f(conv_b),
                    f(hg_lb_logits))
    fng = np.ascontiguousarray(f(final_norm_g).reshape(1, D))
    n_cores = 8
    NB = x.shape[0] // n_cores
    if "nc" not in _CACHE:
        _CACHE["nc"] = build(NB=NB, NL=L)
    nc = _CACHE["nc"]
    in_maps = []
    for ci in range(n_cores):
        bs = slice(ci * NB, (ci + 1) * NB)
        xin = np.ascontiguousarray(np.concatenate([ctx[bs], x[bs]], axis=1))
        rows = np.concatenate([c[bs], c_ctx[None, :]], axis=0)
        cT = np.ascontiguousarray(rows.T.reshape(8, 128, 5).transpose(1, 0, 2))
        m = dict(xin=xin, cT=cT, smallp=sm, fng=fng)
        m.update(W)
        in_maps.append(m)
    res = run_bass_kernel_spmd(nc, in_maps, core_ids=list(range(n_cores)))
    return np.concatenate([r["out"] for r in res.results], axis=0).astype(np.float32)
```

```python
import math
import numpy as np
import concourse.bass as bass
import concourse.mybir as mybir
from concourse.bass_utils import run_bass_kernel_spmd
from contextlib import ExitStack

F32 = mybir.dt.float32
BF16 = mybir.dt.bfloat16
U8 = mybir.dt.uint8
I32 = mybir.dt.int32
AF = mybir.ActivationFunctionType
ALU = mybir.AluOpType
AX = mybir.AxisListType

L = 4
D = 1024
NTOK = 2304
NCTX = 256
NLAT = 2048
NT = 18
EPS = 1e-6
MLA_SCALE = 96.0 ** -0.5
TG = [(0, 512), (512, 1024), (1024, 1536), (1536, 2048), (2048, 2304)]
QG = [(0, 256), (256, 768), (768, 1280), (1280, 1792), (1792, 2304)]

CH_CONV = 0
CH_HG = 16
CH_CQ = 32
CH_CKV = 34
CH_KRA = 35
CH_KRB = 36
CH_GMLA = 37
CH_GATE = 41
CH_HI = 65
NCH = 69

SM_NG = 0
SM_QG = SM_NG + L * 8
SM_KVG = SM_QG + L * 2
SM_HGG = SM_KVG + L
SM_CW = SM_HGG + L * 4
SM_CB = SM_CW + L * 12
SM_LB = SM_CB + L * 4
SM_ROPE = SM_LB + L * 8
NSM = SM_ROPE + 3


class Buf:
    __slots__ = ("name", "w", "r", "chan")

    def __init__(self, name):
        self.name = name
        self.w = {}
        self.r = {}
        self.chan = None


class Eng:
    def __init__(self, name, h):
        self.name = name
        self.h = h
        self.sem = None
        self.count = 0
        self.seen = {}
        self.pend_r = []
        self.pend_w = []


class Sync:
    def __init__(self, nc, es):
        self.nc = nc
        self.es = es
        self.nsem = 0
        self.E = {
            "pe": Eng("pe", nc.tensor),
            "act": Eng("act", nc.scalar),
            "dve": Eng("dve", nc.vector),
            "pool": Eng("pool", nc.gpsimd),
            "sp": Eng("sp", nc.sync),
        }
        for e in self.E.values():
            self._new_sem(e)
        self.chans = []
        self.halted = False

    def sem(self, name):
        self.nsem += 1
        return self.es.enter_context(self.nc.semaphore(f"{name}_{self.nsem}"))

    def _new_sem(self, e):
        e.sem = self.sem("e" + e.name)
        e.count = 0

    def _merge(self, d, src, skip=None):
        for k, (s, v) in src.items():
            if skip is not None and k == skip:
                continue
            if k not in d or d[k][1] < v:
                d[k] = (s, v)

    def _waits(self, e, reads, writes):
        d = {}
        own = e.sem.num if e.name in ("pe", "sp") else None
        for b in reads:
            self._merge(d, b.w)
        for b in writes:
            self._merge(d, b.w, skip=own)
            self._merge(d, b.r, skip=own)
        for k, (s, v) in d.items():
            if e.seen.get(k, 0) < v:
                e.h.wait_ge(s, v)
                e.seen[k] = v

    def op(self, eng, fn, r=(), w=(), mark=True):
        if self.halted:
            return None
        e = self.E[eng]
        self._waits(e, r, w)
        inst = fn(e.h)
        if mark:
            if e.count >= 60000:
                if not e.pend_r and not e.pend_w:
                    self._new_sem(e)
            inst.then_inc(e.sem, 1)
            e.count += 1
            tok = (e.sem, e.count)
            k = e.sem.num
            for b in list(r) + e.pend_r:
                b.r[k] = tok
            for b in list(w) + e.pend_w:
                b.w[k] = tok
            e.pend_r = []
            e.pend_w = []
        else:
            e.pend_r.extend(r)
            e.pend_w.extend(w)
        return inst

    def dma(self, q, out, in_, r=(), w=(), chan=None, **kw):
        if self.halted:
            return None
        e = self.E[q]
        self._waits(e, r, w)
        if chan.chan is None:
            chan.chan = [self.sem("d"), 0]
            self.chans.append(chan)
        inst = e.h.dma_start(out=out, in_=in_, **kw)
        chan.chan[1] += 16
        inst.then_inc(chan.chan[0], 16)
        tok = (chan.chan[0], chan.chan[1])
        k = chan.chan[0].num
        for b in r:
            b.r[k] = tok
        for b in w:
            b.w[k] = tok
        return inst

    def self_wait(self, eng):
        if self.halted:
            return
        e = self.E[eng]
        assert not e.pend_r and not e.pend_w
        e.h.wait_ge(e.sem, e.count)

    def drain(self, eng, bufs):
        if self.halted:
            return
        e = self.E[eng]
        self._waits(e, bufs, bufs)


class _Stop(Exception):
    pass


def build(NB=4, NL=4, dbg=False, stop_after=None):
    nc = bass.Bass("TRN2", target_bir_lowering=False)

    def din(name, shape):
        return nc.dram_tensor(name, shape, F32, kind="ExternalInput").ap()

    xin = din("xin", [NB, NTOK, D])
    cT_d = din("cT", [128, 8, 5])
    ada_d = din("ada", [L, 128, 8, 3072])
    adab_d = din("adab", [L, 1, 3072])
    w_in_d = din("w_in", [L, NCH, 128, 8, 128])
    w_uq_d = din("w_uq", [L, 16, 128, 2, 96])
    w_kn_d = din("w_kn", [L, 128, 8, 64])
    w_v_d = din("w_v", [L, 128, 512])
    w_br_d = din("w_br", [L, 24, 128, 4, 128])
    w_out_d = din("w_out", [L, 128, 8, 1024])
    smallp_d = din("smallp", [128, NSM])
    fng_d = din("fng", [1, D])
    out_d = nc.dram_tensor("out", [NB, NLAT, D], F32, kind="ExternalOutput").ap()
    hbuf_d = nc.dram_tensor("hbuf", [NTOK, D], F32, kind="Internal").ap()
    modrows_d = nc.dram_tensor("modrows", [L, 5, 3072], F32, kind="Internal").ap()
    ropetab_d = nc.dram_tensor("ropetab", [2, 32, NLAT], F32, kind="Internal").ap()
    if dbg:
        dbg_uT = nc.dram_tensor("dbg_uT", [128, 8, NTOK], BF16, kind="ExternalOutput").ap()
        dbg_yT = nc.dram_tensor("dbg_yT", [128, 12, NTOK], BF16, kind="ExternalOutput").ap()
        dbg_h = nc.dram_tensor("dbg_h", [NTOK, D], F32, kind="ExternalOutput").ap()
        dbg_of = nc.dram_tensor("dbg_of", [2, 128, NT * 128], F32, kind="ExternalOutput").ap()
        dbg_S = nc.dram_tensor("dbg_S", [2, 128, 36 * 64], BF16, kind="ExternalOutput").ap()
        dbg_q = nc.dram_tensor("dbg_q", [2, 4, 128, NTOK], BF16, kind="ExternalOutput").ap()
        dbg_st = nc.dram_tensor("dbg_st", [2, 3, 128, 36], F32, kind="ExternalOutput").ap()

    with ExitStack() as es:
        S = Sync(nc, es)

        def ck(n):
            if stop_after == n:
                S.halted = True

        def sb(name, shape, dt):
            return es.enter_context(nc.sbuf_tensor("s_" + name, shape, dt))

        def ps(name, shape, dt):
            return es.enter_context(nc.psum_tensor("p_" + name, shape, dt))

        ident_f = sb("ident_f", [128, 128], F32)
        ident_b = sb("ident_b", [128, 128], BF16)
        ones_f = sb("ones_f", [128, 128], F32)
        maskf32 = sb("maskf32", [128, 2, 128], F32)
        masks = sb("masks", [128, 2, 128], U8)
        smallp = sb("smallp", [128, NSM], F32)
        lbt = sb("lbt", [128, L, 8], F32)
        omlt = sb("omlt", [128, L, 8], F32)
        fng_bc = sb("fng_bc", [128, D], F32)
        resetm = sb("resetm", [128, 512], F32)
        uT = sb("uT", [128, 8, NTOK], BF16)
        yT = sb("yT", [128, 12, NTOK], BF16)
        wslot = [sb(f"wslot{i}", [128, 1024], BF16) for i in range(6)]
        modT = sb("modT", [128, 2, 16], F32)
        cscale = sb("cscale", [128, 2, 8], F32)
        gate_bc = sb("gate_bc", [128, 2, D], F32)
        stat = sb("stat", [128, 64], F32)
        AR_WORDS = 22000
        arena = sb("arena", [128, AR_WORDS], F32)

        B_const = Buf("const")
        B_small = Buf("small")
        B_uT = Buf("uT")
        B_yT = [Buf(f"yT{i}") for i in range(3)]
        B_w = [Buf(f"w{i}") for i in range(6)]
        B_mod = Buf("mod")
        B_gate = Buf("gatebc")
        B_stat = Buf("stat")
        B_hb = [Buf(f"hb{t}") for t in range(NT)]
        B_modrows = Buf("modrows")
        B_rope = Buf("ropetab")

        gen = [ps(f"gen{i}", [128, 512], F32) for i in range(2)]
        aux = [ps(f"scr{i}", [128, 512], F32) for i in range(2)]
        accp = [ps(f"accp{i}", [128, 512], F32) for i in range(2)]
        pbf = [ps(f"pbf{i}", [128, 1024], BF16) for i in range(2)]
        B_gen = [Buf(f"gen{i}") for i in range(2)]
        B_aux = [Buf(f"aux{i}") for i in range(2)]
        B_acc = [Buf("acc0"), Buf("acc1")]
        B_pbf = [Buf("pbf0"), Buf("pbf1")]
        ctr = {"gen": 0, "aux": 0, "pbf": 0, "acc": 0, "w": 0}

        def next_gen():
            i = ctr["gen"] % 2
            ctr["gen"] += 1
            return gen[i], B_gen[i]

        def next_aux():
            i = ctr["aux"] % 2
            ctr["aux"] += 1
            return aux[i], B_aux[i]

        def next_pbf():
            i = ctr["pbf"] % 2
            ctr["pbf"] += 1
            return pbf[i][:, 0:512], B_pbf[i]

        def next_acc():
            i = ctr["acc"] % 2
            ctr["acc"] += 1
            return accp[i], B_acc[i]

        class Arena:
            def __init__(self):
                self.off = 0
                self.bufs = {}

            def buf(self, name):
                if name not in self.bufs:
                    self.bufs[name] = Buf(name)
                return self.bufs[name]

            def reset(self):
                self.off = 0

            def f32(self, words, name):
                a = arena[:, self.off:self.off + words]
                self.off += words
                assert self.off <= AR_WORDS, (name, self.off)
                return a, self.buf(name)

            def bf(self, elems, name):
                words = (elems + 1) // 2
                a = arena[:, self.off:self.off + words].bitcast(BF16)
                self.off += words
                assert self.off <= AR_WORDS, (name, self.off)
                return a, self.buf(name)

        AR = Arena()
        B_arena_all = Buf("arena_all")

        def load_w(src_ap, kc, ncol):
            i = ctr["w"] % 6
            ctr["w"] += 1
            view = wslot[i][:, 0:kc * ncol].rearrange("p (k c) -> p k c", c=ncol)
            S.dma("pool", view, src_ap, r=(), w=(B_w[i],), chan=B_w[i])
            return view, B_w[i]

        S.dma("sp", smallp[:], smallp_d[:, :], w=(B_small,), chan=B_small)
        S.dma("sp", fng_bc[:], fng_d[0:1, :].partition_broadcast(128), w=(B_const,), chan=B_const)
        S.op("pool", lambda h: h.memset(ident_f[:], 1.0), w=(B_const,))
        S.op("pool", lambda h: h.affine_select(out=ident_f[:], in_=ident_f[:], pattern=[[-1, 128]],
                                                compare_op=ALU.is_equal, fill=0.0, base=0, channel_multiplier=1),
             r=(B_const,), w=(B_const,))
        S.op("pool", lambda h: h.tensor_copy(out=ident_b[:], in_=ident_f[:]), r=(B_const,), w=(B_const,))
        S.op("pool", lambda h: h.memset(ones_f[:], 1.0), w=(B_const,))
        S.op("pool", lambda h: h.memset(maskf32[:], 1.0), w=(B_const,))
        S.op("pool", lambda h: h.affine_select(out=maskf32[:, 0, :], in_=maskf32[:, 0, :], pattern=[[1, 128]],
                                                compare_op=ALU.is_ge, fill=0.0, base=0, channel_multiplier=-1),
             r=(B_const,), w=(B_const,))
        S.op("pool", lambda h: h.affine_select(out=maskf32[:, 1, :], in_=maskf32[:, 1, :], pattern=[[-1, 128]],
                                                compare_op=ALU.is_ge, fill=0.0, base=0, channel_multiplier=1),
             r=(B_const,), w=(B_const,))
        S.op("pool", lambda h: h.memset(maskf32[0:64, 0, 64:128], 0.0), r=(B_const,), w=(B_const,))
        S.op("pool", lambda h: h.memset(maskf32[64:128, 1, 0:64], 0.0), r=(B_const,), w=(B_const,))
        S.op("pool", lambda h: h.tensor_copy(out=masks[:], in_=maskf32[:]), r=(B_const,), w=(B_const,))
        S.op("pool", lambda h: h.memset(resetm[:], 1.0), w=(B_const,))
        S.op("pool", lambda h: h.memset(resetm[:].rearrange("p (c t) -> p c t", t=64)[:, :, 0:1], 0.0),
             r=(B_const,), w=(B_const,))

        def fence_arena(bufs):
            for en in ("pe", "act", "dve", "pool", "sp"):
                S.drain(en, bufs)

        ck(0)
        AR.reset()
        e_lb, B_elb = AR.f32(L * 8, "e_lb")
        s_lb, B_slb = AR.f32(8, "s_lb")
        e3 = e_lb.rearrange("p (l k) -> p l k", k=8)
        S.op("act", lambda h: h.activation(out=e_lb, in_=smallp[:, SM_LB:SM_LB + L * 8], func=AF.Exp),
             r=(B_small,), w=(B_elb,))
        S.op("dve", lambda h: h.tensor_tensor(out=s_lb, in0=e3[:, 0, :], in1=e3[:, 1, :], op=ALU.add),
             r=(B_elb,), w=(B_slb,))
        S.op("dve", lambda h: h.tensor_tensor(out=s_lb, in0=s_lb, in1=e3[:, 2, :], op=ALU.add),
             r=(B_elb, B_slb), w=(B_slb,))
        S.op("dve", lambda h: h.tensor_tensor(out=s_lb, in0=s_lb, in1=e3[:, 3, :], op=ALU.add),
             r=(B_elb, B_slb), w=(B_slb,))
        S.op("dve", lambda h: h.reciprocal(out=s_lb, in_=s_lb), r=(B_slb,), w=(B_slb,))
        S.op("dve", lambda h: h.memset(lbt[:, 0, :], 0.0), w=(B_const,))
        for l in range(1, L):
            S.op("dve", lambda h, l=l: h.tensor_tensor(out=e3[:, l, :], in0=e3[:, l, :], in1=s_lb, op=ALU.mult),
                 r=(B_elb, B_slb), w=(B_elb,))
            S.op("dve", lambda h, l=l: h.tensor_tensor(out=lbt[:, l, :], in0=lbt[:, l - 1, :], in1=e3[:, l, :],
                                                        op=ALU.add),
                 r=(B_elb, B_const), w=(B_const,))
        S.op("dve", lambda h: h.tensor_scalar(out=omlt[:], in0=lbt[:], scalar1=-1.0, scalar2=1.0,
                                               op0=ALU.mult, op1=ALU.add), r=(B_const,), w=(B_const,))

        ck(1)
        fence_arena([B_elb, B_slb])
        AR.reset()
        cTs, B_cT = AR.f32(40, "cT")
        adat = [AR.f32(4096, f"adat{i}") for i in range(2)]
        biasr, B_biasr = AR.f32(3072, "biasr")
        mrow = [AR.f32(512, f"mrow{i}") for i in range(2)]
        cT3 = cTs.rearrange("p (k r) -> p k r", r=5)
        S.dma("sp", cT3, cT_d[:, :, :], w=(B_cT,), chan=B_cT)
        S.op("act", lambda h: h.activation(out=cTs, in_=cTs, func=AF.Silu), r=(B_cT,), w=(B_cT,))
        ci = 0
        for l in range(L):
            S.dma("sp", biasr[0:5, :], adab_d[l, 0:1, :].partition_broadcast(5), w=(B_biasr,), chan=B_biasr)
            for cb in range(6):
                at, B_at = adat[ci % 2]
                mr, B_mr = mrow[ci % 2]
                ci += 1
                at3 = at.rearrange("p (k c) -> p k c", c=512)
                S.dma("sp", at3, ada_d[l, :, :, cb * 512:(cb + 1) * 512], w=(B_at,), chan=B_at)
                pg, B_pg = next_gen()
                for kc in range(8):
                    S.op("pe", lambda h, kc=kc: h.matmul(pg[0:5, :], lhsT=cT3[:, kc, :], rhs=at3[:, kc, :],
                                                           start=(kc == 0), stop=(kc == 7)),
                         r=(B_cT, B_at), w=(B_pg,), mark=(kc == 7))
                S.op("dve", lambda h: h.tensor_tensor(out=mr[0:5, :], in0=pg[0:5, :],
                                                       in1=biasr[0:5, cb * 512:(cb + 1) * 512], op=ALU.add),
                     r=(B_pg, B_biasr), w=(B_mr,))
                S.dma("sp", modrows_d[l, :, cb * 512:(cb + 1) * 512], mr[0:5, :], r=(B_mr,), w=(B_modrows,),
                      chan=B_mr)

        ck(2)
        fence_arena([B_cT, B_biasr] + [b for _, b in adat] + [b for _, b in mrow])
        AR.reset()
        rowf, B_r0 = AR.f32(2048, "rowf")
        colf, B_r1 = AR.f32(2048, "colf")
        t_a, B_r2 = AR.f32(2048, "t_a")
        t_b, B_r3 = AR.f32(2048, "t_b")
        t_c, B_r4 = AR.f32(2048, "t_c")
        t_i = arena[:, AR.off:AR.off + 2048].bitcast(I32)
        AR.off += 2048
        B_r5 = AR.buf("t_i")
        RF = smallp[:, SM_ROPE:SM_ROPE + 1]
        RA = smallp[:, SM_ROPE + 1:SM_ROPE + 2]
        RS = smallp[:, SM_ROPE + 2:SM_ROPE + 3]
        S.op("pool", lambda h: h.iota(rowf.rearrange("p (a b) -> p a b", b=64), pattern=[[1, 32], [0, 64]], base=0,
                                       channel_multiplier=0, allow_small_or_imprecise_dtypes=True), w=(B_r0,))
        S.op("pool", lambda h: h.iota(colf.rearrange("p (a b) -> p a b", b=64), pattern=[[0, 32], [1, 64]], base=0,
                                       channel_multiplier=0, allow_small_or_imprecise_dtypes=True), w=(B_r1,))
        S.op("dve", lambda h: h.tensor_tensor(out=colf, in0=colf, in1=rowf, op=ALU.subtract),
             r=(B_r0, B_r1), w=(B_r1,))
        S.op("dve", lambda h: h.scalar_tensor_tensor(out=t_a, in0=colf, scalar=RA, in1=rowf, op0=ALU.mult,
                                                      op1=ALU.add), r=(B_r0, B_r1, B_small), w=(B_r2,))
        S.op("dve", lambda h: h.tensor_scalar(out=t_a, in0=t_a, scalar1=RF, scalar2=None, op0=ALU.mult),
             r=(B_r2, B_small), w=(B_r2,))
        S.op("dve", lambda h: h.tensor_scalar(out=t_b, in0=t_a, scalar1=1.0 / (2 * math.pi), scalar2=None,
                                               op0=ALU.mult), r=(B_r2,), w=(B_r3,))
        S.op("dve", lambda h: h.tensor_copy(out=t_i, in_=t_b), r=(B_r3,), w=(B_r5,))
        S.op("dve", lambda h: h.tensor_copy(out=t_b, in_=t_i), r=(B_r5,), w=(B_r3,))
        S.op("dve", lambda h: h.scalar_tensor_tensor(out=t_a, in0=t_b, scalar=-2 * math.pi, in1=t_a, op0=ALU.mult,
                                                      op1=ALU.add), r=(B_r2, B_r3), w=(B_r2,))
        S.op("act", lambda h: h.activation(out=t_b, in_=t_a, func=AF.Sin, scale=0.25), r=(B_r2,), w=(B_r3,))
        S.op("dve", lambda h: h.tensor_scalar(out=t_c, in0=t_a, scalar1=0.25, scalar2=math.pi / 2, op0=ALU.mult,
                                               op1=ALU.add), r=(B_r2,), w=(B_r4,))
        S.op("act", lambda h: h.activation(out=t_c, in_=t_c, func=AF.Sin), r=(B_r4,), w=(B_r4,))
        S.op("dve", lambda h: h.scalar_tensor_tensor(out=t_a, in0=t_b, scalar=2.0, in1=t_c, op0=ALU.mult,
                                                      op1=ALU.mult), r=(B_r3, B_r4), w=(B_r2,))
        S.op("dve", lambda h: h.tensor_tensor(out=t_b, in0=t_b, in1=t_b, op=ALU.mult), r=(B_r3,), w=(B_r3,))
        S.op("dve", lambda h: h.tensor_scalar(out=t_b, in0=t_b, scalar1=-2.0, scalar2=1.0, op0=ALU.mult,
                                               op1=ALU.add), r=(B_r3,), w=(B_r3,))
        S.op("dve", lambda h: h.scalar_tensor_tensor(out=t_c, in0=t_a, scalar=2.0, in1=t_b, op0=ALU.mult,
                                                      op1=ALU.mult), r=(B_r2, B_r3), w=(B_r4,))
        S.op("dve", lambda h: h.tensor_tensor(out=rowf, in0=t_a, in1=t_a, op=ALU.mult), r=(B_r2,), w=(B_r0,))
        S.op("dve", lambda h: h.tensor_scalar(out=rowf, in0=rowf, scalar1=-2.0, scalar2=1.0, op0=ALU.mult,
                                               op1=ALU.add), r=(B_r0,), w=(B_r0,))
        S.op("dve", lambda h: h.tensor_scalar(out=t_c, in0=t_c, scalar1=RS, scalar2=None, op0=ALU.mult),
             r=(B_r4, B_small), w=(B_r4,))
        S.dma("sp", ropetab_d[0, :, :], rowf[64:96, :], r=(B_r0,), w=(B_rope,), chan=B_r0)
        S.dma("sp", ropetab_d[1, :, :], t_c[64:96, :], r=(B_r4,), w=(B_rope,), chan=B_r4)
        pro_bufs = [B_elb, B_slb, B_cT, B_biasr, B_r0, B_r1, B_r2, B_r3, B_r4, B_r5] + \
                   [b for _, b in adat] + [b for _, b in mrow]

        fence_arena(pro_bufs)
        ck(3)

        def bc3(ap2, n):
            g = ap2.shape[1]
            return ap2.rearrange("p (g o) -> p g o", o=1).broadcast_to([ap2.shape[0], g, n])

        def proj_fm(wv, Bw, kcn, rhs_fn, rbufs, ntok_groups, consumer, m=128):
            for (a0, a1) in ntok_groups:
                pg, B_pg = next_gen()
                n = a1 - a0
                for kc in range(kcn):
                    S.op("pe", lambda h, kc=kc: h.matmul(pg[0:m, 0:n], lhsT=wv[:, kc, 0:m], rhs=rhs_fn(kc, a0, a1),
                                                           start=(kc == 0), stop=(kc == kcn - 1)),
                         r=(Bw,) + tuple(rbufs), w=(B_pg,), mark=(kc == kcn - 1))
                consumer(a0, a1, pg, B_pg)

        def uT_rhs(kc, a0, a1):
            return uT[:, kc, a0:a1]

        def layer(b, l, last):
            r_b = b
            for ri, row in enumerate((r_b, 4)):
                S.dma("sp", modT[:, ri, :], modrows_d[l, row, 0:2048].rearrange("(c p) -> p c", p=128),
                      r=(B_modrows,), w=(B_mod,), chan=B_mod, allow_slow_non_contiguous=True)
                S.drain("sp", (B_mod,))
            for ri, row in enumerate((r_b, 4)):
                S.dma("sp", gate_bc[:, ri, :], modrows_d[l, row:row + 1, 2048:3072].partition_broadcast(128),
                      r=(B_modrows,), w=(B_gate,), chan=B_gate)
                S.drain("sp", (B_gate,))
            S.op("dve", lambda h: h.tensor_scalar(out=cscale[:], in0=modT[:, :, 8:16], scalar1=1.0, scalar2=None,
                                                   op0=ALU.add), r=(B_mod,), w=(B_mod,))
            ng = smallp[:, SM_NG + l * 8:SM_NG + (l + 1) * 8]
            for ri in range(2):
                S.op("dve", lambda h, ri=ri: h.tensor_tensor(out=cscale[:, ri, :], in0=cscale[:, ri, :], in1=ng,
                                                              op=ALU.mult), r=(B_mod, B_small), w=(B_mod,))

            ck(4)
            AR.reset()
            xs = [AR.f32(1024, f"xs{i}") for i in range(8)]
            junk, B_junk = AR.bf(1024, "junk")
            pa_bufs = [bb for _, bb in xs] + [B_junk]
            groups = [(0, 2, 1), (2, 6, 0), (6, 10, 0), (10, 14, 0), (14, 18, 0)]
            si = 0
            for (t0, t1, ri) in groups:
                tiles = []
                for t in range(t0, t1):
                    xt, B_xt = xs[si % 8]
                    si += 1
                    src = xin[b, t * 128:(t + 1) * 128, :] if l == 0 else hbuf_d[t * 128:(t + 1) * 128, :]
                    S.dma("sp", xt, src, r=(() if l == 0 else (B_hb[t],)), w=(B_xt,), chan=B_xt)
                    S.op("act", lambda h: h.activation(out=junk, in_=xt, func=AF.Square,
                                                        accum_out=stat[:, t:t + 1]),
                         r=(B_xt,), w=(B_junk, B_stat))
                    S.op("act", lambda h: h.activation(out=stat[:, t:t + 1], in_=stat[:, t:t + 1], func=AF.Sqrt,
                                                        scale=1.0 / D, bias=EPS), r=(B_stat,), w=(B_stat,))
                    S.op("dve", lambda h: h.reciprocal(out=stat[:, t:t + 1], in_=stat[:, t:t + 1]),
                         r=(B_stat,), w=(B_stat,))
                    S.op("dve", lambda h: h.tensor_scalar(out=xt, in0=xt, scalar1=stat[:, t:t + 1], scalar2=None,
                                                           op0=ALU.mult), r=(B_xt, B_stat), w=(B_xt,))
                    tiles.append((xt, B_xt))
                nt_ = t1 - t0
                for kc in range(8):
                    pg, B_pg = next_gen()
                    for ti, (xt, B_xt) in enumerate(tiles):
                        S.op("pe", lambda h, ti=ti, xt=xt: h.transpose(pg[:, ti * 128:(ti + 1) * 128],
                                                                         xt[:, kc * 128:(kc + 1) * 128], ident_f[:]),
                             r=(B_xt, B_const), w=(B_pg,), mark=(ti == nt_ - 1))
                    S.op("act", lambda h: h.activation(out=uT[:, kc, t0 * 128:t1 * 128], in_=pg[:, 0:nt_ * 128],
                                                        func=AF.Identity, scale=cscale[:, ri, kc:kc + 1],
                                                        bias=modT[:, ri, kc:kc + 1]),
                         r=(B_pg, B_mod), w=(B_uT,))
            fence_arena(pa_bufs + [B_stat])
            if dbg and b == 0 and l == 0:
                S.dma("sp", dbg_uT[:, :, :], uT[:], r=(B_uT,), w=(), chan=B_uT)
                S.drain("sp", (B_uT,))

            ck(5)
            AR.reset()
            cxs, B_cxs = AR.f32(NTOK, "cxs")
            uu, B_uu = AR.f32(NTOK, "uu")
            cacc, B_cacc = AR.f32(NTOK, "cacc")
            cbs, B_cbs = AR.f32(NTOK, "cbs")
            sgc, B_sgc = AR.f32(NTOK, "sgc")
            pb_bufs = [B_cxs, B_uu, B_cacc, B_cbs, B_sgc]
            for j in range(4):
                wv = [load_w(w_in_d[l, CH_CONV + 4 * j + q], 8, 128) for q in range(4)]

                def c_cx(a0, a1, pg, B_pg):
                    S.op("act", lambda h: h.copy(out=cxs[:, a0:a1], in_=pg[:, 0:a1 - a0]), r=(B_pg,), w=(B_cxs,))

                def c_cc(a0, a1, pg, B_pg):
                    S.op("dve", lambda h: h.tensor_tensor(out=uu[:, a0:a1], in0=pg[:, 0:a1 - a0], in1=cxs[:, a0:a1],
                                                           op=ALU.mult), r=(B_pg, B_cxs), w=(B_uu,))

                def c_cb(a0, a1, pg, B_pg):
                    S.op("act", lambda h: h.copy(out=cbs[:, a0:a1], in_=pg[:, 0:a1 - a0]), r=(B_pg,), w=(B_cbs,))

                def c_g(a0, a1, pg, B_pg):
                    S.op("act", lambda h: h.activation(out=sgc[:, a0:a1], in_=pg[:, 0:a1 - a0], func=AF.Silu),
                         r=(B_pg,), w=(B_sgc,))

                for q, cons in enumerate((c_cx, c_cc, c_cb, c_g)):
                    proj_fm(wv[q][0], wv[q][1], 8, uT_rhs, (B_uT,), TG, cons)
                cw = lambda k: smallp[:, SM_CW + l * 12 + k * 4 + j:SM_CW + l * 12 + k * 4 + j + 1]
                cbias = smallp[:, SM_CB + l * 4 + j:SM_CB + l * 4 + j + 1]
                S.op("dve", lambda h: h.tensor_scalar(out=cacc, in0=uu, scalar1=cw(1), scalar2=cbias, op0=ALU.mult,
                                                       op1=ALU.add), r=(B_uu, B_small), w=(B_cacc,))
                for (s0, s1) in ((0, NCTX), (NCTX, NTOK)):
                    S.op("dve", lambda h: h.scalar_tensor_tensor(out=cacc[:, s0 + 1:s1], in0=uu[:, s0:s1 - 1],
                                                                  scalar=cw(0), in1=cacc[:, s0 + 1:s1],
                                                                  op0=ALU.mult, op1=ALU.add),
                         r=(B_uu, B_small, B_cacc), w=(B_cacc,))
                    S.op("dve", lambda h: h.scalar_tensor_tensor(out=cacc[:, s0:s1 - 1], in0=uu[:, s0 + 1:s1],
                                                                  scalar=cw(2), in1=cacc[:, s0:s1 - 1],
                                                                  op0=ALU.mult, op1=ALU.add),
                         r=(B_uu, B_small, B_cacc), w=(B_cacc,))
                S.op("pool", lambda h: h.tensor_tensor(out=cacc, in0=cacc, in1=cbs, op=ALU.mult),
                     r=(B_cacc, B_cbs), w=(B_cacc,))
                S.op("pool", lambda h: h.tensor_tensor(out=yT[:, 8 + j, :], in0=cacc, in1=sgc, op=ALU.mult),
                     r=(B_cacc, B_sgc), w=(B_yT[2],))
            fence_arena(pb_bufs)

            ck(6)
            hgrn2(b, l)
            ck(7)
            mla(b, l)
            ck(8)
            if dbg and b == 0 and l == 0:
                S.dma("sp", dbg_yT[:, :, :], yT[:], r=tuple(B_yT), w=(), chan=B_yT[0])
                S.drain("sp", (B_yT[0],))
            merge_out(b, l, last)

        def hgrn2(b, l):
            AR.reset()
            qd, B_qd = AR.bf(NTOK, "qd")
            qz = [[AR.bf(NTOK, f"qz{i}{hh}") for hh in range(2)] for i in range(2)]
            kdz = [AR.bf(NTOK, f"kdz{hh}") for hh in range(2)]
            kz = [AR.bf(NT * 128, f"kz{i}") for i in range(2)]
            itz = [AR.bf(NT * 128, f"itz{i}") for i in range(2)]
            sgh, B_sgh = AR.bf(NTOK, "sgh")
            Sall, B_Sall = AR.bf(36 * 64, "Sall")
            oacc, B_oacc = AR.f32(NT * 128, "oacc")
            onb, B_onb = qd, B_qd
            Asb = [[AR.bf(128, f"A{d}{i}") for i in range(2)] for d in range(2)]
            Ub = [AR.f32(64, f"U{i}") for i in range(2)]
            hqs, B_hqs = AR.f32(512, "hqs")
            T1, B_T1 = AR.f32(512, "T1")
            T2, B_T2 = AR.f32(512, "T2")
            T3, B_T3 = AR.f32(512, "T3")
            T4, B_T4 = AR.f32(512, "T4")
            T5, B_T5 = AR.f32(512, "T5")
            refs, B_refs = AR.f32(36, "refs")
            lasts, B_lasts = AR.f32(36, "lasts")
            alph, B_alph = AR.f32(36, "alph")
            ssq, B_ssq = AR.f32(36, "ssq")
            allb = [B_qd, B_sgh, B_Sall, B_oacc, B_hqs, B_T1, B_T2, B_T3, B_T4, B_T5, B_refs,
                    B_lasts, B_alph, B_ssq] + [x[1] for x in kz] + [x[1] for x in kdz] + [x[1] for x in itz] + \
                   [x[1] for q_ in qz for x in q_] + [x[1] for d in Asb for x in d] + [x[1] for x in Ub]
            kz3 = [k[0].rearrange("p (t c) -> p t c", c=128) for k in kz]
            itz3 = [k[0].rearrange("p (t c) -> p t c", c=128) for k in itz]
            Sall3 = Sall.rearrange("p (c v) -> p c v", v=64)
            oacc3 = oacc.rearrange("p (t c) -> p t c", c=128)
            onb3 = onb.rearrange("p (t c) -> p t c", c=128)
            for i in range(2):
                for hh in range(2):
                    S.op("pool", lambda h, i=i, hh=hh: h.memset(qz[i][hh][0], 0.0), w=(qz[i][hh][1],))
                S.op("pool", lambda h, i=i: h.memset(kdz[i][0], 0.0), w=(kdz[i][1],))
                S.op("pool", lambda h, i=i: h.memset(itz[i][0], 0.0), w=(itz[i][1],))
                for d in range(2):
                    S.op("pool", lambda h, i=i, d=d: h.memset(Asb[d][i][0], 0.0), w=(Asb[d][i][1],))
            ck(10)
            for hp in range(4):
                w_q = load_w(w_in_d[l, CH_HG + 4 * hp + 0], 8, 128)
                w_f = [load_w(w_in_d[l, CH_HG + 4 * hp + 1 + d], 8, 128) for d in range(2)]
                w_g = load_w(w_in_d[l, CH_HG + 4 * hp + 3], 8, 128)
                w_i = load_w(w_in_d[l, CH_HI + hp], 8, 128)
                proj_fm(w_g[0], w_g[1], 8, uT_rhs, (B_uT,), TG,
                        lambda a0, a1, pg, B_pg: S.op("act", lambda h: h.activation(out=sgh[:, a0:a1],
                                                                                    in_=pg[:, 0:a1 - a0],
                                                                                    func=AF.Silu),
                                                     r=(B_pg,), w=(B_sgh,)))
                for t in range(NT):
                    pg, B_pg = next_gen()
                    for kc in range(8):
                        S.op("pe", lambda h, kc=kc: h.matmul(pg[:, 0:128], lhsT=uT[:, kc, t * 128:(t + 1) * 128],
                                                               rhs=w_i[0][:, kc, :], start=(kc == 0), stop=(kc == 7)),
                             r=(B_uT, w_i[1]), w=(B_pg,), mark=(kc == 7))
                    S.op("act", lambda h: h.copy(out=itz3[0][0:64, t, :], in_=pg[0:64, 0:128]), r=(B_pg,),
                         w=(itz[0][1],))
                    S.op("dve", lambda h: h.tensor_copy(out=itz3[1][64:128, t, :], in_=pg[64:128, 0:128]),
                         r=(B_pg,), w=(itz[1][1],))
                if hp == 0:
                    ck(11)
                lbv = lbt[:, l, :]
                for d in range(2):
                    lbc = lbt[:, l, d * 4 + hp:d * 4 + hp + 1]
                    omc = omlt[:, l, d * 4 + hp:d * 4 + hp + 1]
                    for (a0, a1) in TG:
                        n = a1 - a0
                        nch = n // 64
                        c0 = a0 // 64
                        pq, B_pq = next_gen()
                        for kc in range(8):
                            S.op("pe", lambda h, kc=kc: h.matmul(pq[:, 0:n], lhsT=w_q[0][:, kc, :],
                                                                   rhs=uT[:, kc, a0:a1], start=(kc == 0),
                                                                   stop=(kc == 7)),
                                 r=(w_q[1], B_uT), w=(B_pq,), mark=(kc == 7))
                        pf, B_pf = next_gen()
                        for kc in range(8):
                            S.op("pe", lambda h, kc=kc: h.matmul(pf[:, 0:n], lhsT=w_f[d][0][:, kc, :],
                                                                   rhs=uT[:, kc, a0:a1], start=(kc == 0),
                                                                   stop=(kc == 7)),
                                 r=(w_f[d][1], B_uT), w=(B_pf,), mark=(kc == 7))
                        S.op("act", lambda h: h.copy(out=hqs[:, 0:n], in_=pq[:, 0:n]), r=(B_pq,), w=(B_hqs,))
                        S.op("act", lambda h: h.activation(out=T1[:, 0:n], in_=pf[:, 0:n], func=AF.Sigmoid),
                             r=(B_pf,), w=(B_T1,))
                        S.op("dve", lambda h: h.tensor_scalar(out=T1[:, 0:n], in0=T1[:, 0:n], scalar1=omc,
                                                               scalar2=lbc, op0=ALU.mult, op1=ALU.add),
                             r=(B_T1, B_const), w=(B_T1,))
                        S.op("act", lambda h: h.activation(out=T2[:, 0:n], in_=T1[:, 0:n], func=AF.Ln),
                             r=(B_T1,), w=(B_T2,))
                        S.op("pool", lambda h: h.tensor_scalar(out=T1[:, 0:n], in0=T1[:, 0:n], scalar1=-1.0,
                                                                scalar2=1.0, op0=ALU.mult, op1=ALU.add),
                             r=(B_T1, B_T2), w=(B_T1,))
                        S.op("dve", lambda h: h.tensor_tensor_scan(out=T3[:, 0:n], data0=resetm[:, 0:n],
                                                                    data1=T2[:, 0:n], initial=0.0, op0=ALU.mult,
                                                                    op1=ALU.add), r=(B_T2, B_const), w=(B_T3,))
                        T3v = T3[:, 0:n].rearrange("p (c t) -> p c t", t=64)
                        T2v = T2[:, 0:n].rearrange("p (c t) -> p c t", t=64)
                        if d == 0:
                            cum, B_cum, cumv = T3, B_T3, T3v
                            iref, ilast = 31, 63
                        else:
                            S.op("dve", lambda h: h.tensor_tensor(out=T2[:, 0:n], in0=T2[:, 0:n], in1=T3[:, 0:n],
                                                                   op=ALU.subtract), r=(B_T2, B_T3), w=(B_T2,))
                            S.op("dve", lambda h: h.tensor_tensor(out=T2v, in0=T2v,
                                                                   in1=T3v[:, :, 63:64].broadcast_to([128, nch, 64]),
                                                                   op=ALU.add), r=(B_T2, B_T3), w=(B_T2,))
                            cum, B_cum, cumv = T2, B_T2, T2v
                            iref, ilast = 32, 0
                        S.op("dve", lambda h: h.tensor_copy(out=refs[:, c0:c0 + nch], in_=cumv[:, :, iref]),
                             r=(B_cum,), w=(B_refs,))
                        S.op("dve", lambda h: h.tensor_tensor(out=cumv, in0=cumv,
                                                               in1=bc3(refs[:, c0:c0 + nch], 64), op=ALU.subtract),
                             r=(B_cum, B_refs), w=(B_cum,))
                        S.op("dve", lambda h: h.tensor_copy(out=lasts[:, c0:c0 + nch], in_=cumv[:, :, ilast]),
                             r=(B_cum,), w=(B_lasts,))
                        S.op("act", lambda h: h.activation(out=T4[:, 0:n], in_=cum[:, 0:n], func=AF.Exp),
                             r=(B_cum,), w=(B_T4,))
                        S.op("act", lambda h: h.activation(out=T5[:, 0:n], in_=cum[:, 0:n], func=AF.Exp,
                                                            scale=-1.0), r=(B_cum,), w=(B_T5,))
                        S.op("dve", lambda h: h.tensor_tensor(out=qd[:, a0:a1], in0=hqs[:, 0:n], in1=T4[:, 0:n],
                                                               op=ALU.mult), r=(B_hqs, B_T4), w=(B_qd,))
                        hv = hqs[:, 0:n].rearrange("p (c two t) -> p c two t", two=2, t=64)
                        ev = T4[:, 0:n].rearrange("p (c two t) -> p c two t", two=2, t=64)
                        for i in range(2):
                            for hh in range(2):
                                rw = slice(hh * 64, (hh + 1) * 64)
                                qv = qz[i][hh][0][:, a0:a1].rearrange("p (c two t) -> p c two t", two=2, t=64)
                                S.op("pool", lambda h, i=i, qv=qv, rw=rw: h.tensor_tensor(
                                    out=qv[rw, :, i, :], in0=hv[rw, :, i, :], in1=ev[rw, :, i, :], op=ALU.mult),
                                     r=(B_hqs, B_T4), w=(qz[i][hh][1],))
                        for hh in range(2):
                            rw = slice(hh * 64, (hh + 1) * 64)
                            S.op("dve", lambda h, rw=rw, hh=hh: h.tensor_tensor(out=kdz[hh][0][rw, a0:a1],
                                                                                 in0=T1[rw, 0:n], in1=T5[rw, 0:n],
                                                                                 op=ALU.mult),
                                 r=(B_T1, B_T5), w=(kdz[hh][1],))
                    if hp == 0 and d == 0:
                        ck(12)
                    if d == 0:
                        S.op("dve", lambda h: h.tensor_tensor(out=alph[:, 0:35], in0=lasts[:, 0:35],
                                                               in1=refs[:, 1:36], op=ALU.add),
                             r=(B_lasts, B_refs), w=(B_alph,))
                        order = list(range(36))
                        prev_of = {c: c - 1 for c in range(1, 36)}
                    else:
                        S.op("dve", lambda h: h.tensor_tensor(out=alph[:, 1:36], in0=lasts[:, 1:36],
                                                               in1=refs[:, 0:35], op=ALU.add),
                             r=(B_lasts, B_refs), w=(B_alph,))
                        S.op("dve", lambda h: h.tensor_tensor(out=alph[:, 0:1], in0=lasts[:, 0:1],
                                                               in1=refs[:, 35:36], op=ALU.add),
                             r=(B_lasts, B_refs), w=(B_alph,))
                        order = [3, 2, 1, 0] + list(range(35, 3, -1))
                        prev_of = {order[i]: order[i - 1] for i in range(1, 36)}
                    na_ = 35 if d == 0 else 36
                    S.op("act", lambda h: h.activation(out=alph[:, 0:na_], in_=alph[:, 0:na_], func=AF.Exp),
                         r=(B_alph,), w=(B_alph,))
                    if hp == 0 and d == 0:
                        ck(13)
                    for t in range(NT):
                        for hh in range(2):
                            pt, B_pt = next_pbf()
                            S.op("pe", lambda h: h.transpose(pt[:, 0:128], kdz[hh][0][:, t * 128:(t + 1) * 128],
                                                              ident_b[:]), r=(kdz[hh][1], B_const), w=(B_pt,))
                            if hh == 0:
                                S.op("act", lambda h: h.copy(out=kz3[0][:, t, :], in_=pt[:, 0:128]), r=(B_pt,),
                                     w=(kz[0][1],))
                            else:
                                S.op("dve", lambda h: h.tensor_copy(out=kz3[1][:, t, :], in_=pt[:, 0:128]),
                                     r=(B_pt,), w=(kz[1][1],))
                    if hp == 0 and d == 0:
                        ck(14)
                    first = order[0]
                    import os as _os
                    if not _os.environ.get("DBG_NO_MEMSET"):
                        S.op("dve", lambda h: h.memset(Sall3[:, first, :], 0.0), w=(B_Sall,))
                    Pregs = {}
                    for t in (range(NT) if d == 0 else [1, 0] + list(range(NT - 1, 1, -1))):
                        pa, B_pa = next_gen() if _os.environ.get("DBG_P_GEN") else (next_acc() if _os.environ.get("DBG_P_ACC") else next_aux())
                        for cc in range(2):
                            c = 2 * t + cc
                            Pr = pa[:, cc * 64:(cc + 1) * 64]
                            if _os.environ.get("DBG_NO_P"):
                                continue
                            if _os.environ.get("DBG_P_CC0") and cc == 1:
                                continue
                            if _os.environ.get("DBG_P_CC1") and cc == 0:
                                continue
                            S.op("pe", lambda h: h.matmul(Pr, lhsT=kz3[0][:, t, :], rhs=itz3[cc][:, t, 0:64],
                                                           start=True, stop=False),
                                 r=(kz[0][1], itz[cc][1]), w=(B_pa,), mark=False)
                            S.op("pe", lambda h: h.matmul(Pr, lhsT=kz3[1][:, t, :], rhs=itz3[cc][:, t, 64:128],
                                                           start=False, stop=True),
                                 r=(kz[1][1], itz[cc][1]), w=(B_pa,), mark=(cc == 1))
                            Pregs[c] = (Pr, B_pa)
                        import os as _os
                        for c in ((2 * t, 2 * t + 1) if d == 0 else (2 * t + 1, 2 * t)):
                            if _os.environ.get("DBG_SKIP_CHAIN"):
                                continue
                            Pr, B_pr = Pregs[c]
                            if c == first:
                                U, B_U = Ub[0]
                                S.op("dve", lambda h: h.tensor_copy(out=U, in_=Pr), r=(B_pr,), w=(B_U,))
                                ui = 0
                            else:
                                p = prev_of[c]
                                U, B_U = Ub[ui]
                                Un, B_Un = Ub[1 - ui]
                                S.op("dve", lambda h: h.tensor_scalar(out=Sall3[:, c, :], in0=U,
                                                                       scalar1=alph[:, p:p + 1], scalar2=None,
                                                                       op0=ALU.mult),
                                     r=(B_U, B_alph), w=(B_Sall,))
                                S.op("dve", lambda h: h.scalar_tensor_tensor(out=Un, in0=U, scalar=alph[:, p:p + 1],
                                                                              in1=Pr, op0=ALU.mult, op1=ALU.add),
                                     r=(B_U, B_alph, B_pr), w=(B_Un,))
                                ui = 1 - ui
                        if hp == 0 and d == 0 and t < 3:
                            ck(150 + t)
                    if hp == 0 and d == 0:
                        ck(15)
                    for t in range(NT):
                        po, B_po = next_acc()
                        tl = slice(t * 128, (t + 1) * 128)
                        pas = []
                        for hh in range(2):
                            pa, B_pa = next_aux()
                            S.op("pe", lambda h: h.matmul(pa[:, 0:128], lhsT=kdz[hh][0][:, tl], rhs=qd[:, tl],
                                                           start=True, stop=True), r=(kdz[hh][1], B_qd), w=(B_pa,))
                            pas.append((pa, B_pa))
                        for hh in range(2):
                            pa, B_pa = pas[hh]
                            A, B_A = Asb[d][hh]
                            S.op("dve", lambda h: h.copy_predicated(out=A, mask=masks[:, d, :], data=pa[:, 0:128]),
                                 r=(B_pa, B_const), w=(B_A,))
                        for hh in range(2):
                            A, B_A = Asb[d][hh]
                            oreg = po[:, hh * 64:(hh + 1) * 64]
                            hc = slice(hh * 64, (hh + 1) * 64)
                            S.op("pe", lambda h: h.matmul(oreg, lhsT=A, rhs=itz3[0][:, t, hc], start=True,
                                                           stop=False), r=(B_A, itz[0][1]), w=(B_po,), mark=False)
                            S.op("pe", lambda h: h.matmul(oreg, lhsT=A, rhs=itz3[1][:, t, hc], start=False,
                                                           stop=False), r=(B_A, itz[1][1]), w=(B_po,), mark=False)
                            S.op("pe", lambda h: h.matmul(oreg, lhsT=qz[0][hh][0][:, tl], rhs=Sall3[:, 2 * t, :],
                                                           start=False, stop=False),
                                 r=(qz[0][hh][1], B_Sall), w=(B_po,), mark=False)
                            S.op("pe", lambda h: h.matmul(oreg, lhsT=qz[1][hh][0][:, tl],
                                                           rhs=Sall3[:, 2 * t + 1, :], start=False, stop=True),
                                 r=(qz[1][hh][1], B_Sall), w=(B_po,), mark=(hh == 1))
                        if d == 0:
                            S.op("act", lambda h: h.copy(out=oacc3[:, t, :], in_=po[:, 0:128]), r=(B_po,),
                                 w=(B_oacc,))
                        else:
                            S.op("dve", lambda h: h.tensor_tensor(out=oacc3[:, t, :], in0=po[:, 0:128],
                                                                   in1=oacc3[:, t, :], op=ALU.add),
                                 r=(B_po, B_oacc), w=(B_oacc,))
                    if dbg and hp == 1 and b == 0 and l == 0:
                        S.dma("sp", dbg_of[d], oacc, r=(B_oacc,), w=(), chan=B_oacc)
                        S.drain("sp", (B_oacc,))
                        S.dma("sp", dbg_S[d], Sall, r=(B_Sall,), w=(), chan=B_Sall)
                        S.drain("sp", (B_Sall,))
                        for qi_, (qa, qb) in enumerate(((refs, B_refs), (lasts, B_lasts), (alph, B_alph))):
                            S.dma("sp", dbg_st[d, qi_], qa, r=(qb,), w=(), chan=qb)
                            S.drain("sp", (qb,))
                        for qi_, (qa, qb) in enumerate(((qd, B_qd),)):
                            S.dma("sp", dbg_q[d, qi_], qa, r=(qb,), w=(), chan=qb)
                            S.drain("sp", (qb,))
                if hp == 0:
                    ck(16)
                S.op("pool", lambda h: h.tensor_tensor(out=onb, in0=oacc, in1=oacc, op=ALU.mult),
                     r=(B_oacc,), w=(B_onb,))
                S.op("dve", lambda h: h.tensor_reduce(out=ssq, in_=onb.rearrange("p (g v) -> p g v", v=64),
                                                       axis=AX.X, op=ALU.add), r=(B_onb,), w=(B_ssq,))
                S.op("act", lambda h: h.activation(out=ssq, in_=ssq, func=AF.Sqrt, scale=1.0 / 64, bias=EPS),
                     r=(B_ssq,), w=(B_ssq,))
                S.op("dve", lambda h: h.reciprocal(out=ssq, in_=ssq), r=(B_ssq,), w=(B_ssq,))
                S.op("dve", lambda h: h.tensor_tensor(out=onb.rearrange("p (g v) -> p g v", v=64),
                                                       in0=oacc.rearrange("p (g v) -> p g v", v=64),
                                                       in1=bc3(ssq, 64),
                                                       op=ALU.mult), r=(B_oacc, B_ssq, B_onb), w=(B_onb,))
                gcol = smallp[:, SM_HGG + l * 4 + hp:SM_HGG + l * 4 + hp + 1]
                for t4 in range(0, NT, 4):
                    nt_ = min(4, NT - t4)
                    pt, B_pt = next_pbf()
                    for ti in range(nt_):
                        S.op("pe", lambda h, ti=ti: h.transpose(pt[:, ti * 128:(ti + 1) * 128], onb3[:, t4 + ti, :],
                                                                  ident_b[:]), r=(B_onb, B_const), w=(B_pt,),
                             mark=(ti == nt_ - 1))
                    S.op("dve", lambda h: h.scalar_tensor_tensor(out=yT[:, 4 + hp, t4 * 128:(t4 + nt_) * 128],
                                                                  in0=pt[:, 0:nt_ * 128], scalar=gcol,
                                                                  in1=sgh[:, t4 * 128:(t4 + nt_) * 128],
                                                                  op0=ALU.mult, op1=ALU.mult),
                         r=(B_pt, B_small, B_sgh), w=(B_yT[1],))
            fence_arena(allb)

        def mla(b, l):
            AR.reset()
            kT = [AR.bf(NTOK, f"kT{h_}") for h_ in range(4)]
            krope, B_krope = AR.bf(NTOK, "krope")
            Vaug, B_V = AR.bf(NT * 4 * 65, "Vaug")
            cqn, B_cqn = AR.bf(2 * NTOK, "cqn")
            ckvn, B_ckvn = AR.bf(NTOK, "ckvn")
            qT = [AR.bf(512, f"qT{h_}") for h_ in range(4)]
            wuq, B_wuq = AR.bf(16 * 2 * 96, "wuq")
            wkn, B_wkn = AR.bf(8 * 64, "wkn")
            wvv, B_wvv = AR.bf(512, "wvv")
            sgm, B_sgm = AR.bf(2 * 512, "sgm")
            osb, B_osb = AR.bf(4 * 256, "osb")
            pT = [AR.bf(512, f"pT{i}") for i in range(3)]
            rope, B_ropeS = AR.f32(2 * 512, "rope")
            cqs, B_cqs = AR.f32(2 * 512, "cqs")
            sqs, B_sqs = AR.f32(2 * 512, "sqs")
            rst, B_rst = AR.f32(512, "rst")
            tr1, B_tr1 = AR.f32(512, "tr1")
            tr2, B_tr2 = AR.f32(512, "tr2")
            rec, B_rec = AR.f32(4, "rec")
            allb = [x[1] for x in kT] + [x[1] for x in qT] + [x[1] for x in pT] + \
                   [B_krope, B_V, B_cqn, B_ckvn, B_wuq, B_wkn, B_wvv, B_sgm, B_osb, B_ropeS, B_cqs, B_sqs, B_rst,
                    B_tr1, B_tr2, B_rec]
            V4 = Vaug.rearrange("p (t h c) -> p t h c", h=4, c=65)
            V3 = Vaug.rearrange("p (g c) -> p g c", c=65)
            cqn3 = cqn.rearrange("p (k n) -> p k n", n=NTOK)
            wuq4 = wuq.rearrange("p (g k c) -> p g k c", k=2, c=96)
            wkn3 = wkn.rearrange("p (h c) -> p h c", c=64)
            sgm3 = sgm.rearrange("p (j n) -> p j n", n=512)
            osb3 = osb.rearrange("p (q c) -> p q c", c=256)
            rope3 = rope.rearrange("p (a n) -> p a n", n=512)
            cqs3 = cqs.rearrange("p (k n) -> p k n", n=512)
            sqs3 = sqs.rearrange("p (k n) -> p k n", n=512)
            S.dma("pool", wuq4, w_uq_d[l].rearrange("g p k c -> p g k c"), w=(B_wuq,), chan=B_wuq)
            S.dma("pool", wkn3, w_kn_d[l], w=(B_wkn,), chan=B_wkn)
            S.dma("pool", wvv, w_v_d[l], w=(B_wvv,), chan=B_wvv)
            S.op("pool", lambda h: h.memset(V3[:, :, 64:65], 1.0), w=(B_V,))
            qg_ = smallp[:, SM_QG + l * 2:SM_QG + l * 2 + 2]
            kvg_ = smallp[:, SM_KVG + l:SM_KVG + l + 1]

            def load_rope(n0, n1, off):
                for a in range(2):
                    S.dma("sp", rope3[64:96, a, off:off + (n1 - n0)], ropetab_d[a, :, n0:n1], r=(B_rope,),
                          w=(B_ropeS,), chan=B_ropeS)
                    S.drain("sp", (B_ropeS,))

            def rope_rows(dst, B_dst, pA, B_pA, pB, B_pB, c0, c1, off):
                n = c1 - c0
                S.op("dve", lambda h: h.tensor_tensor(out=tr1[64:96, 0:n], in0=pA[64:96, c0:c1],
                                                       in1=rope3[64:96, 0, off:off + n], op=ALU.mult),
                     r=(B_pA, B_ropeS), w=(B_tr1,))
                S.op("dve", lambda h: h.tensor_tensor(out=tr2[64:96, 0:n], in0=pB[64:96, c0:c1],
                                                       in1=rope3[64:96, 1, off:off + n], op=ALU.mult),
                     r=(B_pB, B_ropeS), w=(B_tr2,))
                S.op("pool", lambda h: h.tensor_tensor(out=dst, in0=tr1[64:96, 0:n], in1=tr2[64:96, 0:n],
                                                        op=ALU.add), r=(B_tr1, B_tr2), w=(B_dst,))

            w_cq = [load_w(w_in_d[l, CH_CQ + j], 8, 128) for j in range(2)]
            w_ckv = load_w(w_in_d[l, CH_CKV], 8, 128)
            w_kra = load_w(w_in_d[l, CH_KRA], 8, 128)
            w_krb = load_w(w_in_d[l, CH_KRB], 8, 128)

            def norm_fm(nk, src3, dst_fn, gcols, a0, a1):
                n = a1 - a0
                for j in range(nk):
                    S.op("pool", lambda h, j=j: h.tensor_tensor(out=sqs3[:, j, 0:n], in0=src3[:, j, 0:n],
                                                                 in1=src3[:, j, 0:n], op=ALU.mult),
                         r=(B_cqs,), w=(B_sqs,))
                pss, B_pss = next_aux()
                for j in range(nk):
                    S.op("pe", lambda h, j=j: h.matmul(pss[:, 0:n], lhsT=ones_f[:], rhs=sqs3[:, j, 0:n],
                                                         start=(j == 0), stop=(j == nk - 1)),
                         r=(B_const, B_sqs), w=(B_pss,), mark=(j == nk - 1))
                S.op("act", lambda h: h.activation(out=rst[:, 0:n], in_=pss[:, 0:n], func=AF.Sqrt,
                                                    scale=1.0 / (128 * nk), bias=EPS), r=(B_pss,), w=(B_rst,))
                S.op("dve", lambda h: h.reciprocal(out=rst[:, 0:n], in_=rst[:, 0:n]), r=(B_rst,), w=(B_rst,))
                for j in range(nk):
                    dst, B_dst = dst_fn(j)
                    S.op("dve", lambda h, j=j, dst=dst: h.scalar_tensor_tensor(out=dst, in0=src3[:, j, 0:n],
                                                                                 scalar=gcols[:, j:j + 1],
                                                                                 in1=rst[:, 0:n], op0=ALU.mult,
                                                                                 op1=ALU.mult),
                         r=(B_cqs, B_rst, B_small), w=(B_dst,))

            for (a0, a1) in TG:
                n = a1 - a0
                for j in range(2):
                    proj_fm(w_cq[j][0], w_cq[j][1], 8, uT_rhs, (B_uT,), [(a0, a1)],
                            lambda x0, x1, pg, B_pg, j=j: S.op("act", lambda h: h.copy(out=cqs3[:, j, 0:n],
                                                                                        in_=pg[:, 0:n]),
                                                               r=(B_pg,), w=(B_cqs,)))
                norm_fm(2, cqs3, lambda j: (cqn3[:, j, a0:a1], B_cqn), qg_, a0, a1)
                proj_fm(w_ckv[0], w_ckv[1], 8, uT_rhs, (B_uT,), [(a0, a1)],
                        lambda x0, x1, pg, B_pg: S.op("act", lambda h: h.copy(out=cqs3[:, 0, 0:n], in_=pg[:, 0:n]),
                                                     r=(B_pg,), w=(B_cqs,)))
                norm_fm(1, cqs3, lambda j: (ckvn[:, a0:a1], B_ckvn), kvg_, a0, a1)
                pA, B_pA = next_gen()
                for kc in range(8):
                    S.op("pe", lambda h, kc=kc: h.matmul(pA[:, 0:n], lhsT=w_kra[0][:, kc, :], rhs=uT[:, kc, a0:a1],
                                                           start=(kc == 0), stop=(kc == 7)),
                         r=(w_kra[1], B_uT), w=(B_pA,), mark=(kc == 7))
                lat0 = max(a0, NCTX)
                if a0 < NCTX:
                    S.op("act", lambda h: h.copy(out=krope[64:96, a0:NCTX], in_=pA[64:96, 0:NCTX - a0]),
                         r=(B_pA,), w=(B_krope,))
                pB, B_pB = next_gen()
                for kc in range(8):
                    S.op("pe", lambda h, kc=kc: h.matmul(pB[:, 0:n], lhsT=w_krb[0][:, kc, :], rhs=uT[:, kc, a0:a1],
                                                           start=(kc == 0), stop=(kc == 7)),
                         r=(w_krb[1], B_uT), w=(B_pB,), mark=(kc == 7))
                load_rope(lat0 - NCTX, a1 - NCTX, 0)
                rope_rows(krope[64:96, lat0:a1], B_krope, pA, B_pA, pB, B_pB, lat0 - a0, a1 - a0, 0)

            for hg_ in range(2):
                for hl in range(4):
                    h_ = hg_ * 4 + hl
                    S.op("pool", lambda h: h.tensor_copy(out=kT[hl][0][64:96, :], in_=krope[64:96, :]),
                         r=(B_krope,), w=(kT[hl][1],))
                    for gi, (a0, a1) in enumerate(TG):
                        n = a1 - a0
                        pg, B_pg = next_gen()
                        S.op("pe", lambda h: h.matmul(pg[0:64, 0:n], lhsT=wkn3[:, h_, :], rhs=ckvn[:, a0:a1],
                                                       start=True, stop=True), r=(B_wkn, B_ckvn), w=(B_pg,))
                        if (hl + gi) % 2 == 0:
                            S.op("act", lambda h: h.copy(out=kT[hl][0][0:64, a0:a1], in_=pg[0:64, 0:n]), r=(B_pg,),
                                 w=(kT[hl][1],))
                        else:
                            S.op("dve", lambda h: h.tensor_copy(out=kT[hl][0][0:64, a0:a1], in_=pg[0:64, 0:n]),
                                 r=(B_pg,), w=(kT[hl][1],))
                for t in range(NT):
                    pg, B_pg = next_gen()
                    S.op("pe", lambda h: h.matmul(pg[:, 0:256], lhsT=ckvn[:, t * 128:(t + 1) * 128],
                                                   rhs=wvv[:, hg_ * 256:(hg_ + 1) * 256], start=True, stop=True),
                         r=(B_ckvn, B_wvv), w=(B_pg,))
                    pv = pg[:, 0:256].rearrange("p (h c) -> p h c", c=64)
                    if t % 2 == 0:
                        S.op("act", lambda h: h.copy(out=V4[:, t, :, 0:64], in_=pv), r=(B_pg,), w=(B_V,))
                    else:
                        S.op("dve", lambda h: h.tensor_copy(out=V4[:, t, :, 0:64], in_=pv), r=(B_pg,), w=(B_V,))
                for qi_, (q0, q1) in enumerate(QG):
                    nq = q1 - q0
                    nqt = nq // 128
                    isctx = (qi_ == 0)
                    kl = list(range(2)) if isctx else list(range(NT))
                    if not isctx:
                        load_rope(q0 - NCTX, q1 - NCTX, 0)
                    for jl in range(2):
                        w_g = load_w(w_in_d[l, CH_GMLA + hg_ * 2 + jl], 8, 128)
                        pg, B_pg = next_gen()
                        for kc in range(8):
                            S.op("pe", lambda h, kc=kc: h.matmul(pg[:, 0:nq], lhsT=w_g[0][:, kc, :],
                                                                   rhs=uT[:, kc, q0:q1], start=(kc == 0),
                                                                   stop=(kc == 7)),
                                 r=(w_g[1], B_uT), w=(B_pg,), mark=(kc == 7))
                        S.op("act", lambda h: h.activation(out=sgm3[:, jl, 0:nq], in_=pg[:, 0:nq], func=AF.Silu),
                             r=(B_pg,), w=(B_sgm,))
                    for hl in range(4):
                        h_ = hg_ * 4 + hl
                        pq, B_pq = next_gen()
                        for kc in range(2):
                            S.op("pe", lambda h, kc=kc: h.matmul(pq[0:96, 0:nq], lhsT=wuq4[:, h_, kc, :],
                                                                   rhs=cqn3[:, kc, q0:q1], start=(kc == 0),
                                                                   stop=(kc == 1)),
                                 r=(B_wuq, B_cqn), w=(B_pq,), mark=(kc == 1))
                        if isctx:
                            S.op("act", lambda h: h.copy(out=qT[hl][0][0:96, 0:nq], in_=pq[0:96, 0:nq]), r=(B_pq,),
                                 w=(qT[hl][1],))
                        else:
                            pq2, B_pq2 = next_gen()
                            for kc in range(2):
                                S.op("pe", lambda h, kc=kc: h.matmul(pq2[0:96, 0:nq], lhsT=wuq4[:, 8 + h_, kc, :],
                                                                       rhs=cqn3[:, kc, q0:q1], start=(kc == 0),
                                                                       stop=(kc == 1)),
                                     r=(B_wuq, B_cqn), w=(B_pq2,), mark=(kc == 1))
                            S.op("act", lambda h: h.copy(out=qT[hl][0][0:64, 0:nq], in_=pq[0:64, 0:nq]), r=(B_pq,),
                                 w=(qT[hl][1],))
                            rope_rows(qT[hl][0][64:96, 0:nq], qT[hl][1], pq, B_pq, pq2, B_pq2, 0, nq, 0)
                    pi_ = 0
                    for hl in range(4):
                        po, B_po = next_acc()
                        po3 = po[:, 0:nqt * 65].rearrange("p (q c) -> p q c", c=65)
                        for ki, kt in enumerate(kl):
                            pa, B_pa = next_aux()
                            S.op("pe", lambda h: h.matmul(pa[:, 0:nq], lhsT=kT[hl][0][0:96, kt * 128:(kt + 1) * 128],
                                                           rhs=qT[hl][0][0:96, 0:nq], start=True, stop=True),
                                 r=(kT[hl][1], qT[hl][1]), w=(B_pa,))
                            pt_, B_pt = pT[pi_ % 3]
                            pi_ += 1
                            S.op("act", lambda h: h.activation(out=pt_[:, 0:nq], in_=pa[:, 0:nq], func=AF.Exp,
                                                                scale=MLA_SCALE), r=(B_pa,), w=(B_pt,))
                            for qq in range(nqt):
                                S.op("pe", lambda h, qq=qq: h.matmul(po3[:, qq, :],
                                                                       lhsT=pt_[:, qq * 128:(qq + 1) * 128],
                                                                       rhs=V4[:, kt, hl, :],
                                                                       start=(ki == 0 and qq == 0),
                                                                       stop=(ki == len(kl) - 1),
                                                                       skip_group_check=True),
                                     r=(B_pt, B_V), w=(B_po,), mark=(qq == nqt - 1))
                        S.op("dve", lambda h: h.reciprocal(out=rec[:, 0:nqt], in_=po3[:, :, 64]), r=(B_po,),
                             w=(B_rec,))
                        S.op("dve", lambda h: h.tensor_tensor(out=osb3[:, 0:nqt, hl * 64:(hl + 1) * 64],
                                                               in0=po3[:, :, 0:64], in1=bc3(rec[:, 0:nqt], 64),
                                                               op=ALU.mult), r=(B_po, B_rec), w=(B_osb,))
                    for jl in range(2):
                        pt, B_pt = next_pbf()
                        for qq in range(nqt):
                            S.op("pe", lambda h, qq=qq: h.transpose(pt[:, qq * 128:(qq + 1) * 128],
                                                                      osb3[:, qq, jl * 128:(jl + 1) * 128],
                                                                      ident_b[:]),
                                 r=(B_osb, B_const), w=(B_pt,), mark=(qq == nqt - 1))
                        S.op("dve", lambda h: h.tensor_tensor(out=yT[:, hg_ * 2 + jl, q0:q1], in0=pt[:, 0:nq],
                                                               in1=sgm3[:, jl, 0:nq], op=ALU.mult),
                             r=(B_pt, B_sgm), w=(B_yT[0],))
            fence_arena(allb)

        def merge_out(b, l, last):
            AR.reset()
            mT, B_mT = AR.bf(8 * NTOK, "mT")
            wo, B_wo = AR.bf(8 * 1024, "wo")
            sgt = [AR.f32(512, f"sg{i}") for i in range(3)]
            macc = [AR.f32(512, f"macc{i}") for i in range(2)]
            tmpm = [AR.f32(512, f"tmpm{i}") for i in range(2)]
            hts = [AR.f32(1024, f"ht{i}") for i in range(3)]
            tmo = [AR.f32(512, f"tmo{i}") for i in range(2)]
            junk, B_junk = AR.bf(1024, "junkm")
            allb = [B_mT, B_wo, B_junk] + [x[1] for x in sgt + macc + tmpm + hts + tmo]
            mT3 = mT.rearrange("p (k n) -> p k n", n=NTOK)
            wo3 = wo.rearrange("p (k c) -> p k c", c=1024)
            S.dma("pool", wo3, w_out_d[l], w=(B_wo,), chan=B_wo)
            si = 0
            for fc in range(8):
                wg = [load_w(w_in_d[l, CH_GATE + 3 * fc + br], 8, 128) for br in range(3)]
                wb = [load_w(w_br_d[l, 3 * fc + br], 4, 128) for br in range(3)]
                for gi, (a0, a1) in enumerate(TG):
                    n = a1 - a0
                    ma, B_ma = macc[gi % 2]
                    for br in range(3):
                        pg, B_pg = next_gen()
                        for kc in range(8):
                            S.op("pe", lambda h, kc=kc: h.matmul(pg[:, 0:n], lhsT=wg[br][0][:, kc, :],
                                                                   rhs=uT[:, kc, a0:a1], start=(kc == 0),
                                                                   stop=(kc == 7)),
                                 r=(wg[br][1], B_uT), w=(B_pg,), mark=(kc == 7))
                        sg, B_sg = sgt[si % 3]
                        si += 1
                        S.op("act", lambda h: h.activation(out=sg[:, 0:n], in_=pg[:, 0:n], func=AF.Sigmoid),
                             r=(B_pg,), w=(B_sg,))
                        pb_, B_pb = next_gen()
                        for kc in range(4):
                            S.op("pe", lambda h, kc=kc: h.matmul(pb_[:, 0:n], lhsT=wb[br][0][:, kc, :],
                                                                   rhs=yT[:, br * 4 + kc, a0:a1], start=(kc == 0),
                                                                   stop=(kc == 3)),
                                 r=(wb[br][1], B_yT[br]), w=(B_pb,), mark=(kc == 3))
                        if br == 0:
                            S.op("dve", lambda h: h.tensor_tensor(out=ma[:, 0:n], in0=pb_[:, 0:n], in1=sg[:, 0:n],
                                                                   op=ALU.mult), r=(B_pb, B_sg), w=(B_ma,))
                        else:
                            tm, B_tm = tmpm[br - 1]
                            S.op("dve", lambda h: h.tensor_tensor(out=tm[:, 0:n], in0=pb_[:, 0:n], in1=sg[:, 0:n],
                                                                   op=ALU.mult), r=(B_pb, B_sg), w=(B_tm,))
                            if br == 1:
                                S.op("pool", lambda h: h.tensor_tensor(out=ma[:, 0:n], in0=ma[:, 0:n],
                                                                        in1=tm[:, 0:n], op=ALU.add),
                                     r=(B_ma, B_tm), w=(B_ma,))
                            else:
                                S.op("pool", lambda h: h.tensor_tensor(out=mT3[:, fc, a0:a1], in0=ma[:, 0:n],
                                                                        in1=tm[:, 0:n], op=ALU.add),
                                     r=(B_ma, B_tm), w=(B_mT,))
            for t in range(NT):
                ri = 1 if t < 2 else 0
                if last and t < 2:
                    continue
                ht, B_ht = hts[t % 3]
                src = xin[b, t * 128:(t + 1) * 128, :] if l == 0 else hbuf_d[t * 128:(t + 1) * 128, :]
                S.dma("sp", ht, src, r=(() if l == 0 else (B_hb[t],)), w=(B_ht,), chan=B_ht)
                for half in range(2):
                    pg, B_pg = next_gen()
                    for kc in range(8):
                        S.op("pe", lambda h, kc=kc: h.matmul(pg[:, :], lhsT=mT3[:, kc, t * 128:(t + 1) * 128],
                                                               rhs=wo3[:, kc, half * 512:(half + 1) * 512],
                                                               start=(kc == 0), stop=(kc == 7)),
                             r=(B_mT, B_wo), w=(B_pg,), mark=(kc == 7))
                    to, B_to = tmo[half]
                    S.op("dve", lambda h: h.tensor_tensor(out=to, in0=pg[:, :],
                                                           in1=gate_bc[:, ri, half * 512:(half + 1) * 512],
                                                           op=ALU.mult), r=(B_pg, B_gate), w=(B_to,))
                    S.op("pool", lambda h: h.tensor_tensor(out=ht[:, half * 512:(half + 1) * 512],
                                                            in0=ht[:, half * 512:(half + 1) * 512], in1=to,
                                                            op=ALU.add), r=(B_ht, B_to), w=(B_ht,))
                if not last:
                    S.dma("sp", hbuf_d[t * 128:(t + 1) * 128, :], ht, r=(B_ht,), w=(B_hb[t],), chan=B_ht)
                    if dbg and b == 0 and l == 0:
                        S.drain("sp", (B_ht,))
                        S.dma("sp", dbg_h[t * 128:(t + 1) * 128, :], ht, r=(B_ht,), w=(), chan=B_ht)
                else:
                    S.op("act", lambda h: h.activation(out=junk, in_=ht, func=AF.Square,
                                                        accum_out=stat[:, 32 + t:33 + t]),
                         r=(B_ht,), w=(B_junk, B_stat))
                    S.op("act", lambda h: h.activation(out=stat[:, 32 + t:33 + t], in_=stat[:, 32 + t:33 + t],
                                                        func=AF.Sqrt, scale=1.0 / D, bias=EPS), r=(B_stat,),
                         w=(B_stat,))
                    S.op("dve", lambda h: h.reciprocal(out=stat[:, 32 + t:33 + t], in_=stat[:, 32 + t:33 + t]),
                         r=(B_stat,), w=(B_stat,))
                    S.op("dve", lambda h: h.tensor_scalar(out=ht, in0=ht, scalar1=stat[:, 32 + t:33 + t],
                                                           scalar2=None, op0=ALU.mult), r=(B_ht, B_stat),
                         w=(B_ht,))
                    S.op("pool", lambda h: h.tensor_tensor(out=ht, in0=ht, in1=fng_bc[:], op=ALU.mult),
                         r=(B_ht, B_const), w=(B_ht,))
                    S.dma("sp", out_d[b, (t - 2) * 128:(t - 1) * 128, :], ht, r=(B_ht,), w=(), chan=B_ht)
            fence_arena(allb + [B_stat])

        for b in range(NB):
            for l in range(NL):
                layer(b, l, l == NL - 1)
        for cb in S.chans:
            S.E["sp"].h.wait_ge(cb.chan[0], cb.chan[1])
    return nc


IN_OFF = {}
_o = 0
for _n, _s in (("cq", 256), ("ckv", 128), ("kr", 32), ("gmla", 512), ("hq", 512), ("hi", 512), ("hff", 512),
               ("hfb", 512), ("ghg", 512), ("cx", 512), ("cb", 512), ("cc", 512), ("gcv", 512), ("gate", 3072)):
    IN_OFF[_n] = _o
    _o += _s


def _stat(wcols):
    K, ncol = wcols.shape
    return np.ascontiguousarray(wcols.reshape(K // 128, 128, ncol).transpose(1, 0, 2))


def _swap_idx():
    d = np.arange(32)
    a, hf, p = d // 16, (d // 8) % 2, d % 8
    return a * 16 + (1 - hf) * 8 + p


def prep_weights(w_in, mla_w_uq, mla_w_ukv, w_branch, w_out, ada_w, ada_b):
    sw = _swap_idx()
    W_in = np.zeros((L, NCH, 128, 8, 128), np.float32)
    W_uq = np.zeros((L, 16, 128, 2, 96), np.float32)
    W_kn = np.zeros((L, 128, 8, 64), np.float32)
    W_v = np.zeros((L, 128, 512), np.float32)
    W_br = np.zeros((L, 24, 128, 4, 128), np.float32)
    W_out = np.zeros((L, 128, 8, 1024), np.float32)
    ADA = np.zeros((L, 128, 8, 3072), np.float32)
    for l in range(L):
        w = w_in[l]

        def cols(name, j, width=128):
            o = IN_OFF[name] + j * width
            return w[:, o:o + width]

        for j in range(4):
            for q, nm in enumerate(("cx", "cc", "cb", "gcv")):
                W_in[l, CH_CONV + 4 * j + q] = _stat(cols(nm, j))
        for hp in range(4):
            for q, nm in enumerate(("hq", "hff", "hfb", "ghg")):
                W_in[l, CH_HG + 4 * hp + q] = _stat(cols(nm, hp))
            W_in[l, CH_HI + hp] = _stat(cols("hi", hp))
        for j in range(2):
            W_in[l, CH_CQ + j] = _stat(cols("cq", j))
        W_in[l, CH_CKV] = _stat(cols("ckv", 0))
        kr = w[:, IN_OFF["kr"]:IN_OFF["kr"] + 32]
        ka = np.zeros((1024, 128), np.float32)
        kb = np.zeros((1024, 128), np.float32)
        ka[:, 64:96] = kr
        kb[:, 64:96] = kr[:, sw]
        W_in[l, CH_KRA] = _stat(ka)
        W_in[l, CH_KRB] = _stat(kb)
        for j in range(4):
            W_in[l, CH_GMLA + j] = _stat(cols("gmla", j))
        for fc in range(8):
            for br in range(3):
                o = IN_OFF["gate"] + br * 1024 + fc * 128
                W_in[l, CH_GATE + 3 * fc + br] = _stat(w[:, o:o + 128])
        uq = mla_w_uq[l]
        for h in range(8):
            blk = uq[:, h * 96:(h + 1) * 96]
            W_uq[l, h] = _stat(blk)
            blk2 = blk.copy()
            blk2[:, 64:96] = blk[:, 64 + sw]
            W_uq[l, 8 + h] = _stat(blk2)
        ukv = mla_w_ukv[l]
        for h in range(8):
            W_kn[l, :, h, :] = ukv[:, h * 128:h * 128 + 64]
            W_v[l, :, h * 64:(h + 1) * 64] = ukv[:, h * 128 + 64:h * 128 + 128]
        for fc in range(8):
            for br in range(3):
                W_br[l, 3 * fc + br] = _stat(w_branch[l, br][:, fc * 128:(fc + 1) * 128])
        W_out[l] = _stat(w_out[l])
        ADA[l] = _stat(ada_w[l])
    return dict(w_in=W_in, w_uq=W_uq, w_kn=W_kn, w_v=W_v, w_br=W_br, w_out=W_out, ada=ADA,
                adab=np.ascontiguousarray(ada_b.reshape(L, 1, 3072)))


def prep_small(norm_g, mla_q_norm_g, mla_kv_norm_g, hg_norm_g, conv_w, conv_b, hg_lb_logits):
    sm = np.zeros((128, NSM), np.float32)
    fm = lambda v: np.ascontiguousarray(v.reshape(-1, 128).T)
    for l in range(L):
        sm[:, SM_NG + l * 8:SM_NG + (l + 1) * 8] = fm(norm_g[l])
        sm[:, SM_QG + l * 2:SM_QG + (l + 1) * 2] = fm(mla_q_norm_g[l])
        sm[:, SM_KVG + l:SM_KVG + l + 1] = fm(mla_kv_norm_g[l])
        sm[:, SM_HGG + l * 4:SM_HGG + (l + 1) * 4] = fm(hg_norm_g[l])
        for k in range(3):
            sm[:, SM_CW + l * 12 + k * 4:SM_CW + l * 12 + (k + 1) * 4] = fm(conv_w[l, k])
        sm[:, SM_CB + l * 4:SM_CB + (l + 1) * 4] = fm(conv_b[l])
        for d in range(2):
            sm[:, SM_LB + l * 8 + d * 4:SM_LB + l * 8 + (d + 1) * 4] = fm(hg_lb_logits[l, d])
    d = np.arange(32)
    a, hf, p = d // 16, (d // 8) % 2, d % 8
    sm[64:96, SM_ROPE] = (10000.0 ** (-(p.astype(np.float64)) / 8.0)).astype(np.float32)
    sm[64:96, SM_ROPE + 1] = a.astype(np.float32)
    sm[64:96, SM_ROPE + 2] = np.where(hf == 0, -1.0, 1.0).astype(np.float32)
    return sm


_CACHE = {}


def kernel(x, c, ctx, c_ctx, ada_w, ada_b, norm_g, w_in, mla_q_norm_g, mla_kv_norm_g, mla_w_uq, mla_w_ukv,
           hg_lb_logits, hg_norm_g, conv_w, conv_b, w_branch, w_out, final_norm_g):
    f = lambda a: np.asarray(a, dtype=np.float32)
    x, c, ctx, c_ctx = f(x), f(c), f(ctx), f(c_ctx)
    W = prep_weights(f(w_in), f(mla_w_uq), f(mla_w_ukv), f(w_branch), f(w_out), f(ada_w), f(ada_b))
    sm = prep_small(f(norm_g), f(mla_q_norm_g), f(mla_kv_norm_g), f(hg_norm_g), f(conv_w), f(conv_b),
                    f(hg_lb_logits))
    fng = np.ascontiguousarray(f(final_norm_g).reshape(1, D))
    n_cores = 8
    NB = x.shape[0] // n_cores
    if "nc" not in _CACHE:
        _CACHE["nc"] = build(NB=NB, NL=L)
    nc = _CACHE["nc"]
    in_maps = []
    for ci in range(n_cores):
        bs = slice(ci * NB, (ci + 1) * NB)
        xin = np.ascontiguousarray(np.concatenate([ctx[bs], x[bs]], axis=1))
        rows = np.concatenate([c[bs], c_ctx[None, :]], axis=0)
        cT = np.ascontiguousarray(rows.T.reshape(8, 128, 5).transpose(1, 0, 2))
        m = dict(xin=xin, cT=cT, smallp=sm, fng=fng)
        m.update(W)
        in_maps.append(m)
    res = run_bass_kernel_spmd(nc, in_maps, core_ids=list(range(n_cores)))
    return np.concatenate([r["out"] for r in res.results], axis=0).astype(np.float32)
```
